# Optimizing a Trainium2 kernel written in Bass

```python
import math
import jax, jax.numpy as jnp
from jax import lax
import numpy as np

D_MODEL = 1024
BATCH = 8
SEQ = 8192
DEPTH = 2
DEC_BATCH = 32
DEC_SEQ = 64
PAST_LEN = 2048

CHUNK = 64
Q_BLOCK = 128
N_DIFF_HEADS = 4
DIFF_DV = 128
DIFF_DK = DIFF_DV // 2
DIFF_WIDTH = N_DIFF_HEADS * DIFF_DV
POOL_WINDOWS = (2, 4, 8, 16)
N_POOL_GROUPS = len(POOL_WINDOWS)
POOL_WIDTH = D_MODEL - DIFF_WIDTH
POOL_GC = POOL_WIDTH // N_POOL_GROUPS
POOL_HIST = max(POOL_WINDOWS) - 1
IN_WIDTH = 3 * DIFF_WIDTH + POOL_WIDTH
N_MEM = 256
N_X_HEADS = 4
X_HEAD_DIM = D_MODEL // N_X_HEADS
D_FF = 4 * D_MODEL
NORM_EPS = 1e-6
SUBLN_EPS = 1e-5

kernel_name = 'hybrid_diffattn_pool_stream_step'


def _rmsnorm(x, g, eps=NORM_EPS):
    xf = x.astype(jnp.float32)
    y = xf * lax.rsqrt(jnp.mean(xf * xf, axis=-1, keepdims=True) + eps)
    return (y * g.astype(jnp.float32)).astype(x.dtype)


def _split_proj(h, w_in):
    B, S, _ = h.shape
    z = h @ w_in
    q = z[..., :DIFF_WIDTH].reshape(B, S, 2, N_DIFF_HEADS, DIFF_DK)
    k = z[..., DIFF_WIDTH:2 * DIFF_WIDTH].reshape(B, S, 2, N_DIFF_HEADS, DIFF_DK)
    v = z[..., 2 * DIFF_WIDTH:3 * DIFF_WIDTH].reshape(B, S, N_DIFF_HEADS, DIFF_DV)
    u = z[..., 3 * DIFF_WIDTH:]
    return q, k, v, u


def _lambda(lam_q, lam_k, lam_init):
    lf = jnp.sum(lam_q.astype(jnp.float32) * lam_k.astype(jnp.float32), axis=-1)
    return jnp.exp(lf[0]) - jnp.exp(lf[1]) + lam_init


def _chunk_mask(q_pos, k_pos):
    return (k_pos[None, :] // CHUNK) <= (q_pos[:, None] // CHUNK)


def _diff_attention(q, k, v, q_pos, k_pos, lam):
    s = jnp.einsum('bqmhd,bkmhd->bmhqk', q, k).astype(jnp.float32) * (DIFF_DK ** -0.5)
    s = jnp.where(_chunk_mask(q_pos, k_pos), s, -jnp.inf)
    p = jax.nn.softmax(s, axis=-1)
    a = (p[:, 0] - lam * p[:, 1]).astype(v.dtype)
    return jnp.einsum('bhqk,bkhd->bqhd', a, v)


def _blocked_diff_attention(q, k, v, lam):
    B, S = q.shape[:2]
    nb = S // Q_BLOCK
    qb = jnp.moveaxis(q.reshape(B, nb, Q_BLOCK, 2, N_DIFF_HEADS, DIFF_DK), 1, 0)
    k_pos = jnp.arange(S)

    def one(args):
        q_blk, i = args
        q_pos = i * Q_BLOCK + jnp.arange(Q_BLOCK)
        return _diff_attention(q_blk, k, v, q_pos, k_pos, lam)

    o = lax.map(one, (qb, jnp.arange(nb)))
    return jnp.moveaxis(o, 0, 1).reshape(B, S, N_DIFF_HEADS, DIFF_DV)


def _diff_post(o, subln_g, lam_init):
    B, S = o.shape[:2]
    return (_rmsnorm(o, subln_g, SUBLN_EPS) * (1.0 - lam_init)).reshape(B, S, DIFF_WIDTH)


def _pool_branch(u, hist, start, w_pool, pool_scale):
    B, S, P = u.shape
    ext = jnp.concatenate([hist, u], axis=1)
    cs = jnp.cumsum(ext.astype(jnp.float32), axis=1)
    cs = jnp.pad(cs, ((0, 0), (1, 0), (0, 0)))
    pos = start + jnp.arange(S)
    means = []
    for g, w in enumerate(POOL_WINDOWS):
        sl = slice(g * POOL_GC, (g + 1) * POOL_GC)
        win = cs[:, POOL_HIST + 1:POOL_HIST + 1 + S, sl] - cs[:, POOL_HIST + 1 - w:POOL_HIST + 1 - w + S, sl]
        cnt = jnp.minimum(w, pos + 1).astype(jnp.float32)[None, :, None]
        means.append(win / cnt)
    mean = jnp.stack(means, axis=2)
    pooled = (mean - u.astype(jnp.float32).reshape(B, S, N_POOL_GROUPS, POOL_GC)).astype(u.dtype)
    y = jnp.einsum('bsgc,gcd->bsgd', pooled, w_pool).reshape(B, S, P) * pool_scale
    return y, ext[:, -POOL_HIST:]


def _mem_kv(mem, g, wk, wv):
    B = mem.shape[0]
    m = _rmsnorm(mem, g)
    mk = (m @ wk).reshape(B, N_MEM, N_X_HEADS, X_HEAD_DIM)
    mv = (m @ wv).reshape(B, N_MEM, N_X_HEADS, X_HEAD_DIM)
    return mk, mv


def _cross_attn(h, mk, mv, wq, wo):
    B, S, _ = h.shape
    q = (h @ wq).reshape(B, S, N_X_HEADS, X_HEAD_DIM)
    s = jnp.einsum('bshd,bmhd->bhsm', q, mk).astype(jnp.float32) * (X_HEAD_DIM ** -0.5)
    p = jax.nn.softmax(s, axis=-1).astype(mv.dtype)
    o = jnp.einsum('bhsm,bmhd->bshd', p, mv).reshape(B, S, D_MODEL)
    return o @ wo


def _mlp(h, w_up, w_down):
    return jnp.square(jax.nn.relu(h @ w_up)) @ w_down


def _rest(x, a, pool_y, w_out, mk, mv, norm_x_g, wq_x, wo_x, norm_mlp_g, w_up, w_down):
    x = x + jnp.concatenate([a, pool_y], axis=-1) @ w_out
    x = x + _cross_attn(_rmsnorm(x, norm_x_g), mk, mv, wq_x, wo_x)
    return x + _mlp(_rmsnorm(x, norm_mlp_g), w_up, w_down)


def setup_inputs(seed: int = 0) -> dict:
    key = jax.random.key(seed)
    ks = iter(jax.random.split(key, 32))
    f32 = jnp.float32

    def nrm(shape, scale=1.0):
        return jax.random.normal(next(ks), shape, f32) * scale

    def gain(shape):
        return 1.0 + 0.05 * jax.random.normal(next(ks), shape, f32)

    return {
        'x_prompt': nrm((BATCH, SEQ, D_MODEL)),
        'x_sample': nrm((DEC_BATCH, DEC_SEQ, D_MODEL)),
        'cache_k': nrm((DEPTH, DEC_BATCH, PAST_LEN, 2, N_DIFF_HEADS, DIFF_DK)),
        'cache_v': nrm((DEPTH, DEC_BATCH, PAST_LEN, N_DIFF_HEADS, DIFF_DV)),
        'state_pool': nrm((DEPTH, DEC_BATCH, POOL_HIST, POOL_WIDTH)),
        'cache_mem_k': nrm((DEPTH, DEC_BATCH, N_MEM, N_X_HEADS, X_HEAD_DIM)),
        'cache_mem_v': nrm((DEPTH, DEC_BATCH, N_MEM, N_X_HEADS, X_HEAD_DIM)),
        'mem_prompt': nrm((BATCH, N_MEM, D_MODEL)),
        'norm_mix_g': gain((DEPTH, D_MODEL)),
        'w_in': nrm((DEPTH, D_MODEL, IN_WIDTH), D_MODEL ** -0.5),
        'lam_q': nrm((DEPTH, 2, DIFF_DK), 0.1),
        'lam_k': nrm((DEPTH, 2, DIFF_DK), 0.1),
        'subln_g': gain((DEPTH, DIFF_DV)),
        'w_pool': nrm((DEPTH, N_POOL_GROUPS, POOL_GC, POOL_GC), POOL_GC ** -0.5),
        'pool_scale': gain((DEPTH, POOL_WIDTH)),
        'w_out': nrm((DEPTH, D_MODEL, D_MODEL), D_MODEL ** -0.5),
        'norm_x_g': gain((DEPTH, D_MODEL)),
        'norm_mem_g': gain((DEPTH, D_MODEL)),
        'wq_x': nrm((DEPTH, D_MODEL, D_MODEL), D_MODEL ** -0.5),
        'wk_x': nrm((DEPTH, D_MODEL, D_MODEL), D_MODEL ** -0.5),
        'wv_x': nrm((DEPTH, D_MODEL, D_MODEL), D_MODEL ** -0.5),
        'wo_x': nrm((DEPTH, D_MODEL, D_MODEL), D_MODEL ** -0.5),
        'norm_mlp_g': gain((DEPTH, D_MODEL)),
        'w_up': nrm((DEPTH, D_MODEL, D_FF), D_MODEL ** -0.5),
        'w_down': nrm((DEPTH, D_FF, D_MODEL), D_FF ** -0.5),
        'final_g': gain((D_MODEL,)),
    }


def reference(x_prompt, x_sample, cache_k, cache_v, state_pool, cache_mem_k, cache_mem_v, mem_prompt,
              norm_mix_g, w_in, lam_q, lam_k, subln_g, w_pool, pool_scale, w_out,
              norm_x_g, norm_mem_g, wq_x, wk_x, wv_x, wo_x, norm_mlp_g, w_up, w_down, final_g):
    xp, xs = x_prompt, x_sample
    past = cache_k.shape[2]
    n_new = xs.shape[1]
    q_pos_s = past + jnp.arange(n_new)
    k_pos_s = jnp.arange(past + n_new)
    zero_hist = jnp.zeros((xp.shape[0], POOL_HIST, POOL_WIDTH), xp.dtype)
    kp_l, vp_l, pp_l, mkp_l, mvp_l, ks_l, vs_l, ps_l = [], [], [], [], [], [], [], []
    for l in range(DEPTH):
        lam_init = 0.8 - 0.6 * math.exp(-0.3 * l)
        lam = _lambda(lam_q[l], lam_k[l], lam_init)

        q, k, v, u = _split_proj(_rmsnorm(xp, norm_mix_g[l]), w_in[l])
        a = _diff_post(_blocked_diff_attention(q, k, v, lam), subln_g[l], lam_init)
        pool_y, hist_p = _pool_branch(u, zero_hist, 0, w_pool[l], pool_scale[l])
        mk, mv = _mem_kv(mem_prompt, norm_mem_g[l], wk_x[l], wv_x[l])
        xp = _rest(xp, a, pool_y, w_out[l], mk, mv, norm_x_g[l], wq_x[l], wo_x[l],
                   norm_mlp_g[l], w_up[l], w_down[l])
        kp_l.append(k); vp_l.append(v); pp_l.append(hist_p); mkp_l.append(mk); mvp_l.append(mv)

        q, k, v, u = _split_proj(_rmsnorm(xs, norm_mix_g[l]), w_in[l])
        k_all = jnp.concatenate([cache_k[l], k], axis=1)
        v_all = jnp.concatenate([cache_v[l], v], axis=1)
        a = _diff_post(_diff_attention(q, k_all, v_all, q_pos_s, k_pos_s, lam), subln_g[l], lam_init)
        pool_y, hist_s = _pool_branch(u, state_pool[l], past, w_pool[l], pool_scale[l])
        xs = _rest(xs, a, pool_y, w_out[l], cache_mem_k[l], cache_mem_v[l], norm_x_g[l], wq_x[l], wo_x[l],
                   norm_mlp_g[l], w_up[l], w_down[l])
        ks_l.append(k); vs_l.append(v); ps_l.append(hist_s)

    y_prompt = _rmsnorm(xp, final_g)
    y_sample = _rmsnorm(xs, final_g)
    return (y_prompt, y_sample,
            jnp.stack(kp_l), jnp.stack(vp_l), jnp.stack(pp_l), jnp.stack(mkp_l), jnp.stack(mvp_l),
            jnp.stack(ks_l), jnp.stack(vs_l), jnp.stack(ps_l))
```

```python
import math
from contextlib import ExitStack

import numpy as np
import concourse.bass as bass
import concourse.mybir as mybir
from concourse.bass_utils import run_bass_kernel_spmd

F32 = mybir.dt.float32
BF16 = mybir.dt.bfloat16
AF = mybir.ActivationFunctionType
ALU = mybir.AluOpType

D = 1024
DEPTH = 2
NBS = 4
SSEQ = 64
NMEM = 256
NSLAB = 30


def _flat(deps):
    if deps is None:
        return
    if isinstance(deps, tuple) and len(deps) == 2 and isinstance(deps[1], int):
        yield deps
        return
    for d in deps:
        yield from _flat(d)


class DSem:
    def __init__(self, sem):
        self.sem = sem
        self.n = 0


class Eng:
    def __init__(self, name, sem):
        self.name = name
        self.sem = sem
        self.n = 0
        self.seen = {}
        self.prog = []
        self.chain = (name in ("act", "dve", "pool"))

    def _waits(self, deps):
        waits = []
        for d in _flat(deps):
            sem, val = d
            k = id(sem)
            if self.seen.get(k, 0) >= val:
                continue
            self.seen[k] = val
            waits.append((sem, val))
        return waits

    def add(self, fn, deps=(), sig=True):
        if self.chain and self.n > 0:
            deps = [deps, (self.sem, self.n)]
        waits = self._waits(deps)
        tok = None
        if sig:
            self.n += 1
            tok = (self.sem, self.n)
        self.prog.append((waits, fn, 1 if sig else 0, None))
        return tok

    def dma(self, dsem, out, in_, deps=(), **kw):
        waits = self._waits(deps)
        dsem.n += 16
        self.prog.append((waits, lambda e: e.dma_start(out=out, in_=in_, **kw), 2, dsem.sem))
        return (dsem.sem, dsem.n)

    def wait_only(self, deps):
        waits = self._waits(deps)
        if waits:
            self.prog.append((waits, None, 0, None))

    def replay(self, e):
        for waits, fn, kind, dsem in self.prog:
            for sem, val in waits:
                e.wait_ge(sem, val)
            if fn is None:
                continue
            inst = fn(e)
            if kind == 1:
                inst.then_inc(self.sem, 1)
            elif kind == 2:
                inst.then_inc(dsem, 16)


class _Stop(Exception):
    pass


def build_program(S=8192, PAST=2048, upto=None, dbg=None):
    NTP = S // 512
    NKB = S // 128
    NCB = PAST // 128
    nc = bass.Bass("TRN2", target_bir_lowering=False)
    es = ExitStack()

    def din(name, shape):
        return nc.dram_tensor(name, shape, F32, kind="ExternalInput").ap()

    def dout(name, shape):
        return nc.dram_tensor(name, shape, F32, kind="ExternalOutput").ap()

    x_prompt = din("x_prompt", [S, D])
    x_sample = din("x_sample", [NBS * SSEQ, D])
    cache_k = din("cache_k", [DEPTH, NBS, PAST, 512])
    cache_v = din("cache_v", [DEPTH, NBS, PAST, 4, 128])
    state_pool = din("state_pool", [DEPTH, NBS, 15, 512])
    cache_mem_k = din("cache_mem_k", [DEPTH, NBS, NMEM, D])
    cache_mem_v = din("cache_mem_v", [DEPTH, NBS, NMEM, D])
    mem_prompt = din("mem_prompt", [NMEM, D])
    norm_mix_g = din("norm_mix_g", [DEPTH, D])
    w_in = din("w_in", [DEPTH, D, 2048])
    lam_q = din("lam_q", [DEPTH, 128])
    lam_k = din("lam_k", [DEPTH, 128])
    subln_g = din("subln_g", [DEPTH, 128])
    w_pool = din("w_pool", [DEPTH, 4, 128, 128])
    pool_scale = din("pool_scale", [DEPTH, 512])
    w_out = din("w_out", [DEPTH, D, D])
    norm_x_g = din("norm_x_g", [DEPTH, D])
    norm_mem_g = din("norm_mem_g", [DEPTH, D])
    wq_x = din("wq_x", [DEPTH, D, D])
    wk_x = din("wk_x", [DEPTH, D, D])
    wv_x = din("wv_x", [DEPTH, D, D])
    wo_x = din("wo_x", [DEPTH, D, D])
    norm_mlp_g = din("norm_mlp_g", [DEPTH, D])
    w_up = din("w_up", [DEPTH, D, 4096])
    w_down = din("w_down", [DEPTH, 4096, D])
    final_g = din("final_g", [1, D])

    y_prompt = dout("y_prompt", [S, D])
    y_sample = dout("y_sample", [NBS * SSEQ, D])
    k_prompt = dout("k_prompt", [DEPTH, S, 512])
    v_prompt = dout("v_prompt", [DEPTH, S, 512])
    pool_prompt = dout("pool_prompt", [DEPTH, 15, 512])
    mem_k_prompt = dout("mem_k_prompt", [DEPTH, NMEM, D])
    mem_v_prompt = dout("mem_v_prompt", [DEPTH, NMEM, D])
    k_sample = dout("k_sample", [DEPTH, NBS * SSEQ, 512])
    v_sample = dout("v_sample", [DEPTH, NBS * SSEQ, 512])
    pool_sample = dout("pool_sample", [DEPTH, NBS, 15, 512])

    wscr = nc.dram_tensor("wscr", [DEPTH, NSLAB, 128, 8, 512], BF16, kind="Internal").ap()
    xscr = nc.dram_tensor("xscr", [S + NBS * SSEQ, D], F32, kind="Internal").ap()
    ktscr = nc.dram_tensor("ktscr", [DEPTH, 4, 128, S], BF16, kind="Internal").ap()

    def sb(name, shape, dt):
        return es.enter_context(nc.sbuf_tensor(name, shape, dt))

    def ps(name, shape, dt):
        return es.enter_context(nc.psum_tensor(name, shape, dt))

    def newsem(name):
        return es.enter_context(nc.semaphore(name))

    PE = Eng("pe", newsem("s_pe"))
    ACT = Eng("act", newsem("s_act"))
    DVE = Eng("dve", newsem("s_dve"))
    POOL = Eng("pool", newsem("s_pool"))
    SP = Eng("sp", newsem("s_sp"))
    _dsn = [0]

    def dsem():
        _dsn[0] += 1
        return DSem(newsem("d%d" % _dsn[0]))

    VRES_COLS = NKB * 4 * 129
    SAMP_COLS = NCB * 512 + 4 * (PAST) + (NCB + 1) * 4 * 129
    vreg = sb("vreg", [128, max(VRES_COLS, SAMP_COLS)], BF16)
    Vres = vreg[:, 0:VRES_COLS].rearrange("p (k h d) -> p k h d", h=4, d=129)
    o0 = 0
    kc_tok = vreg[:, o0:o0 + NCB * 512].rearrange("p (k c) -> p k c", c=512)
    o0 += NCB * 512
    kT_s = vreg[:, o0:o0 + 4 * PAST].rearrange("p (c k) -> p c k", c=4)
    o0 += 4 * PAST
    V_s = vreg[:, o0:o0 + (NCB + 1) * 4 * 129].rearrange("p (k h d) -> p k h d", h=4, d=129)

    xt = sb("xt", [128, 4, D], F32)
    xn = sb("xn", [128, 4, D], BF16)
    mktok = xn[:, 0:2, :]
    hT = sb("hT", [128, 8, 512], BF16)
    slabs = [sb("slab%d" % i, [128, 8, 512], BF16) for i in range(3)]
    NSTAGE = 3
    stage = [sb("stage%d" % i, [128, 512], F32) for i in range(NSTAGE)]
    aT = sb("aT", [128, 4, 512], BF16)
    pyT = sb("pyT", [128, 4, 512], BF16)
    mkT = sb("mkT", [128, 8, NMEM], BF16)
    mv = sb("mv", [128, 2, D], BF16)
    stats = sb("stats", [128, 512], F32)
    ident = sb("ident", [128, 128], BF16)
    identf = sb("identf", [128, 128], F32)
    ones_bf = sb("ones_bf", [128, 128], BF16)
    gcols = sb("gcols", [128, DEPTH, 4, 8], F32)
    pscol = sb("pscol", [128, DEPTH, 4], F32)
    fg_bc = sb("fg_bc", [128, D], F32)
    sg_bc = sb("sg_bc", [128, DEPTH, 128], F32)
    lam_t = sb("lam_t", [128, DEPTH, 8], F32)
    wpool_bf = sb("wpool_bf", [128, DEPTH, 4, 128], BF16)
    icnt = sb("icnt", [128, 4, 16], F32)
    o1b = sb("o1b", [128, 8, 128], F32)
    osq = sb("osq", [128, 128], F32)
    hist_keep = sb("hist_keep", [128, 4, 16], F32)
    RB = 47 * 1024
    region = sb("region", [128, RB], mybir.dt.uint8)

    class Carver:
        def __init__(self):
            self.off = 0

        def take(self, shape, dt):
            n = int(np.prod(shape[1:]))
            bpe = 2 if dt == BF16 else 4
            nbytes = n * bpe
            off = (self.off + 63) // 64 * 64
            assert off + nbytes <= RB, (off, nbytes, RB)
            ap = region[:, off:off + nbytes].bitcast(dt)
            self.off = off + nbytes
            if len(shape) == 2:
                return ap
            names = "abcd"[:len(shape) - 1]
            pat = "p (%s) -> p %s" % (" ".join(names), " ".join(names))
            kw = {names[i]: shape[i + 1] for i in range(1, len(names))}
            return ap.rearrange(pat, **kw)

    ca = Carver()
    qT = ca.take([128, 4, 512], BF16)
    kTcur = ca.take([128, 4, 512], BF16)
    ksreg = ca.take([128, 4096 + 64], BF16)
    kstream = [ksreg[:, i * 2048:(i + 1) * 2048] for i in range(2)]
    vnew = [ksreg[0:64, i * 516:(i + 1) * 516].rearrange("p (h d) -> p h d", h=4) for i in range(4)]
    PT = [ca.take([128, 2, 512], BF16) for _ in range(3)]
    o1 = ca.take([128, 8, 128], F32)
    a_tok = ca.take([128, 4, 512], BF16)
    uext = ca.take([128, 4, 528], F32)
    utmp_all = ca.take([128, 1056], F32)
    utmp = [utmp_all[:, i * 528:(i + 1) * 528] for i in range(2)]
    otmp = utmp_all[:, 0:1024].rearrange("p (a d) -> p a d", a=8)
    pooled2 = [ca.take([128, 512], BF16) for _ in range(4)]
    lamw = region[:, 0:2048].bitcast(F32).rearrange("p (a b) -> p a b", a=4)
    cb = Carver()
    qxT = cb.take([128, 8, 512], BF16)
    oxT = cb.take([128, 8, 512], BF16)
    PxT = [cb.take([128, 2, 512], BF16) for _ in range(2)]
    rrec = [cb.take([128, 512], F32) for _ in range(2)]
    hid = [cb.take([128, 8, 512], BF16) for _ in range(2)]
    rtmp = [cb.take([128, 512], BF16) for _ in range(2)]

    psA = ps("psA", [128, 2, 512], F32)
    psB = ps("psB", [128, 2, 512], F32)
    psO = ps("psO", [128, 3, 512], F32)
    psT = ps("psT", [128, 512], F32)

    st = {"stat": 0, "slab": 0, "slot": 0, "stage": 0, "ev": 0, "tb": 0}
    slab_free = [None, None, None]
    slab_ds = [dsem() for _ in range(3)]
    slot_aps = [psA[:, 0, :], psA[:, 1, :], psB[:, 0, :], psB[:, 1, :], psT[:, :]]
    slot_bf = [a_.bitcast(BF16) for a_ in slot_aps]
    NSLOT = 5
    slot_free = [None] * NSLOT
    stage_ds = [dsem() for _ in range(NSTAGE)]
    stage_free = [None] * NSTAGE
    conv_tok = {}
    out_toks = []
    state = {"hT_free": None, "xt_free": None, "region_free": None, "kst": [None] * max(NTP, 1),
             "o1_tok": [None, None]}

    def G(k):
        return state.get(k)

    ckn = [0]

    def ck(n=None):
        ckn[0] += 1
        if dbg == ckn[0]:
            raise _Stop()

    def stat(n):
        if st["stat"] + n > 512:
            st["stat"] = 0
        a_ = stats[:, st["stat"]:st["stat"] + n]
        st["stat"] += n
        return a_

    def get_slot():
        i = st["slot"]
        st["slot"] = (i + 1) % NSLOT
        return i

    def tr_group(ins, deps, f32=False):
        i = get_slot()
        view = slot_aps[i] if f32 else slot_bf[i]
        idt = identf if f32 else ident
        tp = None
        off = 0
        for k, in_ap in enumerate(ins):
            n = in_ap.shape[0]
            tp = PE.add(lambda e, in_ap=in_ap, off=off, n=n: e.transpose(
                out=view[:, off:off + n], in_=in_ap, identity=idt[0:n, 0:n]),
                [deps, slot_free[i], t_ident, t_idf] if k == 0 else (), sig=(k == len(ins) - 1))
            off += n
        return i, view, tp

    def evac_eng():
        st["ev"] ^= 1
        return ACT if st["ev"] else DVE

    def ev_copy(eng, out, in_, deps):
        if eng is ACT:
            return ACT.add(lambda e: e.copy(out=out, in_=in_), deps)
        return eng.add(lambda e: e.tensor_copy(out=out, in_=in_), deps)

    def ev_scale(eng, out, in_, sc, deps):
        if eng is ACT:
            return ACT.add(lambda e: e.mul(out=out, in_=in_, mul=sc), deps)
        return eng.add(lambda e: e.tensor_scalar(out=out, in0=in_, scalar1=sc, scalar2=None, op0=ALU.mult), deps)

    def load_slab(l, idx):
        b_ = st["slab"]
        st["slab"] = (b_ + 1) % 3
        tok = SP.dma(slab_ds[b_], slabs[b_][:], wscr[l, idx], deps=[conv_tok[(l, idx)], slab_free[b_]])
        return b_, tok

    def get_stage():
        i = st["stage"]
        st["stage"] = (i + 1) % NSTAGE
        return i

    def store_stage(i, dram_ap, src_ap, dep):
        tok = POOL.dma(stage_ds[i], dram_ap, src_ap, deps=[dep])
        stage_free[i] = tok
        out_toks.append(tok)
        return tok

    def acc_ap(a_, r0=0, rows=128, cols=129):
        return psO[r0:r0 + rows, a_ // 3, (a_ % 3) * 160:(a_ % 3) * 160 + cols]

    t_id0 = POOL.add(lambda e: e.memset(identf[:], 0.0))
    t_idf = POOL.add(lambda e: e.affine_select(out=identf[:], in_=identf[:], pattern=[[-1, 128]],
                                               compare_op=ALU.not_equal, fill=1.0, base=0,
                                               channel_multiplier=1), [t_id0])
    t_ident = POOL.add(lambda e: e.tensor_copy(out=ident[:], in_=identf[:]), [t_idf])
    t_ones = POOL.add(lambda e: e.memset(ones_bf[:], 1.0))
    ic_toks = []
    for g in range(4):
        w = 2 << g
        ic_toks.append(POOL.add(lambda e, g=g, w=w: e.memset(icnt[:, g, :], 1.0)))
        for tt in range(min(w - 1, 16)):
            ic_toks.append(POOL.add(lambda e, g=g, tt=tt, w=w: e.memset(icnt[:, g, tt:tt + 1], float(w) / (tt + 1))))
    c_ds = dsem()
    t_const = None
    gsrc = [norm_mix_g, norm_x_g, norm_mem_g, norm_mlp_g]
    for l in range(DEPTH):
        for i, gs in enumerate(gsrc):
            t_const = SP.dma(c_ds, gcols[:, l, i, :], gs[l].rearrange("(c p) -> p c", p=128),
                             allow_slow_non_contiguous=True)
        t_const = SP.dma(c_ds, pscol[:, l, :], pool_scale[l].rearrange("(c p) -> p c", p=128),
                         allow_slow_non_contiguous=True)
        t_const = SP.dma(c_ds, sg_bc[:, l, :], subln_g[l:l + 1, :].broadcast_to([128, 128]))
        t_const = SP.dma(c_ds, lamw[:, l * 2 + 0, :], lam_q[l:l + 1, :].broadcast_to([128, 128]))
        t_const = SP.dma(c_ds, lamw[:, l * 2 + 1, :], lam_k[l:l + 1, :].broadcast_to([128, 128]))
    t_const = SP.dma(c_ds, fg_bc[:], final_g[0:1, :].broadcast_to([128, D]))
    for l in range(DEPTH):
        for g in range(4):
            t_const = DVE.add(lambda e, l=l, g=g: e.tensor_scalar(out=pscol[:, l, g:g + 1], in0=pscol[:, l, g:g + 1],
                                                                 scalar1=1.0 / (2 << g), scalar2=None, op0=ALU.mult),
                              [t_const])
    wp_ds = dsem()
    t_wp = None
    for l in range(DEPTH):
        t_wp = POOL.dma(wp_ds, wpool_bf[:, l, :, :], w_pool[l].rearrange("g c d -> c g d"))

    def conv(l, idx, src2d):
        d_ = dsem()
        conv_tok[(l, idx)] = POOL.dma(d_, wscr[l, idx], src2d.rearrange("(kc p) c -> p kc c", p=128))

    def conv_layer(l):
        for h in range(2):
            conv(l, 8 + h, wk_x[l][:, h * 512:(h + 1) * 512])
        for h in range(2):
            conv(l, 10 + h, wv_x[l][:, h * 512:(h + 1) * 512])
        for s in range(4):
            conv(l, s, w_in[l][:, s * 512:(s + 1) * 512])
        for h in range(2):
            conv(l, 4 + h, w_out[l][:, h * 512:(h + 1) * 512])
        for h in range(2):
            conv(l, 6 + h, wq_x[l][:, h * 512:(h + 1) * 512])
        for h in range(2):
            conv(l, 12 + h, wo_x[l][:, h * 512:(h + 1) * 512])
        for qd in range(4):
            for h in range(2):
                conv(l, 14 + qd * 2 + h, w_up[l][:, (qd * 2 + h) * 512:(qd * 2 + h + 1) * 512])
            for h in range(2):
                conv(l, 22 + qd * 2 + h, w_down[l][qd * 1024:(qd + 1) * 1024, h * 512:(h + 1) * 512])

    conv_layer(0)
    conv_layer(1)

    lam_inits = [0.8 - 0.6 * math.exp(-0.3 * l) for l in range(DEPTH)]
    lam_tok = []
    for l in range(DEPTH):
        tk = None
        for i in range(2):
            tk = DVE.add(lambda e, l=l, i=i: e.scalar_tensor_tensor(
                out=osq[:, i * 64:(i + 1) * 64], in0=lamw[:, l * 2, i * 64:(i + 1) * 64], scalar=1.0,
                in1=lamw[:, l * 2 + 1, i * 64:(i + 1) * 64], op0=ALU.mult, op1=ALU.mult,
                accum_out=lam_t[:, l, i:i + 1]), [t_const, tk])
        t1 = ACT.add(lambda e, l=l: e.activation(out=lam_t[:, l, 2:4], in_=lam_t[:, l, 0:2], func=AF.Exp), [tk])
        t2 = DVE.add(lambda e, l=l: e.tensor_tensor(out=lam_t[:, l, 4:5], in0=lam_t[:, l, 2:3],
                                                    in1=lam_t[:, l, 3:4], op=ALU.subtract), [t1])
        t3 = DVE.add(lambda e, l=l: e.tensor_scalar(out=lam_t[:, l, 5:6], in0=lam_t[:, l, 4:5],
                                                    scalar1=-1.0, scalar2=-lam_inits[l],
                                                    op0=ALU.mult, op1=ALU.add), [t2])
        t4 = DVE.add(lambda e, l=l: e.tensor_scalar(out=sg_bc[:, l, :], in0=sg_bc[:, l, :],
                                                    scalar1=1.0 - lam_inits[l], scalar2=None,
                                                    op0=ALU.mult), [t_const, t3])
        lam_tok.append(t4)
    state["region_free"] = list(lam_tok)

    def rms_rstd(ns, deps_x, junk_deps):
        ss = stat(ns)
        rs = stat(ns)
        tks = []
        for s in range(ns):
            dx = deps_x[s] if (isinstance(deps_x, list) and len(deps_x) == ns and st.get("per_s")) else deps_x
            if s % 2 == 0:
                tks.append(ACT.add(lambda e, s=s: e.activation(out=xn[:, s, :], in_=xt[:, s, :], func=AF.Square,
                                                              accum_out=ss[:, s:s + 1]), [dx, junk_deps]))
            else:
                tks.append(DVE.add(lambda e, s=s: e.scalar_tensor_tensor(
                    out=xn[:, s, :], in0=xt[:, s, :], scalar=1.0, in1=xt[:, s, :], op0=ALU.mult, op1=ALU.mult,
                    accum_out=ss[:, s:s + 1]), [dx, junk_deps]))
        t1 = DVE.add(lambda e: e.tensor_scalar(out=ss, in0=ss, scalar1=1.0 / D, scalar2=1e-6,
                                               op0=ALU.mult, op1=ALU.add), tks)
        t2 = ACT.add(lambda e: e.activation(out=ss, in_=ss, func=AF.Ln), [t1])
        t3 = ACT.add(lambda e: e.activation(out=rs, in_=ss, func=AF.Exp, scale=-0.5), [t2])
        ck()
        return rs, t3

    def rmsnorm_hT(ns, gcol, deps_x, deps_hT_free):
        rs, t3 = rms_rstd(ns, deps_x, G("xn_free"))
        xtk = []
        for s in range(ns):
            xtk.append(ev_scale(evac_eng(), xn[:, s, :], xt[:, s, :], rs[:, s:s + 1], [t3]))
        ck()
        out = []
        tp = None
        for kc in range(8):
            i, view, tp = tr_group([xn[:, s, kc * 128:(kc + 1) * 128] for s in range(ns)], [xtk])
            tk = ev_scale(evac_eng(), hT[:, kc, 0:ns * 128], view[:, 0:ns * 128], gcol[:, kc:kc + 1],
                          [tp, t_const, deps_hT_free])
            slot_free[i] = tk
            out.append(tk)
            ck()
        state["xn_free"] = tp
        ck()
        return out

    def mm_group(lhs_fn, rhs_fn, nk, ncol, deps, m=128, kdeps=None):
        i = get_slot()
        tk = None
        for k in range(nk):
            dk = [deps, slot_free[i]] if k == 0 else []
            if kdeps is not None:
                dk = dk + [kdeps[k]]
            tk = PE.add(lambda e, k=k, i=i: e.matmul(slot_aps[i][0:m, 0:ncol], lhsT=lhs_fn(k), rhs=rhs_fn(k),
                                                     start=(k == 0), stop=(k == nk - 1)),
                        dk, sig=(k == nk - 1))
        return i, tk

    ks_ds = [dsem(), dsem()]
    ks_free = [None, None]
    ks_i = [0]
    kst_ds = [dsem(), dsem()]
    xl_ds = [dsem() for _ in range(4)]
    xs_ds = [dsem() for _ in range(4)]
    ptfree = [None, None, None]
    accfree = [None] * 8
    misc_ds = dsem()
    mem_ds = dsem()
    ps_ds = [dsem() for _ in range(NBS)]
    kc_ds = dsem()
    xt_store = [None] * 4

    def mem_kv_prompt(l):
        d_ = dsem()
        tl = SP.dma(d_, xt[:, 0:2, :], mem_prompt.rearrange("(s p) c -> p s c", p=128), deps=[G("xt_free")])
        hts = rmsnorm_hT(2, gcols[:, l, 2, :], [tl], G("hT_free"))
        last_pe = None
        for which in range(2):
            for h in range(2):
                b_, tsl = load_slab(l, 8 + which * 2 + h)
                slab = slabs[b_]
                ck()
                for s in range(2):
                    i, tk = mm_group(lambda k, s=s: hT[:, k, s * 128:(s + 1) * 128],
                                     lambda k, slab=slab: slab[:, k, :], 8, 512, [tsl], kdeps=hts)
                    si = get_stage()
                    te = ev_copy(evac_eng(), stage[si][:], slot_aps[i][:, :], [tk, stage_free[si]])
                    dst = (mem_k_prompt if which == 0 else mem_v_prompt)[l, s * 128:(s + 1) * 128, h * 512:(h + 1) * 512]
                    store_stage(si, dst, stage[si][:], te)
                    ck()
                    if which == 1:
                        te2 = ev_copy(DVE, mv[:, s, h * 512:(h + 1) * 512], slot_aps[i][:, :], [tk, te])
                        slot_free[i] = [te, te2]
                    else:
                        slot_free[i] = te
                    last_pe = tk
                    ck()
                if which == 0:
                    for cc in range(4):
                        i, tk = mm_group(lambda k, cc=cc, slab=slab: slab[:, k, cc * 128:(cc + 1) * 128],
                                         lambda k: hT[:, k, 0:256], 8, 256, [tsl], kdeps=hts)
                        te = ev_copy(evac_eng(), mkT[:, h * 4 + cc, :], slot_aps[i][:, 0:256], [tk])
                        slot_free[i] = te
                        last_pe = tk
                        ck()
                slab_free[b_] = last_pe
        state["hT_free"] = last_pe
        state["xt_free"] = last_pe
        return last_pe

    def cross_attn(l, c0, ncol, dep_in):
        last = None
        for hx in range(4):
            pb = hx % 2
            ptok = []
            for mb in range(2):
                i = get_slot()
                tk = None
                for half in range(2):
                    tk = PE.add(lambda e, i=i, half=half, mb=mb, hx=hx: e.matmul(
                        slot_aps[i][:, 0:ncol], lhsT=mkT[:, hx * 2 + half, mb * 128:(mb + 1) * 128],
                        rhs=qxT[:, hx * 2 + half, c0:c0 + ncol], start=(half == 0), stop=(half == 1)),
                        [dep_in, slot_free[i]] if half == 0 else (), sig=(half == 1))
                te = ACT.add(lambda e, i=i, mb=mb, pb=pb: e.activation(
                    out=PxT[pb][:, mb, 0:ncol], in_=slot_aps[i][:, 0:ncol], func=AF.Exp, scale=1.0 / 16.0),
                    [tk, G("px_free%d" % pb)])
                slot_free[i] = te
                ptok.append(te)
            i = get_slot()
            tk = None
            for mb in range(2):
                tk = PE.add(lambda e, i=i, mb=mb, pb=pb: e.matmul(
                    slot_aps[i][:, 0:ncol], lhsT=ones_bf[:, :], rhs=PxT[pb][:, mb, 0:ncol],
                    start=(mb == 0), stop=(mb == 1)),
                    [ptok, slot_free[i], t_ones] if mb == 0 else (), sig=(mb == 1))
            tr = DVE.add(lambda e, i=i, pb=pb: e.reciprocal(out=rrec[pb][:, 0:ncol], in_=slot_aps[i][:, 0:ncol]),
                         [tk, G("rr_free%d" % pb)])
            slot_free[i] = tr
            tms = []
            for half in range(2):
                i = get_slot()
                tk = None
                for mb in range(2):
                    tk = PE.add(lambda e, i=i, mb=mb, pb=pb, hx=hx, half=half: e.matmul(
                        slot_aps[i][:, 0:ncol], lhsT=mv[:, mb, hx * 256 + half * 128: hx * 256 + (half + 1) * 128],
                        rhs=PxT[pb][:, mb, 0:ncol], start=(mb == 0), stop=(mb == 1)),
                        [ptok, slot_free[i]] if mb == 0 else (), sig=(mb == 1))
                tm = DVE.add(lambda e, i=i, pb=pb, hx=hx, half=half: e.tensor_tensor(
                    out=oxT[:, hx * 2 + half, c0:c0 + ncol], in0=slot_aps[i][:, 0:ncol],
                    in1=rrec[pb][:, 0:ncol], op=ALU.mult), [tk, tr, G("ox_free")])
                slot_free[i] = tm
                tms.append(tm)
                last = tk
            state["px_free%d" % pb] = last
            state["rr_free%d" % pb] = tms
            state["ox_toks"] = state.get("ox_toks", []) + tms
        return last

    def subln(l, j, tev, r0, rows, items):
        ob = o1 if j == 0 else o1b
        n = len(items)
        ssq = stat(n)
        R = slice(r0, r0 + rows)
        tks = []
        for k, (a_, h, sub) in enumerate(items):
            tks.append(DVE.add(lambda e, a_=a_, k=k: e.scalar_tensor_tensor(
                out=osq[R, :], in0=ob[R, a_, :], scalar=1.0, in1=ob[R, a_, :], op0=ALU.mult, op1=ALU.mult,
                accum_out=ssq[R, k:k + 1]), [tev]))
        t1 = DVE.add(lambda e: e.tensor_scalar(out=ssq[R, :], in0=ssq[R, :], scalar1=1.0 / 128, scalar2=1e-5,
                                               op0=ALU.mult, op1=ALU.add), tks)
        t2 = ACT.add(lambda e: e.activation(out=ssq[R, :], in_=ssq[R, :], func=AF.Ln), [t1])
        t3 = ACT.add(lambda e: e.activation(out=ssq[R, :], in_=ssq[R, :], func=AF.Exp, scale=-0.5), [t2])
        outs = []
        for k, (a_, h, sub) in enumerate(items):
            outs.append(DVE.add(lambda e, a_=a_, h=h, sub=sub, k=k: e.scalar_tensor_tensor(
                out=a_tok[R, sub, h * 128:(h + 1) * 128], in0=ob[R, a_, :],
                scalar=ssq[R, k:k + 1], in1=sg_bc[R, l, :], op0=ALU.mult, op1=ALU.mult),
                [t3, lam_tok[l], G("aT_free")]))
        state["o1_free"] = outs
        return outs

    def evac_pass(l, m, j, tlast, r0, rows, accs):
        ob = o1 if j == 0 else o1b
        rc = stat(8)
        R = slice(r0, r0 + rows)
        tev = []
        for a_ in accs:
            tr = DVE.add(lambda e, a_=a_: e.reciprocal(out=rc[R, a_:a_ + 1],
                                                       in_=acc_ap(a_, r0, rows)[:, 128:129]), [tlast])
            if m == 0:
                te = DVE.add(lambda e, a_=a_: e.tensor_scalar(
                    out=ob[R, a_, :], in0=acc_ap(a_, r0, rows)[:, 0:128], scalar1=rc[R, a_:a_ + 1],
                    scalar2=None, op0=ALU.mult), [tr, G("o1_free")])
            else:
                tn = DVE.add(lambda e, a_=a_: e.tensor_scalar(out=rc[R, a_:a_ + 1], in0=rc[R, a_:a_ + 1],
                                                              scalar1=lam_t[R, l, 5:6], scalar2=None, op0=ALU.mult),
                             [tr, lam_tok[l]])
                te = DVE.add(lambda e, a_=a_: e.scalar_tensor_tensor(
                    out=ob[R, a_, :], in0=acc_ap(a_, r0, rows)[:, 0:128], scalar=rc[R, a_:a_ + 1],
                    in1=ob[R, a_, :], op0=ALU.mult, op1=ALU.add), [tn, state["o1_tok"][j]])
            tev.append(te)
        for a_ in range(8):
            accfree[a_] = tev
        return tev

    def subln_fast(l, j, tev):
        ob = o1 if j == 0 else o1b
        ssq = stat(8)
        t0 = DVE.add(lambda e: e.tensor_tensor(out=otmp[:, :, :], in0=ob[:, :, :], in1=ob[:, :, :], op=ALU.mult),
                     [tev, G("otmp_free")])
        t0b = DVE.add(lambda e: e.tensor_reduce(out=ssq, in_=otmp[:, :, :], axis=mybir.AxisListType.X, op=ALU.add), [t0])
        t1 = DVE.add(lambda e: e.tensor_scalar(out=ssq, in0=ssq, scalar1=1.0 / 128, scalar2=1e-5,
                                               op0=ALU.mult, op1=ALU.add), [t0b])
        t2 = ACT.add(lambda e: e.activation(out=ssq, in_=ssq, func=AF.Ln), [t1])
        t3 = ACT.add(lambda e: e.activation(out=ssq, in_=ssq, func=AF.Exp, scale=-0.5), [t2])
        t4 = DVE.add(lambda e: e.tensor_tensor(out=otmp[:, :, :], in0=ob[:, :, :],
                                               in1=ssq.unsqueeze(2).to_broadcast([128, 8, 128]), op=ALU.mult), [t3])
        dst = a_tok[:, :, 2 * j * 128:(2 * j + 2) * 128].rearrange("p q (h d) -> p h q d", h=2)
        t5 = DVE.add(lambda e: e.tensor_tensor(
            out=dst, in0=otmp[:, :, :].rearrange("p (h q) d -> p h q d", h=2),
            in1=sg_bc[:, l, :].unsqueeze(1).unsqueeze(1).to_broadcast([128, 2, 4, 128]), op=ALU.mult),
            [t4, lam_tok[l], G("aT_free")])
        state["o1_free"] = [t5]
        state["otmp_free"] = t5
        return [t5]

    def evac_pass_fast(l, m, j, tlast, tpool_done):
        ob = o1 if j == 0 else o1b
        rc = stat(8)
        gate = []
        tmul = []
        for b_ in range(3):
            na = 3 if b_ < 2 else 2
            bank = psO[:, b_, 0:480].rearrange("p (a c) -> p a c", c=160)
            rcb = rc[:, 3 * b_:3 * b_ + na]
            tr = DVE.add(lambda e, bank=bank, rcb=rcb, na=na: e.reciprocal(out=rcb, in_=bank[:, 0:na, 128]), [tlast])
            dst = (ob if m == 0 else otmp)[:, 3 * b_:3 * b_ + na, :]
            te = DVE.add(lambda e, bank=bank, rcb=rcb, na=na, dst=dst: e.tensor_tensor(
                out=dst, in0=bank[:, 0:na, 0:128], in1=rcb.unsqueeze(2).to_broadcast([128, na, 128]),
                op=ALU.mult), [tr, G("o1_free") if m == 0 else tpool_done, G("otmp_free")])
            gate.append(te)
            tmul.append(te)
        for a_ in range(8):
            accfree[a_] = gate[a_ // 3]
        if m == 0:
            return tmul
        tadd = DVE.add(lambda e: e.scalar_tensor_tensor(out=ob[:, :, :], in0=otmp[:, :, :], scalar=lam_t[:, l, 5:6],
                                                         in1=ob[:, :, :], op0=ALU.mult, op1=ALU.add),
                       [tmul, state["o1_tok"][j], lam_tok[l]])
        state["otmp_free"] = tadd
        return [tadd]

    def attn_prompt(l, t, tq, tkT, tv, tpool_done):
        nprev = 4 * t
        rf_attn = G("region_free")
        ta_all = []
        tlast = None
        for c in range(4):
            m, j = divmod(c, 2)
            chunks = [(i * 16, min((i + 1) * 16, nprev)) for i in range((nprev + 15) // 16)]
            cbuf = []

            def issue_chunk(ci):
                b0, b1 = chunks[ci]
                bi = ks_i[0]
                ks_i[0] ^= 1
                tkl = SP.dma(ks_ds[bi], kstream[bi][:, 0:(b1 - b0) * 128], ktscr[l, c, :, b0 * 128:b1 * 128],
                             deps=[ks_free[bi], state["kst"][t - 1], rf_attn])
                cbuf.append((bi, tkl))

            if chunks:
                issue_chunk(0)
            nkb = nprev + 4
            pend = {}

            def emit_qk(kb):
                dg = kb - nprev
                q0 = max(0, dg) * 128
                sp_i = kb % 2
                spair = psA if sp_i == 0 else psB
                pti = kb % 3
                tk = None
                if kb < nprev and kb % 16 == 0 and kb // 16 + 1 < len(chunks):
                    issue_chunk(kb // 16 + 1)
                for hl in range(2):
                    if kb < nprev:
                        bi, tkl = cbuf[kb // 16]
                        lo = (kb % 16) * 128
                        ksrc = kstream[bi][hl * 64:(hl + 1) * 64, lo:lo + 128]
                        kdep = tkl
                    else:
                        ksrc = kTcur[hl * 64:(hl + 1) * 64, c, dg * 128:(dg + 1) * 128]
                        kdep = tkT[c]
                    qsrc = qT[hl * 64:(hl + 1) * 64, c, q0:512]
                    tk = PE.add(lambda e, hl=hl, ksrc=ksrc, q0=q0, spair=spair, qsrc=qsrc: e.matmul(
                        spair[:, hl, q0:512], lhsT=ksrc, rhs=qsrc,
                        start=True, stop=True),
                        [kdep, tq[c], slot_free[sp_i * 2], slot_free[sp_i * 2 + 1]] if hl == 0 else (),
                        sig=(hl == 1))
                if kb < nprev and (kb % 16 == 15 or kb == nprev - 1):
                    ks_free[cbuf[kb // 16][0]] = tk
                te = ACT.add(lambda e, spair=spair, pti=pti, q0=q0: e.activation(
                    out=PT[pti][:, :, q0:512], in_=spair[:, :, q0:512], func=AF.Exp, scale=0.125),
                    [tk, ptfree[pti]])
                slot_free[sp_i * 2] = te
                slot_free[sp_i * 2 + 1] = te
                if dg >= 0:
                    te = POOL.add(lambda e, pti=pti, q0=q0: e.memset(PT[pti][64:128, :, q0:q0 + 64], 0.0), [te])
                pend[kb] = te

            def emit_pv(kb):
                dg = kb - nprev
                q0 = max(0, dg) * 128
                pti = kb % 3
                te = pend.pop(kb)
                tk2 = None
                for hl in range(2):
                    h = 2 * j + hl
                    for qs in range(q0 // 128, 4):
                        a_ = hl * 4 + qs
                        first = (kb == 0)
                        lastq = (kb == nprev + qs)
                        dps = [te, tv] if (hl == 0 and qs == q0 // 128) else []
                        if first:
                            dps = dps + [accfree[a_]]
                        stf = first and (a_ % 3 == 0)
                        tk2 = PE.add(lambda e, a_=a_, hl=hl, qs=qs, h=h, stf=stf, lastq=lastq: e.matmul(
                            acc_ap(a_), lhsT=PT[pti][:, hl, qs * 128:(qs + 1) * 128], rhs=Vres[:, kb, h, :],
                            start=stf, stop=lastq, skip_group_check=True), dps, sig=(hl == 1 and qs == 3))
                ptfree[pti] = tk2
                return tk2

            tk2 = None
            for step in range(nkb + 1):
                if step < nkb:
                    emit_qk(step)
                if step >= 1:
                    tk2 = emit_pv(step - 1)
            tlast = tk2
            tev = evac_pass_fast(l, m, j, tlast, tpool_done)
            if m == 0:
                state["o1_tok"][j] = tev
            else:
                ta_all.append(subln_fast(l, j, tev))
        state["vres_free"] = tlast
        return ta_all

    def issue_k_load(l, bq, deps):
        return POOL.dma(kc_ds, kc_tok[:, :, :], cache_k[l, bq].rearrange("(k p) c -> p k c", p=128), deps=deps)

    def issue_v_load(l, bq, deps):
        t2 = None
        for hh in range(4):
            t2 = POOL.dma(misc_ds, V_s[:, 0:NCB, hh, 0:128],
                          cache_v[l, bq][:, hh, :].rearrange("(k p) d -> p k d", p=128), deps=deps)
        t3 = POOL.add(lambda e: e.memset(V_s[:, :, :, 128:129], 1.0), deps)
        return [t2, t3]

    def attn_sample(l, tq, tkT, tv):
        ta_all = []
        pre = state.pop("samp_pre")
        tk_load, tv_load = pre
        for bq in range(NBS):
            r0 = (bq % 2) * 64
            prev = [G("samp_free"), G("vres_free")]
            tc = [tk_load, tv_load]
            tkt = []
            for c in range(4):
                for k0 in range(0, NCB, 4):
                    nn = min(4, NCB - k0)
                    i, view, tp = tr_group([kc_tok[:, k0 + kk, c * 128:(c + 1) * 128] for kk in range(nn)], [tk_load])
                    te = ev_copy(DVE, kT_s[:, c, k0 * 128:(k0 + nn) * 128], view[:, 0:nn * 128], [tp, prev])
                    slot_free[i] = te
                    tkt.append(te)
            tvn = DVE.add(lambda e, bq=bq: e.tensor_copy(out=V_s[0:64, NCB, :, 0:128], in_=vnew[bq][0:64, :, 0:128]),
                          [tv, tv_load])
            if bq + 1 < NBS:
                tk_load = issue_k_load(l, bq + 1, [tp])
            lastpe = None
            for c in range(4):
                m, j = divmod(c, 2)
                pend = {}
                groups = [list(range(g0, min(g0 + 8, NCB))) for g0 in range(0, NCB, 8)] + [[NCB]]

                def s_qk(gi):
                    kbs = groups[gi]
                    nk = 128 if kbs[0] < NCB else 64
                    sp_i = gi % 2
                    spair = psA if sp_i == 0 else psB
                    pti = gi % 3
                    tk = None
                    n = len(kbs)
                    for ki, kb in enumerate(kbs):
                        for hl in range(2):
                            if kb < NCB:
                                ksrc = kT_s[hl * 64:(hl + 1) * 64, c, kb * 128:(kb + 1) * 128]
                            else:
                                ksrc = kTcur[hl * 64:(hl + 1) * 64, c, bq * 64:(bq + 1) * 64]
                            qsrc = qT[hl * 64:(hl + 1) * 64, c, bq * 64:(bq + 1) * 64]
                            osl = spair[0:nk, hl, ki * 64:(ki + 1) * 64]
                            first = (ki == 0 and hl == 0)
                            tk = PE.add(lambda e, ksrc=ksrc, qsrc=qsrc, osl=osl: e.matmul(
                                osl, lhsT=ksrc, rhs=qsrc, start=True, stop=True, skip_group_check=True),
                                [tkt, tkT[c], tq[c], slot_free[sp_i * 2], slot_free[sp_i * 2 + 1]] if first else (),
                                sig=(ki == n - 1 and hl == 1))
                    src_ap = spair[0:nk, :, 0:n * 64]
                    dst_ap = PT[pti][0:nk, :, 0:n * 64]
                    te = ACT.add(lambda e, src_ap=src_ap, dst_ap=dst_ap: e.activation(
                        out=dst_ap, in_=src_ap, func=AF.Exp, scale=0.125), [tk, ptfree[pti]])
                    slot_free[sp_i * 2] = te
                    slot_free[sp_i * 2 + 1] = te
                    pend[gi] = te

                def s_pv(gi):
                    kbs = groups[gi]
                    nk = 128 if kbs[0] < NCB else 64
                    pti = gi % 3
                    te = pend.pop(gi)
                    tk2 = None
                    n = len(kbs)
                    for ki, kb in enumerate(kbs):
                        for hl in range(2):
                            h = 2 * j + hl
                            a_ = hl * 3
                            first = (kb == 0)
                            dps = [te, tvn] if (ki == 0 and hl == 0) else []
                            if first:
                                dps = dps + [accfree[a_]]
                            oacc = acc_ap(a_, r0, 64)
                            lsrc = PT[pti][0:nk, hl, ki * 64:(ki + 1) * 64]
                            rsrc = V_s[0:nk, kb, h, :]
                            tk2 = PE.add(lambda e, oacc=oacc, lsrc=lsrc, rsrc=rsrc, first=first, kb=kb: e.matmul(
                                oacc, lhsT=lsrc, rhs=rsrc, start=first, stop=(kb == NCB), skip_group_check=True),
                                dps, sig=(ki == n - 1 and hl == 1))
                    ptfree[pti] = tk2
                    return tk2

                tk2 = None
                for step in range(len(groups) + 1):
                    if step < len(groups):
                        s_qk(step)
                    if step >= 1:
                        tk2 = s_pv(step - 1)
                lastpe = tk2
                tev = evac_pass(l, m, j, lastpe, r0, 64, [0, 3])
                if m == 0:
                    state["o1_tok"][j] = tev
                else:
                    ta_all.append(subln(l, j, tev, r0, 64, [(hl * 3, 2 * j + hl, bq // 2) for hl in range(2)]))
            state["samp_free"] = [lastpe, ta_all[-2:]]
            if bq + 1 < NBS:
                tv_load = issue_v_load(l, bq + 1, [lastpe])
        return ta_all

    def load_mem_sample(l, bq):
        prev = [G("mem_free")]
        t1 = POOL.dma(mem_ds, mktok[:, :, :], cache_mem_k[l, bq].rearrange("(s p) c -> p s c", p=128),
                      deps=[prev, G("xn_free")])
        t2 = POOL.dma(mem_ds, mv[:, :, :], cache_mem_v[l, bq].rearrange("(s p) c -> p s c", p=128), deps=[prev])
        toks = []
        tp = None
        for cc in range(8):
            i, view, tp = tr_group([mktok[:, s, cc * 128:(cc + 1) * 128] for s in range(2)], [t2, prev])
            te = ev_copy(evac_eng(), mkT[:, cc, :], view[:, 0:256], [tp])
            slot_free[i] = te
            toks.append(te)
        state["xn_free"] = tp
        return toks + [t2]

    def load_pool_state(l):
        uview = uext[:, :, 0:4 * 80].rearrange("p g (b c) -> p g b c", b=4)
        wdeps = [G("uext_free"), G("region_free")]
        toks = []
        for bq in range(NBS):
            si = get_stage()
            t1 = POOL.dma(ps_ds[bq], stage[si][0:15, :], state_pool[l, bq], deps=[stage_free[si]])
            tp = None
            for g in range(4):
                i, view, tp = tr_group([stage[si][0:15, g * 128:(g + 1) * 128]], [t1], f32=True)
                te = DVE.add(lambda e, g=g, bq=bq, view=view: e.tensor_copy(out=uview[:, g, bq, 1:16], in_=view[:, 0:15]),
                             [tp, wdeps])
                slot_free[i] = te
                toks.append(te)
            stage_free[si] = tp
        tz = POOL.add(lambda e: e.memset(uview[:, :, :, 0:1], 0.0), wdeps)
        return toks + [tz]

    def run_tile(l, t):
        is_s = (t == NTP)
        ns = 2 if is_s else 4
        ntok = ns * 128
        row0 = S if is_s else t * 512
        orow = 0 if is_s else row0
        groups = [(b_ * 64, 64) for b_ in range(4)] if is_s else [(s * 128, 128) for s in range(4)]
        lb = 64 if is_s else 512
        nb = 4 if is_s else 1
        uview = uext[:, :, 0:nb * (16 + lb)].rearrange("p g (b c) -> p g b c", b=nb)
        if l == 0:
            src_x = (x_sample if is_s else x_prompt[row0:row0 + ntok, :])
        else:
            src_x = xscr[row0:row0 + ntok, :]
        xf = G("xt_free")
        tl = []
        for s in range(ns):
            dps = [xf[s] if isinstance(xf, list) and s < len(xf) else xf, G("xs_all") if l > 0 else None]
            tl.append(SP.dma(xl_ds[s], xt[:, s, :], src_x[s * 128:(s + 1) * 128, :], deps=dps))
        st["per_s"] = True
        hts = rmsnorm_hT(ns, gcols[:, l, 0, :], tl, [G("hT_free")])
        st["per_s"] = False
        rf = G("region_free")
        if is_s:
            d0 = [G("samp_free"), G("vres_free")]
            state["samp_pre"] = (issue_k_load(l, 0, d0), issue_v_load(l, 0, d0))
        if not is_s:
            state["hist_tok"] = POOL.add(lambda e: e.tensor_copy(out=uext[:, :, 0:16], in_=hist_keep[:, :, :]),
                                         [rf, G("hk_tok"), G("uext_free")])
        b_, tsl = load_slab(l, 0)
        slab = slabs[b_]
        tq = []
        lastpe = None
        for cc in range(4):
            i, tk = mm_group(lambda k, cc=cc, slab=slab: slab[:, k, cc * 128:(cc + 1) * 128],
                             lambda k: hT[:, k, 0:ntok], 8, ntok, [tsl], kdeps=hts)
            te = ev_copy(evac_eng(), qT[:, cc, 0:ntok], slot_aps[i][:, 0:ntok], [tk, rf])
            slot_free[i] = te
            tq.append(te)
            lastpe = tk
        slab_free[b_] = lastpe
        b_, tsl = load_slab(l, 1)
        slab = slabs[b_]
        tkT = []
        for cc in range(4):
            i, tk = mm_group(lambda k, cc=cc, slab=slab: slab[:, k, cc * 128:(cc + 1) * 128],
                             lambda k: hT[:, k, 0:ntok], 8, ntok, [tsl], kdeps=hts)
            te = ev_copy(evac_eng(), kTcur[:, cc, 0:ntok], slot_aps[i][:, 0:ntok], [tk, rf, G("kst_last")])
            slot_free[i] = te
            tkT.append(te)
        if not is_s:
            tks_ = POOL.dma(kst_ds[t % 2], ktscr[l, :, :, t * 512:(t + 1) * 512].rearrange("c p k -> p c k"),
                            kTcur[:, :, :], deps=[tkT])
            state["kst"][t] = tks_
            state["kst_last"] = tks_
            out_toks.append(tks_)
        for gi, (g0, gn) in enumerate(groups):
            i, tk = mm_group(lambda k, g0=g0, gn=gn: hT[:, k, g0:g0 + gn],
                             lambda k, slab=slab: slab[:, k, :], 8, 512, [tsl], m=gn, kdeps=hts)
            si = get_stage()
            te = ev_copy(evac_eng(), stage[si][0:gn, :], slot_aps[i][0:gn, :], [tk, stage_free[si]])
            slot_free[i] = te
            dst = (k_sample if is_s else k_prompt)[l, orow + g0:orow + g0 + gn, :]
            store_stage(si, dst, stage[si][0:gn, :], te)
            lastpe = tk
        slab_free[b_] = lastpe
        b_, tsl = load_slab(l, 2)
        slab = slabs[b_]
        tv = []
        for gi, (g0, gn) in enumerate(groups):
            i, tk = mm_group(lambda k, g0=g0, gn=gn: hT[:, k, g0:g0 + gn],
                             lambda k, slab=slab: slab[:, k, :], 8, 512, [tsl], m=gn, kdeps=hts)
            si = get_stage()
            te = ev_copy(ACT, stage[si][0:gn, :], slot_aps[i][0:gn, :], [tk, stage_free[si]])
            dst = (v_sample if is_s else v_prompt)[l, orow + g0:orow + g0 + gn, :]
            store_stage(si, dst, stage[si][0:gn, :], te)
            vdst = vnew[gi][0:gn, :, 0:128] if is_s else Vres[:, t * 4 + gi, :, 0:128]
            te2 = DVE.add(lambda e, i=i, gn=gn, vdst=vdst: e.tensor_copy(
                out=vdst, in_=slot_aps[i][0:gn, :].rearrange("p (h d) -> p h d", h=4)),
                [tk, te, G("vres_wr"), rf])
            slot_free[i] = [te, te2]
            tv.append(te2)
            lastpe = tk
        slab_free[b_] = lastpe
        b_, tsl = load_slab(l, 3)
        slab = slabs[b_]
        tu = []
        for g in range(4):
            i, tk = mm_group(lambda k, g=g, slab=slab: slab[:, k, g * 128:(g + 1) * 128],
                             lambda k: hT[:, k, 0:ntok], 8, ntok, [tsl], kdeps=hts)
            te = ev_copy(evac_eng(), uview[:, g, :, 16:16 + lb],
                         slot_aps[i][:, 0:ntok].rearrange("p (b c) -> p b c", b=nb),
                         [tk, G("uext_free"), G("hist_tok"), rf])
            slot_free[i] = te
            tu.append(te)
            lastpe = tk
        if is_s or t == NTP - 1:
            for gi, (g0, gn) in enumerate(groups):
                if (not is_s) and gi != 3:
                    continue
                i, tk = mm_group(lambda k, g0=g0, gn=gn: hT[:, k, g0:g0 + gn],
                                 lambda k, slab=slab: slab[:, k, :], 8, 512, [tsl], m=gn, kdeps=hts)
                si = get_stage()
                te = ev_copy(evac_eng(), stage[si][0:gn, :], slot_aps[i][0:gn, :], [tk, stage_free[si]])
                slot_free[i] = te
                dst = pool_sample[l, gi] if is_s else pool_prompt[l]
                store_stage(si, dst, stage[si][gn - 15:gn, :], te)
                lastpe = tk
        slab_free[b_] = lastpe
        state["hT_free"] = lastpe
        first_stream = (not is_s) and t == 0
        tpool = []
        tpy = []
        for g in range(4):
            w = 2 << g
            cur = uview[:, g, :, :]
            lo = 1
            tk = [tu[g]]
            for lev in range(g + 1):
                sh = 1 << lev
                nlo = lo + sh
                dstb = utmp[lev % 2][:, 0:nb * (16 + lb)].rearrange("p (b c) -> p b c", b=nb)
                tk = POOL.add(lambda e, dstb=dstb, cur=cur, nlo=nlo, sh=sh: e.tensor_tensor(
                    out=dstb[:, :, nlo:16 + lb], in0=cur[:, :, nlo:16 + lb],
                    in1=cur[:, :, nlo - sh:16 + lb - sh], op=ALU.add), [tk, G("utmp_free")])
                cur = dstb
                lo = nlo
            pb2 = pooled2[g]
            pvw = pb2[:, 0:ntok].rearrange("p (b c) -> p b c", b=nb)
            tk2 = DVE.add(lambda e, pvw=pvw, cur=cur, g=g, w=w: e.scalar_tensor_tensor(
                out=pvw, in0=uview[:, g, :, 16:16 + lb], scalar=-float(w), in1=cur[:, :, 16:16 + lb],
                op0=ALU.mult, op1=ALU.add), [tk, G("pooled_free")])
            if first_stream:
                tk3 = POOL.add(lambda e, cur=cur, g=g: e.tensor_tensor(
                    out=cur[:, 0, 16:32], in0=cur[:, 0, 16:32], in1=icnt[:, g, :], op=ALU.mult), [tk2, ic_toks])
                tk2 = DVE.add(lambda e, cur=cur, g=g, pb2=pb2, w=w: e.scalar_tensor_tensor(
                    out=pb2[:, 0:16], in0=uext[:, g, 16:32], scalar=-float(w), in1=cur[:, 0, 16:32],
                    op0=ALU.mult, op1=ALU.add), [tk3])
            state["utmp_free"] = tk2
            tpool.append(tk2)
        if not is_s:
            state["hk_tok"] = POOL.add(lambda e: e.tensor_copy(out=hist_keep[:, :, :], in_=uext[:, :, 512:528]), [tpool])
            state["uext_free"] = [state["hk_tok"]] + tpool
        else:
            state["uext_free"] = tpool
        if is_s:
            ta = attn_sample(l, tq, tkT, tv)
        else:
            ta = attn_prompt(l, t, tq, tkT, tv, tpool)
        for g in range(4):
            i = get_slot()
            tk = PE.add(lambda e, i=i, g=g: e.matmul(slot_aps[i][:, 0:ntok], lhsT=wpool_bf[:, l, g, :],
                                                     rhs=pooled2[g][:, 0:ntok], start=True, stop=True),
                        [tpool[g], slot_free[i], t_wp])
            state["pooled_free"] = tk
            te = ev_scale(evac_eng(), pyT[:, g, 0:ntok], slot_aps[i][:, 0:ntok], pscol[:, l, g:g + 1],
                          [tk, t_const, G("pyT_free")])
            slot_free[i] = te
            tpy.append(te)
        taT = []
        tp = None
        for cc in range(4):
            i, view, tp = tr_group([a_tok[:, s, cc * 128:(cc + 1) * 128] for s in range(ns)], [ta])
            tk = ev_copy(evac_eng(), aT[:, cc, 0:ntok], view[:, 0:ntok], [tp, G("aT_free")])
            slot_free[i] = tk
            taT.append(tk)
        txo = []
        for h in range(2):
            b_, tsl = load_slab(l, 4 + h)
            slab = slabs[b_]
            for s in range(ns):
                i, tk = mm_group(lambda k, s=s: (aT if k < 4 else pyT)[:, k % 4, s * 128:(s + 1) * 128],
                                 lambda k, slab=slab: slab[:, k, :], 8, 512, [taT, tpy, tsl])
                te = DVE.add(lambda e, i=i, s=s, h=h: e.tensor_tensor(
                    out=xt[:, s, h * 512:(h + 1) * 512], in0=slot_aps[i][:, :],
                    in1=xt[:, s, h * 512:(h + 1) * 512], op=ALU.add), [tk])
                slot_free[i] = te
                txo.append((s, te))
                lastpe = tk
            slab_free[b_] = lastpe
        state["aT_free"] = lastpe
        state["pyT_free"] = lastpe
        regA_done = [lastpe, ta, tpool, tp, G("kst_last")]
        st["per_s"] = True
        hts = rmsnorm_hT(ns, gcols[:, l, 1, :], [[te for (s2, te) in txo if s2 == s] for s in range(ns)], [G("hT_free")])
        st["per_s"] = False
        tqx = []
        for h in range(2):
            b_, tsl = load_slab(l, 6 + h)
            slab = slabs[b_]
            for cc in range(4):
                i, tk = mm_group(lambda k, cc=cc, slab=slab: slab[:, k, cc * 128:(cc + 1) * 128],
                                 lambda k: hT[:, k, 0:ntok], 8, ntok, [tsl], kdeps=hts)
                te = ev_copy(evac_eng(), qxT[:, h * 4 + cc, 0:ntok], slot_aps[i][:, 0:ntok], [tk, regA_done])
                slot_free[i] = te
                tqx.append(te)
                lastpe = tk
            slab_free[b_] = lastpe
        state["hT_free"] = lastpe
        state["ox_toks"] = []
        state["ox_free"] = regA_done
        if is_s:
            for bq in range(NBS):
                tmk = load_mem_sample(l, bq)
                lastpe = cross_attn(l, bq * 64, 64, [tqx, tmk])
                state["mem_free"] = lastpe
        else:
            lastpe = cross_attn(l, 0, 512, [tqx, G("memkv_tok")])
            state["mem_free"] = lastpe
        tox = state["ox_toks"]
        txo = []
        for h in range(2):
            b_, tsl = load_slab(l, 12 + h)
            slab = slabs[b_]
            for s in range(ns):
                i, tk = mm_group(lambda k, s=s: oxT[:, k, s * 128:(s + 1) * 128],
                                 lambda k, slab=slab: slab[:, k, :], 8, 512, [tsl, tox])
                te = DVE.add(lambda e, i=i, s=s, h=h: e.tensor_tensor(
                    out=xt[:, s, h * 512:(h + 1) * 512], in0=slot_aps[i][:, :],
                    in1=xt[:, s, h * 512:(h + 1) * 512], op=ALU.add), [tk])
                slot_free[i] = te
                txo.append((s, te))
                lastpe = tk
            slab_free[b_] = lastpe
        st["per_s"] = True
        hts = rmsnorm_hT(ns, gcols[:, l, 3, :], [[te for (s2, te) in txo if s2 == s] for s in range(ns)], [G("hT_free")])
        st["per_s"] = False
        txm = []
        for qd in range(4):
            hb = qd % 2
            thid = []
            for h in range(2):
                b_, tsl = load_slab(l, 14 + qd * 2 + h)
                slab = slabs[b_]
                for cc in range(4):
                    i, tk = mm_group(lambda k, cc=cc, slab=slab: slab[:, k, cc * 128:(cc + 1) * 128],
                                     lambda k: hT[:, k, 0:ntok], 8, ntok, [tsl], kdeps=hts)
                    rb = (h * 4 + cc) % 2
                    te = ACT.add(lambda e, i=i, rb=rb: e.activation(out=rtmp[rb][:, 0:ntok], in_=slot_aps[i][:, 0:ntok],
                                                                  func=AF.Relu), [tk, G("rt_free%d" % rb), regA_done])
                    slot_free[i] = te
                    tsq = POOL.add(lambda e, rb=rb, hb=hb, h=h, cc=cc: e.tensor_tensor(
                        out=hid[hb][:, h * 4 + cc, 0:ntok], in0=rtmp[rb][:, 0:ntok], in1=rtmp[rb][:, 0:ntok],
                        op=ALU.mult), [te, G("hid_free%d" % hb), regA_done])
                    state["rt_free%d" % rb] = tsq
                    thid.append(tsq)
                    lastpe = tk
                slab_free[b_] = lastpe
            for h in range(2):
                b_, tsl = load_slab(l, 22 + qd * 2 + h)
                slab = slabs[b_]
                for s in range(ns):
                    i, tk = mm_group(lambda k, s=s, hb=hb: hid[hb][:, k, s * 128:(s + 1) * 128],
                                     lambda k, slab=slab: slab[:, k, :], 8, 512, [tsl], kdeps=thid)
                    te = DVE.add(lambda e, i=i, s=s, h=h: e.tensor_tensor(
                        out=xt[:, s, h * 512:(h + 1) * 512], in0=slot_aps[i][:, :],
                        in1=xt[:, s, h * 512:(h + 1) * 512], op=ALU.add), [tk])
                    slot_free[i] = te
                    if qd == 3:
                        txm.append((s, te))
                    lastpe = tk
                slab_free[b_] = lastpe
            state["hid_free%d" % hb] = lastpe
        state["hT_free"] = lastpe
        state["region_free"] = [lastpe, tox]
        txs = [[te for (s2, te) in txm if s2 == s] for s in range(ns)]
        stores = []
        if l == DEPTH - 1:
            st["per_s"] = True
            rs, t3 = rms_rstd(ns, txs, G("xn_free"))
            st["per_s"] = False
            ydst = y_sample if is_s else y_prompt[row0:row0 + ntok, :]
            for s in range(ns):
                ty = DVE.add(lambda e, s=s: e.scalar_tensor_tensor(
                    out=xt[:, s, :], in0=xt[:, s, :], scalar=rs[:, s:s + 1], in1=fg_bc[:, :],
                    op0=ALU.mult, op1=ALU.mult), [t3, t_const])
                stores.append(POOL.dma(xs_ds[s], ydst[s * 128:(s + 1) * 128, :], xt[:, s, :], deps=[ty]))
        else:
            for s in range(ns):
                stores.append(POOL.dma(xs_ds[s], xscr[row0 + s * 128:row0 + (s + 1) * 128, :], xt[:, s, :],
                                       deps=[txs[s]]))
        out_toks.extend(stores)
        for s in range(ns):
            xt_store[s] = stores[s]
        state["xt_free"] = list(xt_store)
        state["xs_all"] = list(xt_store)

    def schedule():
        step = 0
        for l in range(DEPTH):
            if upto is not None and step >= upto:
                return
            t_v1 = POOL.add(lambda e: e.memset(Vres[:, :, :, 128:129], 1.0), [G("samp_free")])
            state["vres_wr"] = [t_v1, G("samp_free")]
            state["memkv_tok"] = [mem_kv_prompt(l)]
            state["hk_tok"] = POOL.add(lambda e: e.memset(hist_keep[:, :, :], 0.0), [G("hist_tok")])
            step += 1
            for t in range(NTP):
                if upto is not None and step >= upto:
                    return
                run_tile(l, t)
                step += 1
            if upto is not None and step >= upto:
                return
            state["hist_tok"] = load_pool_state(l)
            run_tile(l, NTP)
            step += 1

    try:
        schedule()
    except _Stop:
        pass
    SP.wait_only(out_toks)

    block = es.enter_context(nc.Block())

    @block.tensor
    def _(e):
        PE.replay(e)

    @block.scalar
    def _(e):
        ACT.replay(e)

    @block.vector
    def _(e):
        DVE.replay(e)

    @block.gpsimd
    def _(e):
        POOL.replay(e)

    @block.sync
    def _(e):
        SP.replay(e)

    es.close()
    return nc


def make_in_maps(inputs, ncores=8, S=8192, PAST=2048):
    f = lambda a: np.ascontiguousarray(np.asarray(a, dtype=np.float32))
    maps = []
    for c in range(ncores):
        sl = slice(c * NBS, (c + 1) * NBS)
        m = {
            "x_prompt": f(inputs["x_prompt"][c]),
            "x_sample": f(inputs["x_sample"][sl]).reshape(NBS * SSEQ, D),
            "cache_k": f(inputs["cache_k"][:, sl]).reshape(DEPTH, NBS, PAST, 512),
            "cache_v": f(inputs["cache_v"][:, sl]),
            "state_pool": f(inputs["state_pool"][:, sl]),
            "cache_mem_k": f(inputs["cache_mem_k"][:, sl]).reshape(DEPTH, NBS, NMEM, D),
            "cache_mem_v": f(inputs["cache_mem_v"][:, sl]).reshape(DEPTH, NBS, NMEM, D),
            "mem_prompt": f(inputs["mem_prompt"][c]),
            "lam_q": f(inputs["lam_q"]).reshape(DEPTH, 128),
            "lam_k": f(inputs["lam_k"]).reshape(DEPTH, 128),
            "final_g": f(inputs["final_g"]).reshape(1, D),
        }
        for k in ["norm_mix_g", "w_in", "subln_g", "w_pool", "pool_scale", "w_out", "norm_x_g", "norm_mem_g",
                  "wq_x", "wk_x", "wv_x", "wo_x", "norm_mlp_g", "w_up", "w_down"]:
            m[k] = f(inputs[k])
        maps.append(m)
    return maps


def assemble(results, ncores=8, S=8192):
    def cat(name, axis, shape_fn):
        return np.concatenate([shape_fn(r[name]) for r in results], axis=axis)
    y_prompt = np.stack([r["y_prompt"] for r in results], 0)
    y_sample = np.concatenate([r["y_sample"].reshape(NBS, SSEQ, D) for r in results], 0)
    k_prompt = np.stack([r["k_prompt"].reshape(DEPTH, S, 2, 4, 64) for r in results], 1)
    v_prompt = np.stack([r["v_prompt"].reshape(DEPTH, S, 4, 128) for r in results], 1)
    pool_prompt = np.stack([r["pool_prompt"] for r in results], 1)
    mem_k = np.stack([r["mem_k_prompt"].reshape(DEPTH, NMEM, 4, 256) for r in results], 1)
    mem_v = np.stack([r["mem_v_prompt"].reshape(DEPTH, NMEM, 4, 256) for r in results], 1)
    k_sample = np.concatenate([r["k_sample"].reshape(DEPTH, NBS, SSEQ, 2, 4, 64) for r in results], 1)
    v_sample = np.concatenate([r["v_sample"].reshape(DEPTH, NBS, SSEQ, 4, 128) for r in results], 1)
    pool_sample = np.concatenate([r["pool_sample"] for r in results], 1)
    outs = (y_prompt, y_sample, k_prompt, v_prompt, pool_prompt, mem_k, mem_v, k_sample, v_sample, pool_sample)
    return tuple(np.ascontiguousarray(o, dtype=np.float32) for o in outs)


def kernel(**inputs):
    ncores = 8
    nc = build_program()
    in_maps = make_in_maps(inputs, ncores)
    res = run_bass_kernel_spmd(nc, in_maps, core_ids=list(range(ncores)))
    return assemble(res.results, ncores)
```

```python
import math
from contextlib import ExitStack

import numpy as np
import concourse.bass as bass
import concourse.mybir as mybir
from concourse.bass_utils import run_bass_kernel_spmd

F32 = mybir.dt.float32
BF16 = mybir.dt.bfloat16
AF = mybir.ActivationFunctionType
ALU = mybir.AluOpType

D = 1024
DEPTH = 2
NBS = 4
SSEQ = 64
NMEM = 256
NSLAB = 30


def _flat(deps):
    if deps is None:
        return
    if isinstance(deps, tuple) and len(deps) == 2 and isinstance(deps[1], int):
        yield deps
        return
    for d in deps:
        yield from _flat(d)


class DSem:
    def __init__(self, sem):
        self.sem = sem
        self.n = 0


class Eng:
    def __init__(self, name, sem):
        self.name = name
        self.sem = sem
        self.n = 0
        self.seen = {}
        self.prog = []
        self.chain = (name in ("act", "dve", "pool"))

    def _waits(self, deps):
        waits = []
        for d in _flat(deps):
            sem, val = d
            k = id(sem)
            if self.seen.get(k, 0) >= val:
                continue
            self.seen[k] = val
            waits.append((sem, val))
        return waits

    def add(self, fn, deps=(), sig=True):
        if self.chain and self.n > 0:
            deps = [deps, (self.sem, self.n)]
        waits = self._waits(deps)
        tok = None
        if sig:
            self.n += 1
            tok = (self.sem, self.n)
        self.prog.append((waits, fn, 1 if sig else 0, None))
        return tok

    def dma(self, dsem, out, in_, deps=(), **kw):
        waits = self._waits(deps)
        dsem.n += 16
        self.prog.append((waits, lambda e: e.dma_start(out=out, in_=in_, **kw), 2, dsem.sem))
        return (dsem.sem, dsem.n)

    def wait_only(self, deps):
        waits = self._waits(deps)
        if waits:
            self.prog.append((waits, None, 0, None))

    def replay(self, e):
        for waits, fn, kind, dsem in self.prog:
            for sem, val in waits:
                e.wait_ge(sem, val)
            if fn is None:
                continue
            inst = fn(e)
            if kind == 1:
                inst.then_inc(self.sem, 1)
            elif kind == 2:
                inst.then_inc(dsem, 16)


class _Stop(Exception):
    pass


def build_program(S=8192, PAST=2048, upto=None, dbg=None):
    NTP = S // 512
    NKB = S // 128
    NCB = PAST // 128
    nc = bass.Bass("TRN2", target_bir_lowering=False)
    es = ExitStack()

    def din(name, shape):
        return nc.dram_tensor(name, shape, F32, kind="ExternalInput").ap()

    def dout(name, shape):
        return nc.dram_tensor(name, shape, F32, kind="ExternalOutput").ap()

    x_prompt = din("x_prompt", [S, D])
    x_sample = din("x_sample", [NBS * SSEQ, D])
    cache_k = din("cache_k", [DEPTH, NBS, PAST, 512])
    cache_v = din("cache_v", [DEPTH, NBS, PAST, 4, 128])
    state_pool = din("state_pool", [DEPTH, NBS, 15, 512])
    cache_mem_k = din("cache_mem_k", [DEPTH, NBS, NMEM, D])
    cache_mem_v = din("cache_mem_v", [DEPTH, NBS, NMEM, D])
    mem_prompt = din("mem_prompt", [NMEM, D])
    norm_mix_g = din("norm_mix_g", [DEPTH, D])
    w_in = din("w_in", [DEPTH, D, 2048])
    lam_q = din("lam_q", [DEPTH, 128])
    lam_k = din("lam_k", [DEPTH, 128])
    subln_g = din("subln_g", [DEPTH, 128])
    w_pool = din("w_pool", [DEPTH, 4, 128, 128])
    pool_scale = din("pool_scale", [DEPTH, 512])
    w_out = din("w_out", [DEPTH, D, D])
    norm_x_g = din("norm_x_g", [DEPTH, D])
    norm_mem_g = din("norm_mem_g", [DEPTH, D])
    wq_x = din("wq_x", [DEPTH, D, D])
    wk_x = din("wk_x", [DEPTH, D, D])
    wv_x = din("wv_x", [DEPTH, D, D])
    wo_x = din("wo_x", [DEPTH, D, D])
    norm_mlp_g = din("norm_mlp_g", [DEPTH, D])
    w_up = din("w_up", [DEPTH, D, 4096])
    w_down = din("w_down", [DEPTH, 4096, D])
    final_g = din("final_g", [1, D])

    y_prompt = dout("y_prompt", [S, D])
    y_sample = dout("y_sample", [NBS * SSEQ, D])
    k_prompt = dout("k_prompt", [DEPTH, S, 512])
    v_prompt = dout("v_prompt", [DEPTH, S, 512])
    pool_prompt = dout("pool_prompt", [DEPTH, 15, 512])
    mem_k_prompt = dout("mem_k_prompt", [DEPTH, NMEM, D])
    mem_v_prompt = dout("mem_v_prompt", [DEPTH, NMEM, D])
    k_sample = dout("k_sample", [DEPTH, NBS * SSEQ, 512])
    v_sample = dout("v_sample", [DEPTH, NBS * SSEQ, 512])
    pool_sample = dout("pool_sample", [DEPTH, NBS, 15, 512])

    wscr = nc.dram_tensor("wscr", [DEPTH, NSLAB, 128, 8, 512], BF16, kind="Internal").ap()
    xscr = nc.dram_tensor("xscr", [S + NBS * SSEQ, D], F32, kind="Internal").ap()
    ktscr = nc.dram_tensor("ktscr", [DEPTH, 4, 128, S], BF16, kind="Internal").ap()

    def sb(name, shape, dt):
        return es.enter_context(nc.sbuf_tensor(name, shape, dt))

    def ps(name, shape, dt):
        return es.enter_context(nc.psum_tensor(name, shape, dt))

    def newsem(name):
        return es.enter_context(nc.semaphore(name))

    PE = Eng("pe", newsem("s_pe"))
    ACT = Eng("act", newsem("s_act"))
    DVE = Eng("dve", newsem("s_dve"))
    POOL = Eng("pool", newsem("s_pool"))
    SP = Eng("sp", newsem("s_sp"))
    _dsn = [0]

    def dsem():
        _dsn[0] += 1
        return DSem(newsem("d%d" % _dsn[0]))

    VRES_COLS = NKB * 4 * 129
    SAMP_COLS = NCB * 512 + 4 * (PAST) + (NCB + 1) * 4 * 129
    vreg = sb("vreg", [128, max(VRES_COLS, SAMP_COLS)], BF16)
    Vres = vreg[:, 0:VRES_COLS].rearrange("p (k h d) -> p k h d", h=4, d=129)
    o0 = 0
    kc_tok = vreg[:, o0:o0 + NCB * 512].rearrange("p (k c) -> p k c", c=512)
    o0 += NCB * 512
    kT_s = vreg[:, o0:o0 + 4 * PAST].rearrange("p (c k) -> p c k", c=4)
    o0 += 4 * PAST
    V_s = vreg[:, o0:o0 + (NCB + 1) * 4 * 129].rearrange("p (k h d) -> p k h d", h=4, d=129)

    xt = sb("xt", [128, 4, D], F32)
    xn = sb("xn", [128, 4, D], BF16)
    mktok = xn[:, 0:2, :]
    hT = sb("hT", [128, 8, 512], BF16)
    slabs = [sb("slab%d" % i, [128, 8, 512], BF16) for i in range(3)]
    NSTAGE = 3
    stage = [sb("stage%d" % i, [128, 512], F32) for i in range(NSTAGE)]
    aT = sb("aT", [128, 4, 512], BF16)
    pyT = sb("pyT", [128, 4, 512], BF16)
    mkT = sb("mkT", [128, 8, NMEM], BF16)
    mv = sb("mv", [128, 2, D], BF16)
    stats = sb("stats", [128, 512], F32)
    ident = sb("ident", [128, 128], BF16)
    identf = sb("identf", [128, 128], F32)
    ones_bf = sb("ones_bf", [128, 128], BF16)
    gcols = sb("gcols", [128, DEPTH, 4, 8], F32)
    pscol = sb("pscol", [128, DEPTH, 4], F32)
    fg_bc = sb("fg_bc", [128, D], F32)
    sg_bc = sb("sg_bc", [128, DEPTH, 128], F32)
    lam_t = sb("lam_t", [128, DEPTH, 8], F32)
    wpool_bf = sb("wpool_bf", [128, DEPTH, 4, 128], BF16)
    icnt = sb("icnt", [128, 4, 16], F32)
    o1b = sb("o1b", [128, 8, 128], F32)
    osq = sb("osq", [128, 128], F32)
    hist_keep = sb("hist_keep", [128, 4, 16], F32)
    sgcol = sb("sgcol", [128, DEPTH], F32)
    RB = 47 * 1024
    region = sb("region", [128, RB], mybir.dt.uint8)

    class Carver:
        def __init__(self):
            self.off = 0

        def take(self, shape, dt):
            n = int(np.prod(shape[1:]))
            bpe = 2 if dt == BF16 else 4
            nbytes = n * bpe
            off = (self.off + 63) // 64 * 64
            assert off + nbytes <= RB, (off, nbytes, RB)
            ap = region[:, off:off + nbytes].bitcast(dt)
            self.off = off + nbytes
            if len(shape) == 2:
                return ap
            names = "abcd"[:len(shape) - 1]
            pat = "p (%s) -> p %s" % (" ".join(names), " ".join(names))
            kw = {names[i]: shape[i + 1] for i in range(1, len(names))}
            return ap.rearrange(pat, **kw)

    ca = Carver()
    qT = ca.take([128, 4, 512], BF16)
    kTcur = ca.take([128, 4, 512], BF16)
    ksreg = ca.take([128, 4096 + 64], BF16)
    kstream = [ksreg[:, i * 2048:(i + 1) * 2048] for i in range(2)]
    vnew = [ksreg[0:64, i * 516:(i + 1) * 516].rearrange("p (h d) -> p h d", h=4) for i in range(4)]
    PT = [ca.take([128, 2, 512], BF16) for _ in range(3)]
    o1 = ca.take([128, 8, 128], F32)
    a_tok = ca.take([128, 4, 512], BF16)
    uext = ca.take([128, 4, 528], F32)
    utmp_all = ca.take([128, 1056], F32)
    utmp = [utmp_all[:, i * 528:(i + 1) * 528] for i in range(2)]
    otmp = utmp_all[:, 0:1024].rearrange("p (a d) -> p a d", a=8)
    pooled2 = [ca.take([128, 512], BF16) for _ in range(4)]
    lamw = region[:, 0:2048].bitcast(F32).rearrange("p (a b) -> p a b", a=4)
    cb = Carver()
    qxT = cb.take([128, 8, 512], BF16)
    oxT = cb.take([128, 8, 512], BF16)
    PxT = [cb.take([128, 2, 512], BF16) for _ in range(2)]
    rrec = [cb.take([128, 512], F32) for _ in range(2)]
    hid = [cb.take([128, 8, 512], BF16) for _ in range(2)]
    rtmp = [cb.take([128, 512], BF16) for _ in range(2)]

    psA = ps("psA", [128, 2, 512], F32)
    psB = ps("psB", [128, 2, 512], F32)
    psO = ps("psO", [128, 3, 512], F32)
    psT = ps("psT", [128, 512], F32)

    st = {"stat": 0, "slab": 0, "slot": 0, "stage": 0, "ev": 0, "tb": 0}
    slab_free = [None, None, None]
    slab_ds = [dsem() for _ in range(3)]
    slot_aps = [psA[:, 0, :], psA[:, 1, :], psB[:, 0, :], psB[:, 1, :], psT[:, :]]
    slot_bf = [a_.bitcast(BF16) for a_ in slot_aps]
    NSLOT = 5
    slot_free = [None] * NSLOT
    stage_ds = [dsem() for _ in range(NSTAGE)]
    stage_free = [None] * NSTAGE
    conv_tok = {}
    out_toks = []
    state = {"hT_free": None, "xt_free": None, "region_free": None, "kst": [None] * max(NTP, 1),
             "o1_tok": [None, None]}

    def G(k):
        return state.get(k)

    ckn = [0]

    def ck(n=None):
        ckn[0] += 1
        if dbg == ckn[0]:
            raise _Stop()

    def stat(n):
        if st["stat"] + n > 512:
            st["stat"] = 0
        a_ = stats[:, st["stat"]:st["stat"] + n]
        st["stat"] += n
        return a_

    def get_slot():
        i = st["slot"]
        st["slot"] = (i + 1) % NSLOT
        return i

    def tr_group(ins, deps, f32=False):
        i = get_slot()
        view = slot_aps[i] if f32 else slot_bf[i]
        idt = identf if f32 else ident
        tp = None
        off = 0
        for k, in_ap in enumerate(ins):
            n = in_ap.shape[0]
            tp = PE.add(lambda e, in_ap=in_ap, off=off, n=n: e.transpose(
                out=view[:, off:off + n], in_=in_ap, identity=idt[0:n, 0:n]),
                [deps, slot_free[i], t_ident, t_idf] if k == 0 else (), sig=(k == len(ins) - 1))
            off += n
        return i, view, tp

    def evac_eng():
        st["ev"] ^= 1
        return ACT if st["ev"] else DVE

    def ev_copy(eng, out, in_, deps):
        if eng is ACT:
            return ACT.add(lambda e: e.copy(out=out, in_=in_), deps)
        return eng.add(lambda e: e.tensor_copy(out=out, in_=in_), deps)

    def ev_scale(eng, out, in_, sc, deps):
        if eng is ACT:
            return ACT.add(lambda e: e.mul(out=out, in_=in_, mul=sc), deps)
        return eng.add(lambda e: e.tensor_scalar(out=out, in0=in_, scalar1=sc, scalar2=None, op0=ALU.mult), deps)

    def load_slab(l, idx):
        b_ = st["slab"]
        st["slab"] = (b_ + 1) % 3
        tok = SP.dma(slab_ds[b_], slabs[b_][:], wscr[l, idx], deps=[conv_tok[(l, idx)], slab_free[b_]])
        return b_, tok

    def get_stage():
        i = st["stage"]
        st["stage"] = (i + 1) % NSTAGE
        return i

    def store_stage(i, dram_ap, src_ap, dep):
        tok = POOL.dma(stage_ds[i], dram_ap, src_ap, deps=[dep])
        stage_free[i] = tok
        out_toks.append(tok)
        return tok

    def acc_ap(a_, r0=0, rows=128, cols=129):
        return psO[r0:r0 + rows, a_ // 3, (a_ % 3) * 160:(a_ % 3) * 160 + cols]

    t_id0 = POOL.add(lambda e: e.memset(identf[:], 0.0))
    t_idf = POOL.add(lambda e: e.affine_select(out=identf[:], in_=identf[:], pattern=[[-1, 128]],
                                               compare_op=ALU.not_equal, fill=1.0, base=0,
                                               channel_multiplier=1), [t_id0])
    t_ident = POOL.add(lambda e: e.tensor_copy(out=ident[:], in_=identf[:]), [t_idf])
    t_ones = POOL.add(lambda e: e.memset(ones_bf[:], 1.0))
    ic_toks = []
    for g in range(4):
        w = 2 << g
        ic_toks.append(POOL.add(lambda e, g=g, w=w: e.memset(icnt[:, g, :], 1.0)))
        for tt in range(min(w - 1, 16)):
            ic_toks.append(POOL.add(lambda e, g=g, tt=tt, w=w: e.memset(icnt[:, g, tt:tt + 1], float(w) / (tt + 1))))
    c_ds = dsem()
    t_const = None
    gsrc = [norm_mix_g, norm_x_g, norm_mem_g, norm_mlp_g]
    for l in range(DEPTH):
        for i, gs in enumerate(gsrc):
            t_const = SP.dma(c_ds, gcols[:, l, i, :], gs[l].rearrange("(c p) -> p c", p=128),
                             allow_slow_non_contiguous=True)
        t_const = SP.dma(c_ds, pscol[:, l, :], pool_scale[l].rearrange("(c p) -> p c", p=128),
                         allow_slow_non_contiguous=True)
        t_const = SP.dma(c_ds, sg_bc[:, l, :], subln_g[l:l + 1, :].broadcast_to([128, 128]))
        t_const = SP.dma(c_ds, sgcol[:, l:l + 1], subln_g[l:l + 1, :].rearrange("o d -> d o"),
                         allow_slow_non_contiguous=True)
        t_const = SP.dma(c_ds, lamw[:, l * 2 + 0, :], lam_q[l:l + 1, :].broadcast_to([128, 128]))
        t_const = SP.dma(c_ds, lamw[:, l * 2 + 1, :], lam_k[l:l + 1, :].broadcast_to([128, 128]))
    t_const = SP.dma(c_ds, fg_bc[:], final_g[0:1, :].broadcast_to([128, D]))
    for l in range(DEPTH):
        for g in range(4):
            t_const = DVE.add(lambda e, l=l, g=g: e.tensor_scalar(out=pscol[:, l, g:g + 1], in0=pscol[:, l, g:g + 1],
                                                                 scalar1=1.0 / (2 << g), scalar2=None, op0=ALU.mult),
                              [t_const])
    wp_ds = dsem()
    t_wp = None
    for l in range(DEPTH):
        t_wp = POOL.dma(wp_ds, wpool_bf[:, l, :, :], w_pool[l].rearrange("g c d -> c g d"))

    def conv(l, idx, src2d):
        d_ = dsem()
        conv_tok[(l, idx)] = POOL.dma(d_, wscr[l, idx], src2d.rearrange("(kc p) c -> p kc c", p=128))

    def conv_layer(l):
        for h in range(2):
            conv(l, 8 + h, wk_x[l][:, h * 512:(h + 1) * 512])
        for h in range(2):
            conv(l, 10 + h, wv_x[l][:, h * 512:(h + 1) * 512])
        for s in range(4):
            conv(l, s, w_in[l][:, s * 512:(s + 1) * 512])
        for h in range(2):
            conv(l, 4 + h, w_out[l][:, h * 512:(h + 1) * 512])
        for h in range(2):
            conv(l, 6 + h, wq_x[l][:, h * 512:(h + 1) * 512])
        for h in range(2):
            conv(l, 12 + h, wo_x[l][:, h * 512:(h + 1) * 512])
        for qd in range(4):
            for h in range(2):
                conv(l, 14 + qd * 2 + h, w_up[l][:, (qd * 2 + h) * 512:(qd * 2 + h + 1) * 512])
            for h in range(2):
                conv(l, 22 + qd * 2 + h, w_down[l][qd * 1024:(qd + 1) * 1024, h * 512:(h + 1) * 512])

    conv_layer(0)
    conv_layer(1)

    lam_inits = [0.8 - 0.6 * math.exp(-0.3 * l) for l in range(DEPTH)]
    lam_tok = []
    for l in range(DEPTH):
        tk = None
        for i in range(2):
            tk = DVE.add(lambda e, l=l, i=i: e.scalar_tensor_tensor(
                out=osq[:, i * 64:(i + 1) * 64], in0=lamw[:, l * 2, i * 64:(i + 1) * 64], scalar=1.0,
                in1=lamw[:, l * 2 + 1, i * 64:(i + 1) * 64], op0=ALU.mult, op1=ALU.mult,
                accum_out=lam_t[:, l, i:i + 1]), [t_const, tk])
        t1 = ACT.add(lambda e, l=l: e.activation(out=lam_t[:, l, 2:4], in_=lam_t[:, l, 0:2], func=AF.Exp), [tk])
        t2 = DVE.add(lambda e, l=l: e.tensor_tensor(out=lam_t[:, l, 4:5], in0=lam_t[:, l, 2:3],
                                                    in1=lam_t[:, l, 3:4], op=ALU.subtract), [t1])
        t3 = DVE.add(lambda e, l=l: e.tensor_scalar(out=lam_t[:, l, 5:6], in0=lam_t[:, l, 4:5],
                                                    scalar1=-1.0, scalar2=-lam_inits[l],
                                                    op0=ALU.mult, op1=ALU.add), [t2])
        t4 = DVE.add(lambda e, l=l: e.tensor_scalar(out=sgcol[:, l:l + 1], in0=sgcol[:, l:l + 1],
                                                    scalar1=1.0 - lam_inits[l], scalar2=None,
                                                    op0=ALU.mult), [t_const, t3])
        lam_tok.append(t4)
    state["region_free"] = list(lam_tok)

    def rms_rstd(ns, deps_x, junk_deps):
        ss = stat(ns)
        rs = stat(ns)
        tks = []
        for s in range(ns):
            dx = deps_x[s] if (isinstance(deps_x, list) and len(deps_x) == ns and st.get("per_s")) else deps_x
            if s % 2 == 0:
                tks.append(ACT.add(lambda e, s=s: e.activation(out=xn[:, s, :], in_=xt[:, s, :], func=AF.Square,
                                                              accum_out=ss[:, s:s + 1]), [dx, junk_deps]))
            else:
                tks.append(DVE.add(lambda e, s=s: e.scalar_tensor_tensor(
                    out=xn[:, s, :], in0=xt[:, s, :], scalar=1.0, in1=xt[:, s, :], op0=ALU.mult, op1=ALU.mult,
                    accum_out=ss[:, s:s + 1]), [dx, junk_deps]))
        t1 = DVE.add(lambda e: e.tensor_scalar(out=ss, in0=ss, scalar1=1.0 / D, scalar2=1e-6,
                                               op0=ALU.mult, op1=ALU.add), tks)
        t2 = ACT.add(lambda e: e.activation(out=ss, in_=ss, func=AF.Ln), [t1])
        t3 = ACT.add(lambda e: e.activation(out=rs, in_=ss, func=AF.Exp, scale=-0.5), [t2])
        ck()
        return rs, t3

    def rmsnorm_hT(ns, gcol, deps_x, deps_hT_free):
        rs, t3 = rms_rstd(ns, deps_x, G("xn_free"))
        xtk = []
        for s in range(ns):
            xtk.append(ev_scale(evac_eng(), xn[:, s, :], xt[:, s, :], rs[:, s:s + 1], [t3]))
        ck()
        out = []
        tp = None
        for kc in range(8):
            i, view, tp = tr_group([xn[:, s, kc * 128:(kc + 1) * 128] for s in range(ns)], [xtk])
            tk = ev_scale(evac_eng(), hT[:, kc, 0:ns * 128], view[:, 0:ns * 128], gcol[:, kc:kc + 1],
                          [tp, t_const, deps_hT_free])
            slot_free[i] = tk
            out.append(tk)
            ck()
        state["xn_free"] = tp
        ck()
        return out

    def mm_group(lhs_fn, rhs_fn, nk, ncol, deps, m=128, kdeps=None):
        i = get_slot()
        tk = None
        for k in range(nk):
            dk = [deps, slot_free[i]] if k == 0 else []
            if kdeps is not None:
                dk = dk + [kdeps[k]]
            tk = PE.add(lambda e, k=k, i=i: e.matmul(slot_aps[i][0:m, 0:ncol], lhsT=lhs_fn(k), rhs=rhs_fn(k),
                                                     start=(k == 0), stop=(k == nk - 1)),
                        dk, sig=(k == nk - 1))
        return i, tk

    ks_ds = [dsem(), dsem()]
    ks_free = [None, None]
    ks_i = [0]
    kst_ds = [dsem(), dsem()]
    xl_ds = [dsem() for _ in range(4)]
    xs_ds = [dsem() for _ in range(4)]
    ptfree = [None, None, None]
    accfree = [None] * 8
    misc_ds = dsem()
    mem_ds = dsem()
    ps_ds = [dsem() for _ in range(NBS)]
    kc_ds = dsem()
    xt_store = [None] * 4

    def mem_kv_prompt(l):
        d_ = dsem()
        tl = SP.dma(d_, xt[:, 0:2, :], mem_prompt.rearrange("(s p) c -> p s c", p=128), deps=[G("xt_free")])
        hts = rmsnorm_hT(2, gcols[:, l, 2, :], [tl], G("hT_free"))
        last_pe = None
        for which in range(2):
            for h in range(2):
                b_, tsl = load_slab(l, 8 + which * 2 + h)
                slab = slabs[b_]
                ck()
                for s in range(2):
                    i, tk = mm_group(lambda k, s=s: hT[:, k, s * 128:(s + 1) * 128],
                                     lambda k, slab=slab: slab[:, k, :], 8, 512, [tsl], kdeps=hts)
                    si = get_stage()
                    te = ev_copy(evac_eng(), stage[si][:], slot_aps[i][:, :], [tk, stage_free[si]])
                    dst = (mem_k_prompt if which == 0 else mem_v_prompt)[l, s * 128:(s + 1) * 128, h * 512:(h + 1) * 512]
                    store_stage(si, dst, stage[si][:], te)
                    ck()
                    if which == 1:
                        te2 = ev_copy(DVE, mv[:, s, h * 512:(h + 1) * 512], slot_aps[i][:, :], [tk, te])
                        slot_free[i] = [te, te2]
                    else:
                        slot_free[i] = te
                    last_pe = tk
                    ck()
                if which == 0:
                    for cc in range(4):
                        i, tk = mm_group(lambda k, cc=cc, slab=slab: slab[:, k, cc * 128:(cc + 1) * 128],
                                         lambda k: hT[:, k, 0:256], 8, 256, [tsl], kdeps=hts)
                        te = ev_copy(evac_eng(), mkT[:, h * 4 + cc, :], slot_aps[i][:, 0:256], [tk])
                        slot_free[i] = te
                        last_pe = tk
                        ck()
                slab_free[b_] = last_pe
        state["hT_free"] = last_pe
        state["xt_free"] = last_pe
        return last_pe

    def cross_attn(l, c0, ncol, dep_in):
        last = None
        for hx in range(4):
            pb = hx % 2
            ptok = []
            for mb in range(2):
                i = get_slot()
                tk = None
                for half in range(2):
                    tk = PE.add(lambda e, i=i, half=half, mb=mb, hx=hx: e.matmul(
                        slot_aps[i][:, 0:ncol], lhsT=mkT[:, hx * 2 + half, mb * 128:(mb + 1) * 128],
                        rhs=qxT[:, hx * 2 + half, c0:c0 + ncol], start=(half == 0), stop=(half == 1)),
                        [dep_in, slot_free[i]] if half == 0 else (), sig=(half == 1))
                te = ACT.add(lambda e, i=i, mb=mb, pb=pb: e.activation(
                    out=PxT[pb][:, mb, 0:ncol], in_=slot_aps[i][:, 0:ncol], func=AF.Exp, scale=1.0 / 16.0),
                    [tk, G("px_free%d" % pb)])
                slot_free[i] = te
                ptok.append(te)
            i = get_slot()
            tk = None
            for mb in range(2):
                tk = PE.add(lambda e, i=i, mb=mb, pb=pb: e.matmul(
                    slot_aps[i][:, 0:ncol], lhsT=ones_bf[:, :], rhs=PxT[pb][:, mb, 0:ncol],
                    start=(mb == 0), stop=(mb == 1)),
                    [ptok, slot_free[i], t_ones] if mb == 0 else (), sig=(mb == 1))
            tr = DVE.add(lambda e, i=i, pb=pb: e.reciprocal(out=rrec[pb][:, 0:ncol], in_=slot_aps[i][:, 0:ncol]),
                         [tk, G("rr_free%d" % pb)])
            slot_free[i] = tr
            tms = []
            for half in range(2):
                i = get_slot()
                tk = None
                for mb in range(2):
                    tk = PE.add(lambda e, i=i, mb=mb, pb=pb, hx=hx, half=half: e.matmul(
                        slot_aps[i][:, 0:ncol], lhsT=mv[:, mb, hx * 256 + half * 128: hx * 256 + (half + 1) * 128],
                        rhs=PxT[pb][:, mb, 0:ncol], start=(mb == 0), stop=(mb == 1)),
                        [ptok, slot_free[i]] if mb == 0 else (), sig=(mb == 1))
                tm = DVE.add(lambda e, i=i, pb=pb, hx=hx, half=half: e.tensor_tensor(
                    out=oxT[:, hx * 2 + half, c0:c0 + ncol], in0=slot_aps[i][:, 0:ncol],
                    in1=rrec[pb][:, 0:ncol], op=ALU.mult), [tk, tr, G("ox_free")])
                slot_free[i] = tm
                tms.append(tm)
                last = tk
            state["px_free%d" % pb] = last
            state["rr_free%d" % pb] = tms
            state["ox_toks"] = state.get("ox_toks", []) + tms
        return last

    def subln(l, j, tev, r0, rows, items):
        ob = o1 if j == 0 else o1b
        n = len(items)
        ssq = stat(n)
        R = slice(r0, r0 + rows)
        tks = []
        for k, (a_, h, sub) in enumerate(items):
            tks.append(DVE.add(lambda e, a_=a_, k=k: e.scalar_tensor_tensor(
                out=osq[R, :], in0=ob[R, a_, :], scalar=1.0, in1=ob[R, a_, :], op0=ALU.mult, op1=ALU.mult,
                accum_out=ssq[R, k:k + 1]), [tev]))
        t1 = DVE.add(lambda e: e.tensor_scalar(out=ssq[R, :], in0=ssq[R, :], scalar1=1.0 / 128, scalar2=1e-5,
                                               op0=ALU.mult, op1=ALU.add), tks)
        t2 = ACT.add(lambda e: e.activation(out=ssq[R, :], in_=ssq[R, :], func=AF.Ln), [t1])
        t3 = ACT.add(lambda e: e.activation(out=ssq[R, :], in_=ssq[R, :], func=AF.Exp, scale=-0.5), [t2])
        outs = []
        for k, (a_, h, sub) in enumerate(items):
            outs.append(DVE.add(lambda e, a_=a_, h=h, sub=sub, k=k: e.tensor_scalar(
                out=a_tok[R, sub, h * 128:(h + 1) * 128], in0=ob[R, a_, :],
                scalar1=ssq[R, k:k + 1], scalar2=None, op0=ALU.mult),
                [t3, lam_tok[l], G("aT_free")]))
        state["o1_free"] = outs
        return outs

    def evac_pass(l, m, j, tlast, r0, rows, accs):
        ob = o1 if j == 0 else o1b
        rc = stat(8)
        R = slice(r0, r0 + rows)
        tev = []
        for a_ in accs:
            tr = DVE.add(lambda e, a_=a_: e.reciprocal(out=rc[R, a_:a_ + 1],
                                                       in_=acc_ap(a_, r0, rows)[:, 128:129]), [tlast])
            if m == 0:
                te = DVE.add(lambda e, a_=a_: e.tensor_scalar(
                    out=ob[R, a_, :], in0=acc_ap(a_, r0, rows)[:, 0:128], scalar1=rc[R, a_:a_ + 1],
                    scalar2=None, op0=ALU.mult), [tr, G("o1_free")])
            else:
                tn = DVE.add(lambda e, a_=a_: e.tensor_scalar(out=rc[R, a_:a_ + 1], in0=rc[R, a_:a_ + 1],
                                                              scalar1=lam_t[R, l, 5:6], scalar2=None, op0=ALU.mult),
                             [tr, lam_tok[l]])
                te = DVE.add(lambda e, a_=a_: e.scalar_tensor_tensor(
                    out=ob[R, a_, :], in0=acc_ap(a_, r0, rows)[:, 0:128], scalar=rc[R, a_:a_ + 1],
                    in1=ob[R, a_, :], op0=ALU.mult, op1=ALU.add), [tn, state["o1_tok"][j]])
            tev.append(te)
        for a_ in range(8):
            accfree[a_] = tev
        return tev

    def subln_fast(l, j, tev):
        ob = o1 if j == 0 else o1b
        ssq = stat(8)
        t0 = DVE.add(lambda e: e.tensor_tensor(out=otmp[:, :, :], in0=ob[:, :, :], in1=ob[:, :, :], op=ALU.mult),
                     [tev, G("otmp_free")])
        t0b = DVE.add(lambda e: e.tensor_reduce(out=ssq, in_=otmp[:, :, :], axis=mybir.AxisListType.X, op=ALU.add), [t0])
        t1 = DVE.add(lambda e: e.tensor_scalar(out=ssq, in0=ssq, scalar1=1.0 / 128, scalar2=1e-5,
                                               op0=ALU.mult, op1=ALU.add), [t0b])
        t2 = ACT.add(lambda e: e.activation(out=ssq, in_=ssq, func=AF.Ln), [t1])
        t3 = ACT.add(lambda e: e.activation(out=ssq, in_=ssq, func=AF.Exp, scale=-0.5), [t2])
        dst = a_tok[:, :, 2 * j * 128:(2 * j + 2) * 128].rearrange("p q (h d) -> p h q d", h=2)
        t5 = DVE.add(lambda e: e.tensor_tensor(
            out=dst, in0=ob[:, :, :].rearrange("p (h q) d -> p h q d", h=2),
            in1=ssq.rearrange("p (h q) -> p h q", h=2).unsqueeze(3).to_broadcast([128, 2, 4, 128]), op=ALU.mult),
            [t3, lam_tok[l], G("aT_free")])
        state["o1_free"] = [t5]
        state["otmp_free"] = t5
        return [t5]

    def evac_pass_fast(l, m, j, tlast, tpool_done):
        ob = o1 if j == 0 else o1b
        rc = stat(8)
        gate = []
        tmul = []
        for b_ in range(3):
            na = 3 if b_ < 2 else 2
            bank = psO[:, b_, 0:480].rearrange("p (a c) -> p a c", c=160)
            rcb = rc[:, 3 * b_:3 * b_ + na]
            tr = DVE.add(lambda e, bank=bank, rcb=rcb, na=na: e.reciprocal(out=rcb, in_=bank[:, 0:na, 128]), [tlast])
            dst = (ob if m == 0 else otmp)[:, 3 * b_:3 * b_ + na, :]
            te = DVE.add(lambda e, bank=bank, rcb=rcb, na=na, dst=dst: e.tensor_tensor(
                out=dst, in0=bank[:, 0:na, 0:128], in1=rcb.unsqueeze(2).to_broadcast([128, na, 128]),
                op=ALU.mult), [tr, G("o1_free") if m == 0 else tpool_done, G("otmp_free")])
            gate.append(te)
            tmul.append(te)
        for a_ in range(8):
            accfree[a_] = gate[a_ // 3]
        if m == 0:
            return tmul
        tadd = DVE.add(lambda e: e.scalar_tensor_tensor(out=ob[:, :, :], in0=otmp[:, :, :], scalar=lam_t[:, l, 5:6],
                                                         in1=ob[:, :, :], op0=ALU.mult, op1=ALU.add),
                       [tmul, state["o1_tok"][j], lam_tok[l]])
        state["otmp_free"] = tadd
        return [tadd]

    def attn_prompt(l, t, tq, tkT, tv, tpool_done):
        nprev = 4 * t
        rf_attn = G("region_free")
        deferred = []
        ta_all = []
        tlast = None
        for c in range(4):
            m, j = divmod(c, 2)
            chunks = [(i * 16, min((i + 1) * 16, nprev)) for i in range((nprev + 15) // 16)]
            cbuf = []

            def issue_chunk(ci):
                b0, b1 = chunks[ci]
                bi = ks_i[0]
                ks_i[0] ^= 1
                tkl = SP.dma(ks_ds[bi], kstream[bi][:, 0:(b1 - b0) * 128], ktscr[l, c, :, b0 * 128:b1 * 128],
                             deps=[ks_free[bi], state["kst"][t - 1], rf_attn])
                cbuf.append((bi, tkl))

            if chunks:
                issue_chunk(0)
            nkb = nprev + 4
            pend = {}

            def emit_qk(kb):
                dg = kb - nprev
                q0 = max(0, dg) * 128
                sp_i = kb % 2
                spair = psA if sp_i == 0 else psB
                pti = kb % 3
                tk = None
                if kb < nprev and kb % 16 == 0 and kb // 16 + 1 < len(chunks):
                    issue_chunk(kb // 16 + 1)
                for hl in range(2):
                    if kb < nprev:
                        bi, tkl = cbuf[kb // 16]
                        lo = (kb % 16) * 128
                        ksrc = kstream[bi][hl * 64:(hl + 1) * 64, lo:lo + 128]
                        kdep = tkl
                    else:
                        ksrc = kTcur[hl * 64:(hl + 1) * 64, c, dg * 128:(dg + 1) * 128]
                        kdep = tkT[c]
                    qsrc = qT[hl * 64:(hl + 1) * 64, c, q0:512]
                    tk = PE.add(lambda e, hl=hl, ksrc=ksrc, q0=q0, spair=spair, qsrc=qsrc: e.matmul(
                        spair[:, hl, q0:512], lhsT=ksrc, rhs=qsrc,
                        start=True, stop=True),
                        [kdep, tq[c], slot_free[sp_i * 2], slot_free[sp_i * 2 + 1]] if hl == 0 else (),
                        sig=(hl == 1))
                if kb < nprev and (kb % 16 == 15 or kb == nprev - 1):
                    ks_free[cbuf[kb // 16][0]] = tk
                te = ACT.add(lambda e, spair=spair, pti=pti, q0=q0: e.activation(
                    out=PT[pti][:, :, q0:512], in_=spair[:, :, q0:512], func=AF.Exp, scale=0.125),
                    [tk, ptfree[pti]])
                slot_free[sp_i * 2] = te
                slot_free[sp_i * 2 + 1] = te
                if dg >= 0:
                    te = POOL.add(lambda e, pti=pti, q0=q0: e.memset(PT[pti][64:128, :, q0:q0 + 64], 0.0), [te])
                pend[kb] = te

            def emit_pv(kb):
                dg = kb - nprev
                q0 = max(0, dg) * 128
                pti = kb % 3
                te = pend.pop(kb)
                tk2 = None
                for hl in range(2):
                    h = 2 * j + hl
                    for qs in range(q0 // 128, 4):
                        a_ = hl * 4 + qs
                        first = (kb == 0)
                        lastq = (kb == nprev + qs)
                        dps = [te, tv] if (hl == 0 and qs == q0 // 128) else []
                        if first:
                            dps = dps + [accfree[a_]]
                        stf = first and (a_ % 3 == 0)
                        tk2 = PE.add(lambda e, a_=a_, hl=hl, qs=qs, h=h, stf=stf, lastq=lastq: e.matmul(
                            acc_ap(a_), lhsT=PT[pti][:, hl, qs * 128:(qs + 1) * 128], rhs=Vres[:, kb, h, :],
                            start=stf, stop=lastq, skip_group_check=True), dps, sig=(hl == 1 and qs == 3))
                ptfree[pti] = tk2
                return tk2

            tk2 = None
            for step in range(nkb + 1):
                if step < nkb:
                    emit_qk(step)
                if step >= 1:
                    tk2 = emit_pv(step - 1)
                if step == 3 and deferred:
                    ta_all.append(deferred.pop()())
            tlast = tk2
            tev = evac_pass_fast(l, m, j, tlast, tpool_done)
            if m == 0:
                state["o1_tok"][j] = tev
            elif c == 2:
                deferred.append(lambda j=j, tev=tev: subln_fast(l, j, tev))
            else:
                ta_all.append(subln_fast(l, j, tev))
        state["vres_free"] = tlast
        return ta_all

    def issue_k_load(l, bq, deps):
        return POOL.dma(kc_ds, kc_tok[:, :, :], cache_k[l, bq].rearrange("(k p) c -> p k c", p=128), deps=deps)

    def issue_v_load(l, bq, deps):
        t2 = None
        for hh in range(4):
            t2 = POOL.dma(misc_ds, V_s[:, 0:NCB, hh, 0:128],
                          cache_v[l, bq][:, hh, :].rearrange("(k p) d -> p k d", p=128), deps=deps)
        t3 = POOL.add(lambda e: e.memset(V_s[:, :, :, 128:129], 1.0), deps)
        return [t2, t3]

    def attn_sample(l, tq, tkT, tv):
        ta_all = []
        pre = state.pop("samp_pre")
        tk_load, tv_load = pre
        for bq in range(NBS):
            r0 = (bq % 2) * 64
            prev = [G("samp_free"), G("vres_free")]
            tc = [tk_load, tv_load]
            tkt = []
            for c in range(4):
                for k0 in range(0, NCB, 4):
                    nn = min(4, NCB - k0)
                    i, view, tp = tr_group([kc_tok[:, k0 + kk, c * 128:(c + 1) * 128] for kk in range(nn)], [tk_load])
                    te = ev_copy(DVE, kT_s[:, c, k0 * 128:(k0 + nn) * 128], view[:, 0:nn * 128], [tp, prev])
                    slot_free[i] = te
                    tkt.append(te)
            tvn = DVE.add(lambda e, bq=bq: e.tensor_copy(out=V_s[0:64, NCB, :, 0:128], in_=vnew[bq][0:64, :, 0:128]),
                          [tv, tv_load])
            if bq + 1 < NBS:
                tk_load = issue_k_load(l, bq + 1, [tp])
            lastpe = None
            for c in range(4):
                m, j = divmod(c, 2)
                pend = {}
                groups = [list(range(g0, min(g0 + 8, NCB))) for g0 in range(0, NCB, 8)] + [[NCB]]

                def s_qk(gi):
                    kbs = groups[gi]
                    nk = 128 if kbs[0] < NCB else 64
                    sp_i = gi % 2
                    spair = psA if sp_i == 0 else psB
                    pti = gi % 3
                    tk = None
                    n = len(kbs)
                    for ki, kb in enumerate(kbs):
                        for hl in range(2):
                            if kb < NCB:
                                ksrc = kT_s[hl * 64:(hl + 1) * 64, c, kb * 128:(kb + 1) * 128]
                            else:
                                ksrc = kTcur[hl * 64:(hl + 1) * 64, c, bq * 64:(bq + 1) * 64]
                            qsrc = qT[hl * 64:(hl + 1) * 64, c, bq * 64:(bq + 1) * 64]
                            osl = spair[0:nk, hl, ki * 64:(ki + 1) * 64]
                            first = (ki == 0 and hl == 0)
                            tk = PE.add(lambda e, ksrc=ksrc, qsrc=qsrc, osl=osl: e.matmul(
                                osl, lhsT=ksrc, rhs=qsrc, start=True, stop=True, skip_group_check=True),
                                [tkt, tkT[c], tq[c], slot_free[sp_i * 2], slot_free[sp_i * 2 + 1]] if first else (),
                                sig=(ki == n - 1 and hl == 1))
                    src_ap = spair[0:nk, :, 0:n * 64]
                    dst_ap = PT[pti][0:nk, :, 0:n * 64]
                    te = ACT.add(lambda e, src_ap=src_ap, dst_ap=dst_ap: e.activation(
                        out=dst_ap, in_=src_ap, func=AF.Exp, scale=0.125), [tk, ptfree[pti]])
                    slot_free[sp_i * 2] = te
                    slot_free[sp_i * 2 + 1] = te
                    pend[gi] = te

                def s_pv(gi):
                    kbs = groups[gi]
                    nk = 128 if kbs[0] < NCB else 64
                    pti = gi % 3
                    te = pend.pop(gi)
                    tk2 = None
                    n = len(kbs)
                    for ki, kb in enumerate(kbs):
                        for hl in range(2):
                            h = 2 * j + hl
                            a_ = hl * 3
                            first = (kb == 0)
                            dps = [te, tvn] if (ki == 0 and hl == 0) else []
                            if first:
                                dps = dps + [accfree[a_]]
                            oacc = acc_ap(a_, r0, 64)
                            lsrc = PT[pti][0:nk, hl, ki * 64:(ki + 1) * 64]
                            rsrc = V_s[0:nk, kb, h, :]
                            tk2 = PE.add(lambda e, oacc=oacc, lsrc=lsrc, rsrc=rsrc, first=first, kb=kb: e.matmul(
                                oacc, lhsT=lsrc, rhs=rsrc, start=first, stop=(kb == NCB), skip_group_check=True),
                                dps, sig=(ki == n - 1 and hl == 1))
                    ptfree[pti] = tk2
                    return tk2

                tk2 = None
                for step in range(len(groups) + 1):
                    if step < len(groups):
                        s_qk(step)
                    if step >= 1:
                        tk2 = s_pv(step - 1)
                lastpe = tk2
                tev = evac_pass(l, m, j, lastpe, r0, 64, [0, 3])
                if m == 0:
                    state["o1_tok"][j] = tev
                else:
                    ta_all.append(subln(l, j, tev, r0, 64, [(hl * 3, 2 * j + hl, bq // 2) for hl in range(2)]))
            state["samp_free"] = [lastpe, ta_all[-2:]]
            if bq + 1 < NBS:
                tv_load = issue_v_load(l, bq + 1, [lastpe])
        return ta_all

    def load_mem_sample(l, bq):
        prev = [G("mem_free")]
        t1 = POOL.dma(mem_ds, mktok[:, :, :], cache_mem_k[l, bq].rearrange("(s p) c -> p s c", p=128),
                      deps=[prev, G("xn_free")])
        t2 = POOL.dma(mem_ds, mv[:, :, :], cache_mem_v[l, bq].rearrange("(s p) c -> p s c", p=128), deps=[prev])
        toks = []
        tp = None
        for cc in range(8):
            i, view, tp = tr_group([mktok[:, s, cc * 128:(cc + 1) * 128] for s in range(2)], [t2, prev])
            te = ev_copy(evac_eng(), mkT[:, cc, :], view[:, 0:256], [tp])
            slot_free[i] = te
            toks.append(te)
        state["xn_free"] = tp
        return toks + [t2]

    def load_pool_state(l):
        uview = uext[:, :, 0:4 * 80].rearrange("p g (b c) -> p g b c", b=4)
        wdeps = [G("uext_free"), G("region_free")]
        toks = []
        for bq in range(NBS):
            si = get_stage()
            t1 = POOL.dma(ps_ds[bq], stage[si][0:15, :], state_pool[l, bq], deps=[stage_free[si]])
            tp = None
            for g in range(4):
                i, view, tp = tr_group([stage[si][0:15, g * 128:(g + 1) * 128]], [t1], f32=True)
                te = DVE.add(lambda e, g=g, bq=bq, view=view: e.tensor_copy(out=uview[:, g, bq, 1:16], in_=view[:, 0:15]),
                             [tp, wdeps])
                slot_free[i] = te
                toks.append(te)
            stage_free[si] = tp
        tz = POOL.add(lambda e: e.memset(uview[:, :, :, 0:1], 0.0), wdeps)
        return toks + [tz]

    def run_tile(l, t):
        is_s = (t == NTP)
        ns = 2 if is_s else 4
        ntok = ns * 128
        row0 = S if is_s else t * 512
        orow = 0 if is_s else row0
        groups = [(b_ * 64, 64) for b_ in range(4)] if is_s else [(s * 128, 128) for s in range(4)]
        lb = 64 if is_s else 512
        nb = 4 if is_s else 1
        uview = uext[:, :, 0:nb * (16 + lb)].rearrange("p g (b c) -> p g b c", b=nb)
        if l == 0:
            src_x = (x_sample if is_s else x_prompt[row0:row0 + ntok, :])
        else:
            src_x = xscr[row0:row0 + ntok, :]
        xf = G("xt_free")
        tl = []
        for s in range(ns):
            dps = [xf[s] if isinstance(xf, list) and s < len(xf) else xf, G("xs_all") if l > 0 else None]
            tl.append(SP.dma(xl_ds[s], xt[:, s, :], src_x[s * 128:(s + 1) * 128, :], deps=dps))
        st["per_s"] = True
        hts = rmsnorm_hT(ns, gcols[:, l, 0, :], tl, [G("hT_free")])
        st["per_s"] = False
        rf = G("region_free")
        if is_s:
            d0 = [G("samp_free"), G("vres_free")]
            state["samp_pre"] = (issue_k_load(l, 0, d0), issue_v_load(l, 0, d0))
        if not is_s:
            state["hist_tok"] = POOL.add(lambda e: e.tensor_copy(out=uext[:, :, 0:16], in_=hist_keep[:, :, :]),
                                         [rf, G("hk_tok"), G("uext_free")])
        b_, tsl = load_slab(l, 0)
        slab = slabs[b_]
        tq = []
        lastpe = None
        for cc in range(4):
            i, tk = mm_group(lambda k, cc=cc, slab=slab: slab[:, k, cc * 128:(cc + 1) * 128],
                             lambda k: hT[:, k, 0:ntok], 8, ntok, [tsl], kdeps=hts)
            te = ev_copy(evac_eng(), qT[:, cc, 0:ntok], slot_aps[i][:, 0:ntok], [tk, rf])
            slot_free[i] = te
            tq.append(te)
            lastpe = tk
        slab_free[b_] = lastpe
        b_, tsl = load_slab(l, 1)
        slab = slabs[b_]
        tkT = []
        for cc in range(4):
            i, tk = mm_group(lambda k, cc=cc, slab=slab: slab[:, k, cc * 128:(cc + 1) * 128],
                             lambda k: hT[:, k, 0:ntok], 8, ntok, [tsl], kdeps=hts)
            te = ev_copy(evac_eng(), kTcur[:, cc, 0:ntok], slot_aps[i][:, 0:ntok], [tk, rf, G("kst_last")])
            slot_free[i] = te
            tkT.append(te)
        if not is_s:
            tks_ = POOL.dma(kst_ds[t % 2], ktscr[l, :, :, t * 512:(t + 1) * 512].rearrange("c p k -> p c k"),
                            kTcur[:, :, :], deps=[tkT])
            state["kst"][t] = tks_
            state["kst_last"] = tks_
            out_toks.append(tks_)
        for gi, (g0, gn) in enumerate(groups):
            i, tk = mm_group(lambda k, g0=g0, gn=gn: hT[:, k, g0:g0 + gn],
                             lambda k, slab=slab: slab[:, k, :], 8, 512, [tsl], m=gn, kdeps=hts)
            si = get_stage()
            te = ev_copy(evac_eng(), stage[si][0:gn, :], slot_aps[i][0:gn, :], [tk, stage_free[si]])
            slot_free[i] = te
            dst = (k_sample if is_s else k_prompt)[l, orow + g0:orow + g0 + gn, :]
            store_stage(si, dst, stage[si][0:gn, :], te)
            lastpe = tk
        slab_free[b_] = lastpe
        b_, tsl = load_slab(l, 2)
        slab = slabs[b_]
        tv = []
        for gi, (g0, gn) in enumerate(groups):
            i, tk = mm_group(lambda k, g0=g0, gn=gn: hT[:, k, g0:g0 + gn],
                             lambda k, slab=slab: slab[:, k, :], 8, 512, [tsl], m=gn, kdeps=hts)
            si = get_stage()
            te = ev_copy(ACT, stage[si][0:gn, :], slot_aps[i][0:gn, :], [tk, stage_free[si]])
            dst = (v_sample if is_s else v_prompt)[l, orow + g0:orow + g0 + gn, :]
            store_stage(si, dst, stage[si][0:gn, :], te)
            vdst = vnew[gi][0:gn, :, 0:128] if is_s else Vres[:, t * 4 + gi, :, 0:128]
            te2 = DVE.add(lambda e, i=i, gn=gn, vdst=vdst: e.tensor_copy(
                out=vdst, in_=slot_aps[i][0:gn, :].rearrange("p (h d) -> p h d", h=4)),
                [tk, te, G("vres_wr"), rf])
            slot_free[i] = [te, te2]
            tv.append(te2)
            lastpe = tk
        slab_free[b_] = lastpe
        b_, tsl = load_slab(l, 3)
        slab = slabs[b_]
        tu = []
        for g in range(4):
            i, tk = mm_group(lambda k, g=g, slab=slab: slab[:, k, g * 128:(g + 1) * 128],
                             lambda k: hT[:, k, 0:ntok], 8, ntok, [tsl], kdeps=hts)
            te = ev_copy(evac_eng(), uview[:, g, :, 16:16 + lb],
                         slot_aps[i][:, 0:ntok].rearrange("p (b c) -> p b c", b=nb),
                         [tk, G("uext_free"), G("hist_tok"), rf])
            slot_free[i] = te
            tu.append(te)
            lastpe = tk
        if is_s or t == NTP - 1:
            for gi, (g0, gn) in enumerate(groups):
                if (not is_s) and gi != 3:
                    continue
                i, tk = mm_group(lambda k, g0=g0, gn=gn: hT[:, k, g0:g0 + gn],
                                 lambda k, slab=slab: slab[:, k, :], 8, 512, [tsl], m=gn, kdeps=hts)
                si = get_stage()
                te = ev_copy(evac_eng(), stage[si][0:gn, :], slot_aps[i][0:gn, :], [tk, stage_free[si]])
                slot_free[i] = te
                dst = pool_sample[l, gi] if is_s else pool_prompt[l]
                store_stage(si, dst, stage[si][gn - 15:gn, :], te)
                lastpe = tk
        slab_free[b_] = lastpe
        state["hT_free"] = lastpe
        first_stream = (not is_s) and t == 0
        tpool = []
        tpy = []
        for g in range(4):
            w = 2 << g
            cur = uview[:, g, :, :]
            lo = 1
            tk = [tu[g]]
            for lev in range(g + 1):
                sh = 1 << lev
                nlo = lo + sh
                dstb = utmp[lev % 2][:, 0:nb * (16 + lb)].rearrange("p (b c) -> p b c", b=nb)
                tk = POOL.add(lambda e, dstb=dstb, cur=cur, nlo=nlo, sh=sh: e.tensor_tensor(
                    out=dstb[:, :, nlo:16 + lb], in0=cur[:, :, nlo:16 + lb],
                    in1=cur[:, :, nlo - sh:16 + lb - sh], op=ALU.add), [tk, G("utmp_free")])
                cur = dstb
                lo = nlo
            pb2 = pooled2[g]
            pvw = pb2[:, 0:ntok].rearrange("p (b c) -> p b c", b=nb)
            tk2 = DVE.add(lambda e, pvw=pvw, cur=cur, g=g, w=w: e.scalar_tensor_tensor(
                out=pvw, in0=uview[:, g, :, 16:16 + lb], scalar=-float(w), in1=cur[:, :, 16:16 + lb],
                op0=ALU.mult, op1=ALU.add), [tk, G("pooled_free")])
            if first_stream:
                tk3 = POOL.add(lambda e, cur=cur, g=g: e.tensor_tensor(
                    out=cur[:, 0, 16:32], in0=cur[:, 0, 16:32], in1=icnt[:, g, :], op=ALU.mult), [tk2, ic_toks])
                tk2 = DVE.add(lambda e, cur=cur, g=g, pb2=pb2, w=w: e.scalar_tensor_tensor(
                    out=pb2[:, 0:16], in0=uext[:, g, 16:32], scalar=-float(w), in1=cur[:, 0, 16:32],
                    op0=ALU.mult, op1=ALU.add), [tk3])
            state["utmp_free"] = tk2
            tpool.append(tk2)
        if not is_s:
            state["hk_tok"] = POOL.add(lambda e: e.tensor_copy(out=hist_keep[:, :, :], in_=uext[:, :, 512:528]), [tpool])
            state["uext_free"] = [state["hk_tok"]] + tpool
        else:
            state["uext_free"] = tpool
        if is_s:
            ta = attn_sample(l, tq, tkT, tv)
        else:
            ta = attn_prompt(l, t, tq, tkT, tv, tpool)
        for g in range(4):
            i = get_slot()
            tk = PE.add(lambda e, i=i, g=g: e.matmul(slot_aps[i][:, 0:ntok], lhsT=wpool_bf[:, l, g, :],
                                                     rhs=pooled2[g][:, 0:ntok], start=True, stop=True),
                        [tpool[g], slot_free[i], t_wp])
            state["pooled_free"] = tk
            te = ev_scale(evac_eng(), pyT[:, g, 0:ntok], slot_aps[i][:, 0:ntok], pscol[:, l, g:g + 1],
                          [tk, t_const, G("pyT_free")])
            slot_free[i] = te
            tpy.append(te)
        taT = []
        tp = None
        for cc in range(4):
            i, view, tp = tr_group([a_tok[:, s, cc * 128:(cc + 1) * 128] for s in range(ns)], [ta])
            tk = ev_scale(evac_eng(), aT[:, cc, 0:ntok], view[:, 0:ntok], sgcol[:, l:l + 1], [tp, G("aT_free"), lam_tok[l]])
            slot_free[i] = tk
            taT.append(tk)
        txo = []
        for h in range(2):
            b_, tsl = load_slab(l, 4 + h)
            slab = slabs[b_]
            for s in range(ns):
                i, tk = mm_group(lambda k, s=s: (aT if k < 4 else pyT)[:, k % 4, s * 128:(s + 1) * 128],
                                 lambda k, slab=slab: slab[:, k, :], 8, 512, [taT, tpy, tsl])
                te = DVE.add(lambda e, i=i, s=s, h=h: e.tensor_tensor(
                    out=xt[:, s, h * 512:(h + 1) * 512], in0=slot_aps[i][:, :],
                    in1=xt[:, s, h * 512:(h + 1) * 512], op=ALU.add), [tk])
                slot_free[i] = te
                txo.append((s, te))
                lastpe = tk
            slab_free[b_] = lastpe
        state["aT_free"] = lastpe
        state["pyT_free"] = lastpe
        regA_done = [lastpe, ta, tpool, tp, G("kst_last")]
        st["per_s"] = True
        hts = rmsnorm_hT(ns, gcols[:, l, 1, :], [[te for (s2, te) in txo if s2 == s] for s in range(ns)], [G("hT_free")])
        st["per_s"] = False
        tqx = []
        for h in range(2):
            b_, tsl = load_slab(l, 6 + h)
            slab = slabs[b_]
            for cc in range(4):
                i, tk = mm_group(lambda k, cc=cc, slab=slab: slab[:, k, cc * 128:(cc + 1) * 128],
                                 lambda k: hT[:, k, 0:ntok], 8, ntok, [tsl], kdeps=hts)
                te = ev_copy(evac_eng(), qxT[:, h * 4 + cc, 0:ntok], slot_aps[i][:, 0:ntok], [tk, regA_done])
                slot_free[i] = te
                tqx.append(te)
                lastpe = tk
            slab_free[b_] = lastpe
        state["hT_free"] = lastpe
        state["ox_toks"] = []
        state["ox_free"] = regA_done
        if is_s:
            for bq in range(NBS):
                tmk = load_mem_sample(l, bq)
                lastpe = cross_attn(l, bq * 64, 64, [tqx, tmk])
                state["mem_free"] = lastpe
        else:
            lastpe = cross_attn(l, 0, 512, [tqx, G("memkv_tok")])
            state["mem_free"] = lastpe
        tox = state["ox_toks"]
        txo = []
        for h in range(2):
            b_, tsl = load_slab(l, 12 + h)
            slab = slabs[b_]
            for s in range(ns):
                i, tk = mm_group(lambda k, s=s: oxT[:, k, s * 128:(s + 1) * 128],
                                 lambda k, slab=slab: slab[:, k, :], 8, 512, [tsl, tox])
                te = DVE.add(lambda e, i=i, s=s, h=h: e.tensor_tensor(
                    out=xt[:, s, h * 512:(h + 1) * 512], in0=slot_aps[i][:, :],
                    in1=xt[:, s, h * 512:(h + 1) * 512], op=ALU.add), [tk])
                slot_free[i] = te
                txo.append((s, te))
                lastpe = tk
            slab_free[b_] = lastpe
        st["per_s"] = True
        hts = rmsnorm_hT(ns, gcols[:, l, 3, :], [[te for (s2, te) in txo if s2 == s] for s in range(ns)], [G("hT_free")])
        st["per_s"] = False
        txm = []
        for qd in range(4):
            hb = qd % 2
            thid = []
            for h in range(2):
                b_, tsl = load_slab(l, 14 + qd * 2 + h)
                slab = slabs[b_]
                for cc in range(4):
                    i, tk = mm_group(lambda k, cc=cc, slab=slab: slab[:, k, cc * 128:(cc + 1) * 128],
                                     lambda k: hT[:, k, 0:ntok], 8, ntok, [tsl], kdeps=hts)
                    rb = (h * 4 + cc) % 2
                    te = ACT.add(lambda e, i=i, rb=rb: e.activation(out=rtmp[rb][:, 0:ntok], in_=slot_aps[i][:, 0:ntok],
                                                                  func=AF.Relu), [tk, G("rt_free%d" % rb), regA_done])
                    slot_free[i] = te
                    tsq = POOL.add(lambda e, rb=rb, hb=hb, h=h, cc=cc: e.tensor_tensor(
                        out=hid[hb][:, h * 4 + cc, 0:ntok], in0=rtmp[rb][:, 0:ntok], in1=rtmp[rb][:, 0:ntok],
                        op=ALU.mult), [te, G("hid_free%d" % hb), regA_done])
                    state["rt_free%d" % rb] = tsq
                    thid.append(tsq)
                    lastpe = tk
                slab_free[b_] = lastpe
            for h in range(2):
                b_, tsl = load_slab(l, 22 + qd * 2 + h)
                slab = slabs[b_]
                for s in range(ns):
                    i, tk = mm_group(lambda k, s=s, hb=hb: hid[hb][:, k, s * 128:(s + 1) * 128],
                                     lambda k, slab=slab: slab[:, k, :], 8, 512, [tsl], kdeps=thid)
                    te = DVE.add(lambda e, i=i, s=s, h=h: e.tensor_tensor(
                        out=xt[:, s, h * 512:(h + 1) * 512], in0=slot_aps[i][:, :],
                        in1=xt[:, s, h * 512:(h + 1) * 512], op=ALU.add), [tk])
                    slot_free[i] = te
                    if qd == 3:
                        txm.append((s, te))
                    lastpe = tk
                slab_free[b_] = lastpe
            state["hid_free%d" % hb] = lastpe
        state["hT_free"] = lastpe
        state["region_free"] = [lastpe, tox]
        txs = [[te for (s2, te) in txm if s2 == s] for s in range(ns)]
        stores = []
        if l == DEPTH - 1:
            st["per_s"] = True
            rs, t3 = rms_rstd(ns, txs, G("xn_free"))
            st["per_s"] = False
            ydst = y_sample if is_s else y_prompt[row0:row0 + ntok, :]
            for s in range(ns):
                ty = DVE.add(lambda e, s=s: e.scalar_tensor_tensor(
                    out=xt[:, s, :], in0=xt[:, s, :], scalar=rs[:, s:s + 1], in1=fg_bc[:, :],
                    op0=ALU.mult, op1=ALU.mult), [t3, t_const])
                stores.append(POOL.dma(xs_ds[s], ydst[s * 128:(s + 1) * 128, :], xt[:, s, :], deps=[ty]))
        else:
            for s in range(ns):
                stores.append(POOL.dma(xs_ds[s], xscr[row0 + s * 128:row0 + (s + 1) * 128, :], xt[:, s, :],
                                       deps=[txs[s]]))
        out_toks.extend(stores)
        for s in range(ns):
            xt_store[s] = stores[s]
        state["xt_free"] = list(xt_store)
        state["xs_all"] = list(xt_store)

    def schedule():
        step = 0
        for l in range(DEPTH):
            if upto is not None and step >= upto:
                return
            t_v1 = POOL.add(lambda e: e.memset(Vres[:, :, :, 128:129], 1.0), [G("samp_free")])
            state["vres_wr"] = [t_v1, G("samp_free")]
            state["memkv_tok"] = [mem_kv_prompt(l)]
            state["hk_tok"] = POOL.add(lambda e: e.memset(hist_keep[:, :, :], 0.0), [G("hist_tok")])
            step += 1
            for t in range(NTP):
                if upto is not None and step >= upto:
                    return
                run_tile(l, t)
                step += 1
            if upto is not None and step >= upto:
                return
            state["hist_tok"] = load_pool_state(l)
            run_tile(l, NTP)
            step += 1

    try:
        schedule()
    except _Stop:
        pass
    SP.wait_only(out_toks)

    block = es.enter_context(nc.Block())

    @block.tensor
    def _(e):
        PE.replay(e)

    @block.scalar
    def _(e):
        ACT.replay(e)

    @block.vector
    def _(e):
        DVE.replay(e)

    @block.gpsimd
    def _(e):
        POOL.replay(e)

    @block.sync
    def _(e):
        SP.replay(e)

    es.close()
    return nc


def make_in_maps(inputs, ncores=8, S=8192, PAST=2048):
    f = lambda a: np.ascontiguousarray(np.asarray(a, dtype=np.float32))
    maps = []
    for c in range(ncores):
        sl = slice(c * NBS, (c + 1) * NBS)
        m = {
            "x_prompt": f(inputs["x_prompt"][c]),
            "x_sample": f(inputs["x_sample"][sl]).reshape(NBS * SSEQ, D),
            "cache_k": f(inputs["cache_k"][:, sl]).reshape(DEPTH, NBS, PAST, 512),
            "cache_v": f(inputs["cache_v"][:, sl]),
            "state_pool": f(inputs["state_pool"][:, sl]),
            "cache_mem_k": f(inputs["cache_mem_k"][:, sl]).reshape(DEPTH, NBS, NMEM, D),
            "cache_mem_v": f(inputs["cache_mem_v"][:, sl]).reshape(DEPTH, NBS, NMEM, D),
            "mem_prompt": f(inputs["mem_prompt"][c]),
            "lam_q": f(inputs["lam_q"]).reshape(DEPTH, 128),
            "lam_k": f(inputs["lam_k"]).reshape(DEPTH, 128),
            "final_g": f(inputs["final_g"]).reshape(1, D),
        }
        for k in ["norm_mix_g", "w_in", "subln_g", "w_pool", "pool_scale", "w_out", "norm_x_g", "norm_mem_g",
                  "wq_x", "wk_x", "wv_x", "wo_x", "norm_mlp_g", "w_up", "w_down"]:
            m[k] = f(inputs[k])
        maps.append(m)
    return maps


def assemble(results, ncores=8, S=8192):
    def cat(name, axis, shape_fn):
        return np.concatenate([shape_fn(r[name]) for r in results], axis=axis)
    y_prompt = np.stack([r["y_prompt"] for r in results], 0)
    y_sample = np.concatenate([r["y_sample"].reshape(NBS, SSEQ, D) for r in results], 0)
    k_prompt = np.stack([r["k_prompt"].reshape(DEPTH, S, 2, 4, 64) for r in results], 1)
    v_prompt = np.stack([r["v_prompt"].reshape(DEPTH, S, 4, 128) for r in results], 1)
    pool_prompt = np.stack([r["pool_prompt"] for r in results], 1)
    mem_k = np.stack([r["mem_k_prompt"].reshape(DEPTH, NMEM, 4, 256) for r in results], 1)
    mem_v = np.stack([r["mem_v_prompt"].reshape(DEPTH, NMEM, 4, 256) for r in results], 1)
    k_sample = np.concatenate([r["k_sample"].reshape(DEPTH, NBS, SSEQ, 2, 4, 64) for r in results], 1)
    v_sample = np.concatenate([r["v_sample"].reshape(DEPTH, NBS, SSEQ, 4, 128) for r in results], 1)
    pool_sample = np.concatenate([r["pool_sample"] for r in results], 1)
    outs = (y_prompt, y_sample, k_prompt, v_prompt, pool_prompt, mem_k, mem_v, k_sample, v_sample, pool_sample)
    return tuple(np.ascontiguousarray(o, dtype=np.float32) for o in outs)


def kernel(**inputs):
    ncores = 8
    nc = build_program()
    in_maps = make_in_maps(inputs, ncores)
    res = run_bass_kernel_spmd(nc, in_maps, core_ids=list(range(ncores)))
    return assemble(res.results, ncores)
```

```python
import math
from contextlib import ExitStack

import numpy as np
import concourse.bass as bass
import concourse.mybir as mybir
from concourse.bass_utils import run_bass_kernel_spmd

F32 = mybir.dt.float32
BF16 = mybir.dt.bfloat16
AF = mybir.ActivationFunctionType
ALU = mybir.AluOpType

D = 1024
DEPTH = 2
NBS = 4
SSEQ = 64
NMEM = 256
NSLAB = 30


def _flat(deps):
    if deps is None:
        return
    if isinstance(deps, tuple) and len(deps) == 2 and isinstance(deps[1], int):
        yield deps
        return
    for d in deps:
        yield from _flat(d)


class DSem:
    def __init__(self, sem):
        self.sem = sem
        self.n = 0


class Eng:
    def __init__(self, name, sem):
        self.name = name
        self.sem = sem
        self.n = 0
        self.seen = {}
        self.prog = []
        self.chain = (name in ("act", "dve", "pool"))

    def _waits(self, deps):
        waits = []
        for d in _flat(deps):
            sem, val = d
            k = id(sem)
            if self.seen.get(k, 0) >= val:
                continue
            self.seen[k] = val
            waits.append((sem, val))
        return waits

    def add(self, fn, deps=(), sig=True):
        if self.chain and self.n > 0:
            deps = [deps, (self.sem, self.n)]
        waits = self._waits(deps)
        tok = None
        if sig:
            self.n += 1
            tok = (self.sem, self.n)
        self.prog.append((waits, fn, 1 if sig else 0, None))
        return tok

    def dma(self, dsem, out, in_, deps=(), **kw):
        waits = self._waits(deps)
        dsem.n += 16
        self.prog.append((waits, lambda e: e.dma_start(out=out, in_=in_, **kw), 2, dsem.sem))
        return (dsem.sem, dsem.n)

    def wait_only(self, deps):
        waits = self._waits(deps)
        if waits:
            self.prog.append((waits, None, 0, None))

    def replay(self, e):
        for waits, fn, kind, dsem in self.prog:
            for sem, val in waits:
                e.wait_ge(sem, val)
            if fn is None:
                continue
            inst = fn(e)
            if kind == 1:
                inst.then_inc(self.sem, 1)
            elif kind == 2:
                inst.then_inc(dsem, 16)


class _Stop(Exception):
    pass


def build_program(S=8192, PAST=2048, upto=None, dbg=None):
    NTP = S // 512
    NKB = S // 128
    NCB = PAST // 128
    nc = bass.Bass("TRN2", target_bir_lowering=False)
    es = ExitStack()

    def din(name, shape):
        return nc.dram_tensor(name, shape, F32, kind="ExternalInput").ap()

    def dout(name, shape):
        return nc.dram_tensor(name, shape, F32, kind="ExternalOutput").ap()

    x_prompt = din("x_prompt", [S, D])
    x_sample = din("x_sample", [NBS * SSEQ, D])
    cache_k = din("cache_k", [DEPTH, NBS, PAST, 512])
    cache_v = din("cache_v", [DEPTH, NBS, PAST, 4, 128])
    state_pool = din("state_pool", [DEPTH, NBS, 15, 512])
    cache_mem_k = din("cache_mem_k", [DEPTH, NBS, NMEM, D])
    cache_mem_v = din("cache_mem_v", [DEPTH, NBS, NMEM, D])
    mem_prompt = din("mem_prompt", [NMEM, D])
    norm_mix_g = din("norm_mix_g", [DEPTH, D])
    w_in = din("w_in", [DEPTH, D, 2048])
    lam_q = din("lam_q", [DEPTH, 128])
    lam_k = din("lam_k", [DEPTH, 128])
    subln_g = din("subln_g", [DEPTH, 128])
    w_pool = din("w_pool", [DEPTH, 4, 128, 128])
    pool_scale = din("pool_scale", [DEPTH, 512])
    w_out = din("w_out", [DEPTH, D, D])
    norm_x_g = din("norm_x_g", [DEPTH, D])
    norm_mem_g = din("norm_mem_g", [DEPTH, D])
    wq_x = din("wq_x", [DEPTH, D, D])
    wk_x = din("wk_x", [DEPTH, D, D])
    wv_x = din("wv_x", [DEPTH, D, D])
    wo_x = din("wo_x", [DEPTH, D, D])
    norm_mlp_g = din("norm_mlp_g", [DEPTH, D])
    w_up = din("w_up", [DEPTH, D, 4096])
    w_down = din("w_down", [DEPTH, 4096, D])
    final_g = din("final_g", [1, D])

    y_prompt = dout("y_prompt", [S, D])
    y_sample = dout("y_sample", [NBS * SSEQ, D])
    k_prompt = dout("k_prompt", [DEPTH, S, 512])
    v_prompt = dout("v_prompt", [DEPTH, S, 512])
    pool_prompt = dout("pool_prompt", [DEPTH, 15, 512])
    mem_k_prompt = dout("mem_k_prompt", [DEPTH, NMEM, D])
    mem_v_prompt = dout("mem_v_prompt", [DEPTH, NMEM, D])
    k_sample = dout("k_sample", [DEPTH, NBS * SSEQ, 512])
    v_sample = dout("v_sample", [DEPTH, NBS * SSEQ, 512])
    pool_sample = dout("pool_sample", [DEPTH, NBS, 15, 512])

    wscr = nc.dram_tensor("wscr", [DEPTH, NSLAB, 128, 8, 512], BF16, kind="Internal").ap()
    xscr = nc.dram_tensor("xscr", [S + NBS * SSEQ, D], F32, kind="Internal").ap()
    ktscr = nc.dram_tensor("ktscr", [DEPTH, 4, 128, S], BF16, kind="Internal").ap()

    def sb(name, shape, dt):
        return es.enter_context(nc.sbuf_tensor(name, shape, dt))

    def ps(name, shape, dt):
        return es.enter_context(nc.psum_tensor(name, shape, dt))

    def newsem(name):
        return es.enter_context(nc.semaphore(name))

    PE = Eng("pe", newsem("s_pe"))
    ACT = Eng("act", newsem("s_act"))
    DVE = Eng("dve", newsem("s_dve"))
    POOL = Eng("pool", newsem("s_pool"))
    SP = Eng("sp", newsem("s_sp"))
    _dsn = [0]

    def dsem():
        _dsn[0] += 1
        return DSem(newsem("d%d" % _dsn[0]))

    VRES_COLS = NKB * 4 * 129
    SAMP_COLS = NCB * 512 + 4 * (PAST) + (NCB + 1) * 4 * 129
    vreg = sb("vreg", [128, max(VRES_COLS, SAMP_COLS)], BF16)
    Vres = vreg[:, 0:VRES_COLS].rearrange("p (k h d) -> p k h d", h=4, d=129)
    o0 = 0
    kc_tok = vreg[:, o0:o0 + NCB * 512].rearrange("p (k c) -> p k c", c=512)
    o0 += NCB * 512
    kT_s = vreg[:, o0:o0 + 4 * PAST].rearrange("p (c k) -> p c k", c=4)
    o0 += 4 * PAST
    V_s = vreg[:, o0:o0 + (NCB + 1) * 4 * 129].rearrange("p (k h d) -> p k h d", h=4, d=129)

    xt = sb("xt", [128, 4, D], F32)
    xn = sb("xn", [128, 4, D], BF16)
    mktok = xn[:, 0:2, :]
    hT = sb("hT", [128, 8, 512], BF16)
    slabs = [sb("slab%d" % i, [128, 8, 512], BF16) for i in range(3)]
    NSTAGE = 3
    stage = [sb("stage%d" % i, [128, 512], F32) for i in range(NSTAGE)]
    aT = sb("aT", [128, 4, 512], BF16)
    pyT = sb("pyT", [128, 4, 512], BF16)
    mkT = sb("mkT", [128, 8, NMEM], BF16)
    mv = sb("mv", [128, 2, D], BF16)
    stats = sb("stats", [128, 512], F32)
    ident = sb("ident", [128, 128], BF16)
    identf = sb("identf", [128, 128], F32)
    ones_bf = sb("ones_bf", [128, 128], BF16)
    gcols = sb("gcols", [128, DEPTH, 4, 8], F32)
    pscol = sb("pscol", [128, DEPTH, 4], F32)
    fg_bc = sb("fg_bc", [128, D], F32)
    sg_bc = sb("sg_bc", [128, DEPTH, 128], F32)
    lam_t = sb("lam_t", [128, DEPTH, 8], F32)
    wpool_bf = sb("wpool_bf", [128, DEPTH, 4, 128], BF16)
    icnt = sb("icnt", [128, 4, 16], F32)
    o1b = sb("o1b", [128, 8, 128], F32)
    osq = sb("osq", [128, 128], F32)
    hist_keep = sb("hist_keep", [128, 4, 16], F32)
    sgcol = sb("sgcol", [128, DEPTH], F32)
    RB = 47 * 1024
    region = sb("region", [128, RB], mybir.dt.uint8)

    class Carver:
        def __init__(self):
            self.off = 0

        def take(self, shape, dt):
            n = int(np.prod(shape[1:]))
            bpe = 2 if dt == BF16 else 4
            nbytes = n * bpe
            off = (self.off + 63) // 64 * 64
            assert off + nbytes <= RB, (off, nbytes, RB)
            ap = region[:, off:off + nbytes].bitcast(dt)
            self.off = off + nbytes
            if len(shape) == 2:
                return ap
            names = "abcd"[:len(shape) - 1]
            pat = "p (%s) -> p %s" % (" ".join(names), " ".join(names))
            kw = {names[i]: shape[i + 1] for i in range(1, len(names))}
            return ap.rearrange(pat, **kw)

    ca = Carver()
    qT = ca.take([128, 4, 512], BF16)
    kTcur = ca.take([128, 4, 512], BF16)
    ksreg = ca.take([128, 4096 + 64], BF16)
    kstream = [ksreg[:, i * 2048:(i + 1) * 2048] for i in range(2)]
    vnew = [ksreg[0:64, i * 516:(i + 1) * 516].rearrange("p (h d) -> p h d", h=4) for i in range(4)]
    PT = [ca.take([128, 2, 512], BF16) for _ in range(3)]
    o1 = ca.take([128, 8, 128], F32)
    a_tok = ca.take([128, 4, 512], BF16)
    uext = ca.take([128, 4, 528], F32)
    utmp_all = ca.take([128, 1056], F32)
    utmp = [utmp_all[:, i * 528:(i + 1) * 528] for i in range(2)]
    otmp = utmp_all[:, 0:1024].rearrange("p (a d) -> p a d", a=8)
    pooled2 = [ca.take([128, 512], BF16) for _ in range(4)]
    lamw = region[:, 0:2048].bitcast(F32).rearrange("p (a b) -> p a b", a=4)
    cb = Carver()
    qxT = cb.take([128, 8, 512], BF16)
    oxT = cb.take([128, 8, 512], BF16)
    PxT = [cb.take([128, 2, 512], BF16) for _ in range(2)]
    rrec = [cb.take([128, 512], F32) for _ in range(2)]
    hid = [cb.take([128, 8, 512], BF16) for _ in range(2)]
    rtmp = [cb.take([128, 512], BF16) for _ in range(2)]

    psA = ps("psA", [128, 2, 512], F32)
    psB = ps("psB", [128, 2, 512], F32)
    psO = ps("psO", [128, 3, 512], F32)
    psT = ps("psT", [128, 512], F32)

    st = {"stat": 0, "slab": 0, "slot": 0, "stage": 0, "ev": 0, "tb": 0}
    slab_free = [None, None, None]
    slab_ds = [dsem() for _ in range(3)]
    slot_aps = [psA[:, 0, :], psA[:, 1, :], psB[:, 0, :], psB[:, 1, :], psT[:, :]]
    slot_bf = [a_.bitcast(BF16) for a_ in slot_aps]
    NSLOT = 5
    slot_free = [None] * NSLOT
    stage_ds = [dsem() for _ in range(NSTAGE)]
    stage_free = [None] * NSTAGE
    conv_tok = {}
    out_toks = []
    state = {"hT_free": None, "xt_free": None, "region_free": None, "kst": [None] * max(NTP, 1),
             "o1_tok": [None, None]}

    def G(k):
        return state.get(k)

    ckn = [0]

    def ck(n=None):
        ckn[0] += 1
        if dbg == ckn[0]:
            raise _Stop()

    def stat(n):
        if st["stat"] + n > 512:
            st["stat"] = 0
        a_ = stats[:, st["stat"]:st["stat"] + n]
        st["stat"] += n
        return a_

    def get_slot():
        i = st["slot"]
        st["slot"] = (i + 1) % NSLOT
        return i

    def tr_group(ins, deps, f32=False):
        i = get_slot()
        view = slot_aps[i] if f32 else slot_bf[i]
        idt = identf if f32 else ident
        tp = None
        off = 0
        for k, in_ap in enumerate(ins):
            n = in_ap.shape[0]
            tp = PE.add(lambda e, in_ap=in_ap, off=off, n=n: e.transpose(
                out=view[:, off:off + n], in_=in_ap, identity=idt[0:n, 0:n]),
                [deps, slot_free[i], t_ident, t_idf] if k == 0 else (), sig=(k == len(ins) - 1))
            off += n
        return i, view, tp

    def evac_eng():
        st["ev"] ^= 1
        return ACT if st["ev"] else DVE

    def ev_copy(eng, out, in_, deps):
        if eng is ACT:
            return ACT.add(lambda e: e.copy(out=out, in_=in_), deps)
        return eng.add(lambda e: e.tensor_copy(out=out, in_=in_), deps)

    def ev_scale(eng, out, in_, sc, deps):
        if eng is ACT:
            return ACT.add(lambda e: e.mul(out=out, in_=in_, mul=sc), deps)
        return eng.add(lambda e: e.tensor_scalar(out=out, in0=in_, scalar1=sc, scalar2=None, op0=ALU.mult), deps)

    def load_slab(l, idx):
        b_ = st["slab"]
        st["slab"] = (b_ + 1) % 3
        tok = SP.dma(slab_ds[b_], slabs[b_][:], wscr[l, idx], deps=[conv_tok[(l, idx)], slab_free[b_]])
        return b_, tok

    def get_stage():
        i = st["stage"]
        st["stage"] = (i + 1) % NSTAGE
        return i

    def store_stage(i, dram_ap, src_ap, dep):
        tok = POOL.dma(stage_ds[i], dram_ap, src_ap, deps=[dep])
        stage_free[i] = tok
        out_toks.append(tok)
        return tok

    def acc_ap(a_, r0=0, rows=128, cols=129):
        return psO[r0:r0 + rows, a_ // 3, (a_ % 3) * 160:(a_ % 3) * 160 + cols]

    t_id0 = POOL.add(lambda e: e.memset(identf[:], 0.0))
    t_idf = POOL.add(lambda e: e.affine_select(out=identf[:], in_=identf[:], pattern=[[-1, 128]],
                                               compare_op=ALU.not_equal, fill=1.0, base=0,
                                               channel_multiplier=1), [t_id0])
    t_ident = POOL.add(lambda e: e.tensor_copy(out=ident[:], in_=identf[:]), [t_idf])
    t_ones = POOL.add(lambda e: e.memset(ones_bf[:], 1.0))
    ic_toks = []
    for g in range(4):
        w = 2 << g
        ic_toks.append(POOL.add(lambda e, g=g, w=w: e.memset(icnt[:, g, :], 1.0)))
        for tt in range(min(w - 1, 16)):
            ic_toks.append(POOL.add(lambda e, g=g, tt=tt, w=w: e.memset(icnt[:, g, tt:tt + 1], float(w) / (tt + 1))))
    c_ds = dsem()
    t_const = None
    gsrc = [norm_mix_g, norm_x_g, norm_mem_g, norm_mlp_g]
    for l in range(DEPTH):
        for i, gs in enumerate(gsrc):
            t_const = SP.dma(c_ds, gcols[:, l, i, :], gs[l].rearrange("(c p) -> p c", p=128),
                             allow_slow_non_contiguous=True)
        t_const = SP.dma(c_ds, pscol[:, l, :], pool_scale[l].rearrange("(c p) -> p c", p=128),
                         allow_slow_non_contiguous=True)
        t_const = SP.dma(c_ds, sg_bc[:, l, :], subln_g[l:l + 1, :].broadcast_to([128, 128]))
        t_const = SP.dma(c_ds, sgcol[:, l:l + 1], subln_g[l:l + 1, :].rearrange("o d -> d o"),
                         allow_slow_non_contiguous=True)
        t_const = SP.dma(c_ds, lamw[:, l * 2 + 0, :], lam_q[l:l + 1, :].broadcast_to([128, 128]))
        t_const = SP.dma(c_ds, lamw[:, l * 2 + 1, :], lam_k[l:l + 1, :].broadcast_to([128, 128]))
    t_const = SP.dma(c_ds, fg_bc[:], final_g[0:1, :].broadcast_to([128, D]))
    for l in range(DEPTH):
        for g in range(4):
            t_const = DVE.add(lambda e, l=l, g=g: e.tensor_scalar(out=pscol[:, l, g:g + 1], in0=pscol[:, l, g:g + 1],
                                                                 scalar1=1.0 / (2 << g), scalar2=None, op0=ALU.mult),
                              [t_const])
    wp_ds = dsem()
    t_wp = None
    for l in range(DEPTH):
        t_wp = POOL.dma(wp_ds, wpool_bf[:, l, :, :], w_pool[l].rearrange("g c d -> c g d"))

    def conv(l, idx, src2d):
        d_ = dsem()
        conv_tok[(l, idx)] = POOL.dma(d_, wscr[l, idx], src2d.rearrange("(kc p) c -> p kc c", p=128))

    def conv_list(l):
        out = []
        for h in range(2):
            out.append((8 + h, wk_x[l][:, h * 512:(h + 1) * 512]))
        for h in range(2):
            out.append((10 + h, wv_x[l][:, h * 512:(h + 1) * 512]))
        for s in range(4):
            out.append((s, w_in[l][:, s * 512:(s + 1) * 512]))
        for h in range(2):
            out.append((4 + h, w_out[l][:, h * 512:(h + 1) * 512]))
        for h in range(2):
            out.append((6 + h, wq_x[l][:, h * 512:(h + 1) * 512]))
        for h in range(2):
            out.append((12 + h, wo_x[l][:, h * 512:(h + 1) * 512]))
        for qd in range(4):
            for h in range(2):
                out.append((14 + qd * 2 + h, w_up[l][:, (qd * 2 + h) * 512:(qd * 2 + h + 1) * 512]))
            for h in range(2):
                out.append((22 + qd * 2 + h, w_down[l][qd * 1024:(qd + 1) * 1024, h * 512:(h + 1) * 512]))
        return out

    for (idx_, src_) in conv_list(0):
        conv(0, idx_, src_)
    pending_conv = {l: conv_list(l) for l in range(1, DEPTH)}

    def drip_conv(l, n):
        lst = pending_conv.get(l, [])
        for _ in range(min(n, len(lst))):
            idx_, src_ = lst.pop(0)
            conv(l, idx_, src_)

    lam_inits = [0.8 - 0.6 * math.exp(-0.3 * l) for l in range(DEPTH)]
    lam_tok = []
    for l in range(DEPTH):
        tk = None
        for i in range(2):
            tk = DVE.add(lambda e, l=l, i=i: e.scalar_tensor_tensor(
                out=osq[:, i * 64:(i + 1) * 64], in0=lamw[:, l * 2, i * 64:(i + 1) * 64], scalar=1.0,
                in1=lamw[:, l * 2 + 1, i * 64:(i + 1) * 64], op0=ALU.mult, op1=ALU.mult,
                accum_out=lam_t[:, l, i:i + 1]), [t_const, tk])
        t1 = ACT.add(lambda e, l=l: e.activation(out=lam_t[:, l, 2:4], in_=lam_t[:, l, 0:2], func=AF.Exp), [tk])
        t2 = DVE.add(lambda e, l=l: e.tensor_tensor(out=lam_t[:, l, 4:5], in0=lam_t[:, l, 2:3],
                                                    in1=lam_t[:, l, 3:4], op=ALU.subtract), [t1])
        t3 = DVE.add(lambda e, l=l: e.tensor_scalar(out=lam_t[:, l, 5:6], in0=lam_t[:, l, 4:5],
                                                    scalar1=-1.0, scalar2=-lam_inits[l],
                                                    op0=ALU.mult, op1=ALU.add), [t2])
        t4 = DVE.add(lambda e, l=l: e.tensor_scalar(out=sgcol[:, l:l + 1], in0=sgcol[:, l:l + 1],
                                                    scalar1=1.0 - lam_inits[l], scalar2=None,
                                                    op0=ALU.mult), [t_const, t3])
        lam_tok.append(t4)
    state["region_free"] = list(lam_tok)

    def rms_rstd(ns, deps_x, junk_deps):
        ss = stat(ns)
        rs = stat(ns)
        tks = []
        for s in range(ns):
            dx = deps_x[s] if (isinstance(deps_x, list) and len(deps_x) == ns and st.get("per_s")) else deps_x
            if s % 2 == 0:
                tks.append(ACT.add(lambda e, s=s: e.activation(out=xn[:, s, :], in_=xt[:, s, :], func=AF.Square,
                                                              accum_out=ss[:, s:s + 1]), [dx, junk_deps]))
            else:
                tks.append(DVE.add(lambda e, s=s: e.scalar_tensor_tensor(
                    out=xn[:, s, :], in0=xt[:, s, :], scalar=1.0, in1=xt[:, s, :], op0=ALU.mult, op1=ALU.mult,
                    accum_out=ss[:, s:s + 1]), [dx, junk_deps]))
        t1 = DVE.add(lambda e: e.tensor_scalar(out=ss, in0=ss, scalar1=1.0 / D, scalar2=1e-6,
                                               op0=ALU.mult, op1=ALU.add), tks)
        t2 = ACT.add(lambda e: e.activation(out=ss, in_=ss, func=AF.Ln), [t1])
        t3 = ACT.add(lambda e: e.activation(out=rs, in_=ss, func=AF.Exp, scale=-0.5), [t2])
        ck()
        return rs, t3

    def rmsnorm_hT(ns, gcol, deps_x, deps_hT_free):
        rs, t3 = rms_rstd(ns, deps_x, G("xn_free"))
        xtk = []
        for s in range(ns):
            xtk.append(ev_scale(evac_eng(), xn[:, s, :], xt[:, s, :], rs[:, s:s + 1], [t3]))
        ck()
        out = []
        tp = None
        for kc in range(8):
            i, view, tp = tr_group([xn[:, s, kc * 128:(kc + 1) * 128] for s in range(ns)], [xtk])
            tk = ev_scale(evac_eng(), hT[:, kc, 0:ns * 128], view[:, 0:ns * 128], gcol[:, kc:kc + 1],
                          [tp, t_const, deps_hT_free])
            slot_free[i] = tk
            out.append(tk)
            ck()
        state["xn_free"] = tp
        ck()
        return out

    def mm_group(lhs_fn, rhs_fn, nk, ncol, deps, m=128, kdeps=None):
        i = get_slot()
        tk = None
        for k in range(nk):
            dk = [deps, slot_free[i]] if k == 0 else []
            if kdeps is not None:
                dk = dk + [kdeps[k]]
            tk = PE.add(lambda e, k=k, i=i: e.matmul(slot_aps[i][0:m, 0:ncol], lhsT=lhs_fn(k), rhs=rhs_fn(k),
                                                     start=(k == 0), stop=(k == nk - 1)),
                        dk, sig=(k == nk - 1))
        return i, tk

    ks_ds = [dsem(), dsem()]
    ks_free = [None, None]
    ks_i = [0]
    kst_ds = [dsem(), dsem()]
    xl_ds = [dsem() for _ in range(4)]
    xs_ds = [dsem() for _ in range(4)]
    ptfree = [None, None, None]
    accfree = [None] * 8
    misc_ds = dsem()
    mem_ds = dsem()
    ps_ds = [dsem() for _ in range(NBS)]
    kc_ds = dsem()
    xt_store = [None] * 4

    def mem_kv_prompt(l):
        d_ = dsem()
        tl = SP.dma(d_, xt[:, 0:2, :], mem_prompt.rearrange("(s p) c -> p s c", p=128), deps=[G("xt_free")])
        hts = rmsnorm_hT(2, gcols[:, l, 2, :], [tl], G("hT_free"))
        last_pe = None
        for which in range(2):
            for h in range(2):
                b_, tsl = load_slab(l, 8 + which * 2 + h)
                slab = slabs[b_]
                ck()
                for s in range(2):
                    i, tk = mm_group(lambda k, s=s: hT[:, k, s * 128:(s + 1) * 128],
                                     lambda k, slab=slab: slab[:, k, :], 8, 512, [tsl], kdeps=hts)
                    si = get_stage()
                    te = ev_copy(evac_eng(), stage[si][:], slot_aps[i][:, :], [tk, stage_free[si]])
                    dst = (mem_k_prompt if which == 0 else mem_v_prompt)[l, s * 128:(s + 1) * 128, h * 512:(h + 1) * 512]
                    store_stage(si, dst, stage[si][:], te)
                    ck()
                    if which == 1:
                        te2 = ev_copy(DVE, mv[:, s, h * 512:(h + 1) * 512], slot_aps[i][:, :], [tk, te])
                        slot_free[i] = [te, te2]
                    else:
                        slot_free[i] = te
                    last_pe = tk
                    ck()
                if which == 0:
                    for cc in range(4):
                        i, tk = mm_group(lambda k, cc=cc, slab=slab: slab[:, k, cc * 128:(cc + 1) * 128],
                                         lambda k: hT[:, k, 0:256], 8, 256, [tsl], kdeps=hts)
                        te = ev_copy(evac_eng(), mkT[:, h * 4 + cc, :], slot_aps[i][:, 0:256], [tk])
                        slot_free[i] = te
                        last_pe = tk
                        ck()
                slab_free[b_] = last_pe
        state["hT_free"] = last_pe
        state["xt_free"] = last_pe
        return last_pe

    def cross_attn(l, c0, ncol, dep_in):
        last = None
        for hx in range(4):
            pb = hx % 2
            ptok = []
            for mb in range(2):
                i = get_slot()
                tk = None
                for half in range(2):
                    tk = PE.add(lambda e, i=i, half=half, mb=mb, hx=hx: e.matmul(
                        slot_aps[i][:, 0:ncol], lhsT=mkT[:, hx * 2 + half, mb * 128:(mb + 1) * 128],
                        rhs=qxT[:, hx * 2 + half, c0:c0 + ncol], start=(half == 0), stop=(half == 1)),
                        [dep_in, slot_free[i]] if half == 0 else (), sig=(half == 1))
                te = ACT.add(lambda e, i=i, mb=mb, pb=pb: e.activation(
                    out=PxT[pb][:, mb, 0:ncol], in_=slot_aps[i][:, 0:ncol], func=AF.Exp, scale=1.0 / 16.0),
                    [tk, G("px_free%d" % pb)])
                slot_free[i] = te
                ptok.append(te)
            i = get_slot()
            tk = None
            for mb in range(2):
                tk = PE.add(lambda e, i=i, mb=mb, pb=pb: e.matmul(
                    slot_aps[i][:, 0:ncol], lhsT=ones_bf[:, :], rhs=PxT[pb][:, mb, 0:ncol],
                    start=(mb == 0), stop=(mb == 1)),
                    [ptok, slot_free[i], t_ones] if mb == 0 else (), sig=(mb == 1))
            tr = DVE.add(lambda e, i=i, pb=pb: e.reciprocal(out=rrec[pb][:, 0:ncol], in_=slot_aps[i][:, 0:ncol]),
                         [tk, G("rr_free%d" % pb)])
            slot_free[i] = tr
            tms = []
            for half in range(2):
                i = get_slot()
                tk = None
                for mb in range(2):
                    tk = PE.add(lambda e, i=i, mb=mb, pb=pb, hx=hx, half=half: e.matmul(
                        slot_aps[i][:, 0:ncol], lhsT=mv[:, mb, hx * 256 + half * 128: hx * 256 + (half + 1) * 128],
                        rhs=PxT[pb][:, mb, 0:ncol], start=(mb == 0), stop=(mb == 1)),
                        [ptok, slot_free[i]] if mb == 0 else (), sig=(mb == 1))
                tm = DVE.add(lambda e, i=i, pb=pb, hx=hx, half=half: e.tensor_tensor(
                    out=oxT[:, hx * 2 + half, c0:c0 + ncol], in0=slot_aps[i][:, 0:ncol],
                    in1=rrec[pb][:, 0:ncol], op=ALU.mult), [tk, tr, G("ox_free")])
                slot_free[i] = tm
                tms.append(tm)
                last = tk
            state["px_free%d" % pb] = last
            state["rr_free%d" % pb] = tms
            state["ox_toks"] = state.get("ox_toks", []) + tms
        return last

    def subln(l, j, tev, r0, rows, items):
        ob = o1 if j == 0 else o1b
        n = len(items)
        ssq = stat(n)
        R = slice(r0, r0 + rows)
        tks = []
        for k, (a_, h, sub) in enumerate(items):
            tks.append(DVE.add(lambda e, a_=a_, k=k: e.scalar_tensor_tensor(
                out=osq[R, :], in0=ob[R, a_, :], scalar=1.0, in1=ob[R, a_, :], op0=ALU.mult, op1=ALU.mult,
                accum_out=ssq[R, k:k + 1]), [tev]))
        t1 = DVE.add(lambda e: e.tensor_scalar(out=ssq[R, :], in0=ssq[R, :], scalar1=1.0 / 128, scalar2=1e-5,
                                               op0=ALU.mult, op1=ALU.add), tks)
        t2 = ACT.add(lambda e: e.activation(out=ssq[R, :], in_=ssq[R, :], func=AF.Ln), [t1])
        t3 = ACT.add(lambda e: e.activation(out=ssq[R, :], in_=ssq[R, :], func=AF.Exp, scale=-0.5), [t2])
        outs = []
        for k, (a_, h, sub) in enumerate(items):
            outs.append(DVE.add(lambda e, a_=a_, h=h, sub=sub, k=k: e.tensor_scalar(
                out=a_tok[R, sub, h * 128:(h + 1) * 128], in0=ob[R, a_, :],
                scalar1=ssq[R, k:k + 1], scalar2=None, op0=ALU.mult),
                [t3, lam_tok[l], G("aT_free")]))
        state["o1_free"] = outs
        return outs

    def evac_pass(l, m, j, tlast, r0, rows, accs):
        ob = o1 if j == 0 else o1b
        rc = stat(8)
        R = slice(r0, r0 + rows)
        tev = []
        for a_ in accs:
            tr = DVE.add(lambda e, a_=a_: e.reciprocal(out=rc[R, a_:a_ + 1],
                                                       in_=acc_ap(a_, r0, rows)[:, 128:129]), [tlast])
            if m == 0:
                te = DVE.add(lambda e, a_=a_: e.tensor_scalar(
                    out=ob[R, a_, :], in0=acc_ap(a_, r0, rows)[:, 0:128], scalar1=rc[R, a_:a_ + 1],
                    scalar2=None, op0=ALU.mult), [tr, G("o1_free")])
            else:
                tn = DVE.add(lambda e, a_=a_: e.tensor_scalar(out=rc[R, a_:a_ + 1], in0=rc[R, a_:a_ + 1],
                                                              scalar1=lam_t[R, l, 5:6], scalar2=None, op0=ALU.mult),
                             [tr, lam_tok[l]])
                te = DVE.add(lambda e, a_=a_: e.scalar_tensor_tensor(
                    out=ob[R, a_, :], in0=acc_ap(a_, r0, rows)[:, 0:128], scalar=rc[R, a_:a_ + 1],
                    in1=ob[R, a_, :], op0=ALU.mult, op1=ALU.add), [tn, state["o1_tok"][j]])
            tev.append(te)
        for a_ in range(8):
            accfree[a_] = tev
        return tev

    def subln_fast(l, j, tev):
        ob = o1 if j == 0 else o1b
        ssq = stat(8)
        t0 = DVE.add(lambda e: e.tensor_tensor(out=otmp[:, :, :], in0=ob[:, :, :], in1=ob[:, :, :], op=ALU.mult),
                     [tev, G("otmp_free")])
        t0b = DVE.add(lambda e: e.tensor_reduce(out=ssq, in_=otmp[:, :, :], axis=mybir.AxisListType.X, op=ALU.add), [t0])
        t1 = DVE.add(lambda e: e.tensor_scalar(out=ssq, in0=ssq, scalar1=1.0 / 128, scalar2=1e-5,
                                               op0=ALU.mult, op1=ALU.add), [t0b])
        t2 = ACT.add(lambda e: e.activation(out=ssq, in_=ssq, func=AF.Ln), [t1])
        t3 = ACT.add(lambda e: e.activation(out=ssq, in_=ssq, func=AF.Exp, scale=-0.5), [t2])
        dst = a_tok[:, :, 2 * j * 128:(2 * j + 2) * 128].rearrange("p q (h d) -> p h q d", h=2)
        t5 = DVE.add(lambda e: e.tensor_tensor(
            out=dst, in0=ob[:, :, :].rearrange("p (h q) d -> p h q d", h=2),
            in1=ssq.rearrange("p (h q) -> p h q", h=2).unsqueeze(3).to_broadcast([128, 2, 4, 128]), op=ALU.mult),
            [t3, lam_tok[l], G("aT_free")])
        state["o1_free"] = [t5]
        state["otmp_free"] = t5
        return [t5]

    def evac_pass_fast(l, m, j, tlast, tpool_done):
        ob = o1 if j == 0 else o1b
        rc = stat(8)
        gate = []
        tmul = []
        for b_ in range(3):
            na = 3 if b_ < 2 else 2
            bank = psO[:, b_, 0:480].rearrange("p (a c) -> p a c", c=160)
            rcb = rc[:, 3 * b_:3 * b_ + na]
            tr = DVE.add(lambda e, bank=bank, rcb=rcb, na=na: e.reciprocal(out=rcb, in_=bank[:, 0:na, 128]), [tlast])
            dst = (ob if m == 0 else otmp)[:, 3 * b_:3 * b_ + na, :]
            te = DVE.add(lambda e, bank=bank, rcb=rcb, na=na, dst=dst: e.tensor_tensor(
                out=dst, in0=bank[:, 0:na, 0:128], in1=rcb.unsqueeze(2).to_broadcast([128, na, 128]),
                op=ALU.mult), [tr, G("o1_free") if m == 0 else tpool_done, G("otmp_free")])
            gate.append(te)
            tmul.append(te)
        for a_ in range(8):
            accfree[a_] = gate[a_ // 3]
        if m == 0:
            return tmul
        tadd = DVE.add(lambda e: e.scalar_tensor_tensor(out=ob[:, :, :], in0=otmp[:, :, :], scalar=lam_t[:, l, 5:6],
                                                         in1=ob[:, :, :], op0=ALU.mult, op1=ALU.add),
                       [tmul, state["o1_tok"][j], lam_tok[l]])
        state["otmp_free"] = tadd
        return [tadd]

    def attn_prompt(l, t, tq, tkT, tv, tpool_done):
        nprev = 4 * t
        rf_attn = G("region_free")
        deferred = []
        ta_all = []
        tlast = None
        for c in range(4):
            m, j = divmod(c, 2)
            chunks = [(i * 16, min((i + 1) * 16, nprev)) for i in range((nprev + 15) // 16)]
            cbuf = []

            def issue_chunk(ci):
                b0, b1 = chunks[ci]
                bi = ks_i[0]
                ks_i[0] ^= 1
                tkl = SP.dma(ks_ds[bi], kstream[bi][:, 0:(b1 - b0) * 128], ktscr[l, c, :, b0 * 128:b1 * 128],
                             deps=[ks_free[bi], state["kst"][t - 1], rf_attn])
                cbuf.append((bi, tkl))

            if chunks:
                issue_chunk(0)
            nkb = nprev + 4
            pend = {}

            def emit_qk(kb):
                dg = kb - nprev
                q0 = max(0, dg) * 128
                sp_i = kb % 2
                spair = psA if sp_i == 0 else psB
                pti = kb % 3
                tk = None
                if kb < nprev and kb % 16 == 0 and kb // 16 + 1 < len(chunks):
                    issue_chunk(kb // 16 + 1)
                for hl in range(2):
                    if kb < nprev:
                        bi, tkl = cbuf[kb // 16]
                        lo = (kb % 16) * 128
                        ksrc = kstream[bi][hl * 64:(hl + 1) * 64, lo:lo + 128]
                        kdep = tkl
                    else:
                        ksrc = kTcur[hl * 64:(hl + 1) * 64, c, dg * 128:(dg + 1) * 128]
                        kdep = tkT[c]
                    qsrc = qT[hl * 64:(hl + 1) * 64, c, q0:512]
                    tk = PE.add(lambda e, hl=hl, ksrc=ksrc, q0=q0, spair=spair, qsrc=qsrc: e.matmul(
                        spair[:, hl, q0:512], lhsT=ksrc, rhs=qsrc,
                        start=True, stop=True),
                        [kdep, tq[c], slot_free[sp_i * 2], slot_free[sp_i * 2 + 1]] if hl == 0 else (),
                        sig=(hl == 1))
                if kb < nprev and (kb % 16 == 15 or kb == nprev - 1):
                    ks_free[cbuf[kb // 16][0]] = tk
                te = ACT.add(lambda e, spair=spair, pti=pti, q0=q0: e.activation(
                    out=PT[pti][:, :, q0:512], in_=spair[:, :, q0:512], func=AF.Exp, scale=0.125),
                    [tk, ptfree[pti]])
                slot_free[sp_i * 2] = te
                slot_free[sp_i * 2 + 1] = te
                if dg >= 0:
                    te = POOL.add(lambda e, pti=pti, q0=q0: e.memset(PT[pti][64:128, :, q0:q0 + 64], 0.0), [te])
                pend[kb] = te

            def emit_pv(kb):
                dg = kb - nprev
                q0 = max(0, dg) * 128
                pti = kb % 3
                te = pend.pop(kb)
                tk2 = None
                for hl in range(2):
                    h = 2 * j + hl
                    for qs in range(q0 // 128, 4):
                        a_ = hl * 4 + qs
                        first = (kb == 0)
                        lastq = (kb == nprev + qs)
                        dps = [te, tv] if (hl == 0 and qs == q0 // 128) else []
                        if first:
                            dps = dps + [accfree[a_]]
                        stf = first and (a_ % 3 == 0)
                        tk2 = PE.add(lambda e, a_=a_, hl=hl, qs=qs, h=h, stf=stf, lastq=lastq: e.matmul(
                            acc_ap(a_), lhsT=PT[pti][:, hl, qs * 128:(qs + 1) * 128], rhs=Vres[:, kb, h, :],
                            start=stf, stop=lastq, skip_group_check=True), dps, sig=(hl == 1 and qs == 3))
                ptfree[pti] = tk2
                return tk2

            tk2 = None
            for step in range(nkb + 1):
                if step < nkb:
                    emit_qk(step)
                if step >= 1:
                    tk2 = emit_pv(step - 1)
                if step == 3 and deferred:
                    ta_all.append(deferred.pop()())
            tlast = tk2
            tev = evac_pass_fast(l, m, j, tlast, tpool_done)
            if m == 0:
                state["o1_tok"][j] = tev
            elif c == 2:
                deferred.append(lambda j=j, tev=tev: subln_fast(l, j, tev))
            else:
                ta_all.append(subln_fast(l, j, tev))
        state["vres_free"] = tlast
        return ta_all

    def issue_k_load(l, bq, deps):
        return POOL.dma(kc_ds, kc_tok[:, :, :], cache_k[l, bq].rearrange("(k p) c -> p k c", p=128), deps=deps)

    def issue_v_load(l, bq, deps):
        t2 = None
        for hh in range(4):
            t2 = POOL.dma(misc_ds, V_s[:, 0:NCB, hh, 0:128],
                          cache_v[l, bq][:, hh, :].rearrange("(k p) d -> p k d", p=128), deps=deps)
        t3 = POOL.add(lambda e: e.memset(V_s[:, :, :, 128:129], 1.0), deps)
        return [t2, t3]

    def attn_sample(l, tq, tkT, tv):
        ta_all = []
        pre = state.pop("samp_pre")
        tk_load, tv_load = pre
        for bq in range(NBS):
            r0 = (bq % 2) * 64
            prev = [G("samp_free"), G("vres_free")]
            tc = [tk_load, tv_load]
            tkt = []
            for c in range(4):
                for k0 in range(0, NCB, 4):
                    nn = min(4, NCB - k0)
                    i, view, tp = tr_group([kc_tok[:, k0 + kk, c * 128:(c + 1) * 128] for kk in range(nn)], [tk_load])
                    te = ev_copy(DVE, kT_s[:, c, k0 * 128:(k0 + nn) * 128], view[:, 0:nn * 128], [tp, prev])
                    slot_free[i] = te
                    tkt.append(te)
            tvn = DVE.add(lambda e, bq=bq: e.tensor_copy(out=V_s[0:64, NCB, :, 0:128], in_=vnew[bq][0:64, :, 0:128]),
                          [tv, tv_load])
            if bq + 1 < NBS:
                tk_load = issue_k_load(l, bq + 1, [tp])
            lastpe = None
            for c in range(4):
                m, j = divmod(c, 2)
                pend = {}
                groups = [list(range(g0, min(g0 + 8, NCB))) for g0 in range(0, NCB, 8)] + [[NCB]]

                def s_qk(gi):
                    kbs = groups[gi]
                    nk = 128 if kbs[0] < NCB else 64
                    sp_i = gi % 2
                    spair = psA if sp_i == 0 else psB
                    pti = gi % 3
                    tk = None
                    n = len(kbs)
                    for ki, kb in enumerate(kbs):
                        for hl in range(2):
                            if kb < NCB:
                                ksrc = kT_s[hl * 64:(hl + 1) * 64, c, kb * 128:(kb + 1) * 128]
                            else:
                                ksrc = kTcur[hl * 64:(hl + 1) * 64, c, bq * 64:(bq + 1) * 64]
                            qsrc = qT[hl * 64:(hl + 1) * 64, c, bq * 64:(bq + 1) * 64]
                            osl = spair[0:nk, hl, ki * 64:(ki + 1) * 64]
                            first = (ki == 0 and hl == 0)
                            tk = PE.add(lambda e, ksrc=ksrc, qsrc=qsrc, osl=osl: e.matmul(
                                osl, lhsT=ksrc, rhs=qsrc, start=True, stop=True, skip_group_check=True),
                                [tkt, tkT[c], tq[c], slot_free[sp_i * 2], slot_free[sp_i * 2 + 1]] if first else (),
                                sig=(ki == n - 1 and hl == 1))
                    src_ap = spair[0:nk, :, 0:n * 64]
                    dst_ap = PT[pti][0:nk, :, 0:n * 64]
                    te = ACT.add(lambda e, src_ap=src_ap, dst_ap=dst_ap: e.activation(
                        out=dst_ap, in_=src_ap, func=AF.Exp, scale=0.125), [tk, ptfree[pti]])
                    slot_free[sp_i * 2] = te
                    slot_free[sp_i * 2 + 1] = te
                    pend[gi] = te

                def s_pv(gi):
                    kbs = groups[gi]
                    nk = 128 if kbs[0] < NCB else 64
                    pti = gi % 3
                    te = pend.pop(gi)
                    tk2 = None
                    n = len(kbs)
                    for ki, kb in enumerate(kbs):
                        for hl in range(2):
                            h = 2 * j + hl
                            a_ = hl * 3
                            first = (kb == 0)
                            dps = [te, tvn] if (ki == 0 and hl == 0) else []
                            if first:
                                dps = dps + [accfree[a_]]
                            oacc = acc_ap(a_, r0, 64)
                            lsrc = PT[pti][0:nk, hl, ki * 64:(ki + 1) * 64]
                            rsrc = V_s[0:nk, kb, h, :]
                            tk2 = PE.add(lambda e, oacc=oacc, lsrc=lsrc, rsrc=rsrc, first=first, kb=kb: e.matmul(
                                oacc, lhsT=lsrc, rhs=rsrc, start=first, stop=(kb == NCB), skip_group_check=True),
                                dps, sig=(ki == n - 1 and hl == 1))
                    ptfree[pti] = tk2
                    return tk2

                tk2 = None
                for step in range(len(groups) + 1):
                    if step < len(groups):
                        s_qk(step)
                    if step >= 1:
                        tk2 = s_pv(step - 1)
                lastpe = tk2
                tev = evac_pass(l, m, j, lastpe, r0, 64, [0, 3])
                if m == 0:
                    state["o1_tok"][j] = tev
                else:
                    ta_all.append(subln(l, j, tev, r0, 64, [(hl * 3, 2 * j + hl, bq // 2) for hl in range(2)]))
            state["samp_free"] = [lastpe, ta_all[-2:]]
            if bq + 1 < NBS:
                tv_load = issue_v_load(l, bq + 1, [lastpe])
        return ta_all

    def load_mem_sample(l, bq):
        prev = [G("mem_free")]
        t1 = POOL.dma(mem_ds, mktok[:, :, :], cache_mem_k[l, bq].rearrange("(s p) c -> p s c", p=128),
                      deps=[prev, G("xn_free")])
        t2 = POOL.dma(mem_ds, mv[:, :, :], cache_mem_v[l, bq].rearrange("(s p) c -> p s c", p=128), deps=[prev])
        toks = []
        tp = None
        for cc in range(8):
            i, view, tp = tr_group([mktok[:, s, cc * 128:(cc + 1) * 128] for s in range(2)], [t2, prev])
            te = ev_copy(evac_eng(), mkT[:, cc, :], view[:, 0:256], [tp])
            slot_free[i] = te
            toks.append(te)
        state["xn_free"] = tp
        return toks + [t2]

    def load_pool_state(l):
        uview = uext[:, :, 0:4 * 80].rearrange("p g (b c) -> p g b c", b=4)
        wdeps = [G("uext_free"), G("region_free")]
        toks = []
        for bq in range(NBS):
            si = get_stage()
            t1 = POOL.dma(ps_ds[bq], stage[si][0:15, :], state_pool[l, bq], deps=[stage_free[si]])
            tp = None
            for g in range(4):
                i, view, tp = tr_group([stage[si][0:15, g * 128:(g + 1) * 128]], [t1], f32=True)
                te = DVE.add(lambda e, g=g, bq=bq, view=view: e.tensor_copy(out=uview[:, g, bq, 1:16], in_=view[:, 0:15]),
                             [tp, wdeps])
                slot_free[i] = te
                toks.append(te)
            stage_free[si] = tp
        tz = POOL.add(lambda e: e.memset(uview[:, :, :, 0:1], 0.0), wdeps)
        return toks + [tz]

    def run_tile(l, t):
        is_s = (t == NTP)
        ns = 2 if is_s else 4
        ntok = ns * 128
        row0 = S if is_s else t * 512
        orow = 0 if is_s else row0
        groups = [(b_ * 64, 64) for b_ in range(4)] if is_s else [(s * 128, 128) for s in range(4)]
        lb = 64 if is_s else 512
        nb = 4 if is_s else 1
        uview = uext[:, :, 0:nb * (16 + lb)].rearrange("p g (b c) -> p g b c", b=nb)
        if l == 0:
            src_x = (x_sample if is_s else x_prompt[row0:row0 + ntok, :])
        else:
            src_x = xscr[row0:row0 + ntok, :]
        xf = G("xt_free")
        tl = []
        for s in range(ns):
            dps = [xf[s] if isinstance(xf, list) and s < len(xf) else xf, G("xs_all") if l > 0 else None]
            tl.append(SP.dma(xl_ds[s], xt[:, s, :], src_x[s * 128:(s + 1) * 128, :], deps=dps))
        st["per_s"] = True
        hts = rmsnorm_hT(ns, gcols[:, l, 0, :], tl, [G("hT_free")])
        st["per_s"] = False
        if l + 1 < DEPTH:
            per = -(-NSLAB // max(NTP - 1, 1))
            drip_conv(l + 1, NSLAB if (is_s or t == NTP - 1) else per)
        rf = G("region_free")
        if is_s:
            d0 = [G("samp_free"), G("vres_free")]
            state["samp_pre"] = (issue_k_load(l, 0, d0), issue_v_load(l, 0, d0))
        if not is_s:
            state["hist_tok"] = POOL.add(lambda e: e.tensor_copy(out=uext[:, :, 0:16], in_=hist_keep[:, :, :]),
                                         [rf, G("hk_tok"), G("uext_free")])
        b_, tsl = load_slab(l, 0)
        slab = slabs[b_]
        tq = []
        lastpe = None
        for cc in range(4):
            i, tk = mm_group(lambda k, cc=cc, slab=slab: slab[:, k, cc * 128:(cc + 1) * 128],
                             lambda k: hT[:, k, 0:ntok], 8, ntok, [tsl], kdeps=hts)
            te = ev_copy(evac_eng(), qT[:, cc, 0:ntok], slot_aps[i][:, 0:ntok], [tk, rf])
            slot_free[i] = te
            tq.append(te)
            lastpe = tk
        slab_free[b_] = lastpe
        b_, tsl = load_slab(l, 1)
        slab = slabs[b_]
        tkT = []
        for cc in range(4):
            i, tk = mm_group(lambda k, cc=cc, slab=slab: slab[:, k, cc * 128:(cc + 1) * 128],
                             lambda k: hT[:, k, 0:ntok], 8, ntok, [tsl], kdeps=hts)
            te = ev_copy(evac_eng(), kTcur[:, cc, 0:ntok], slot_aps[i][:, 0:ntok], [tk, rf, G("kst_last")])
            slot_free[i] = te
            tkT.append(te)
        if not is_s:
            tks_ = POOL.dma(kst_ds[t % 2], ktscr[l, :, :, t * 512:(t + 1) * 512].rearrange("c p k -> p c k"),
                            kTcur[:, :, :], deps=[tkT])
            state["kst"][t] = tks_
            state["kst_last"] = tks_
            out_toks.append(tks_)
        for gi, (g0, gn) in enumerate(groups):
            i, tk = mm_group(lambda k, g0=g0, gn=gn: hT[:, k, g0:g0 + gn],
                             lambda k, slab=slab: slab[:, k, :], 8, 512, [tsl], m=gn, kdeps=hts)
            si = get_stage()
            te = ev_copy(evac_eng(), stage[si][0:gn, :], slot_aps[i][0:gn, :], [tk, stage_free[si]])
            slot_free[i] = te
            dst = (k_sample if is_s else k_prompt)[l, orow + g0:orow + g0 + gn, :]
            store_stage(si, dst, stage[si][0:gn, :], te)
            lastpe = tk
        slab_free[b_] = lastpe
        b_, tsl = load_slab(l, 2)
        slab = slabs[b_]
        tv = []
        for gi, (g0, gn) in enumerate(groups):
            i, tk = mm_group(lambda k, g0=g0, gn=gn: hT[:, k, g0:g0 + gn],
                             lambda k, slab=slab: slab[:, k, :], 8, 512, [tsl], m=gn, kdeps=hts)
            si = get_stage()
            te = ev_copy(ACT, stage[si][0:gn, :], slot_aps[i][0:gn, :], [tk, stage_free[si]])
            dst = (v_sample if is_s else v_prompt)[l, orow + g0:orow + g0 + gn, :]
            store_stage(si, dst, stage[si][0:gn, :], te)
            vdst = vnew[gi][0:gn, :, 0:128] if is_s else Vres[:, t * 4 + gi, :, 0:128]
            te2 = DVE.add(lambda e, i=i, gn=gn, vdst=vdst: e.tensor_copy(
                out=vdst, in_=slot_aps[i][0:gn, :].rearrange("p (h d) -> p h d", h=4)),
                [tk, te, G("vres_wr"), rf])
            slot_free[i] = [te, te2]
            tv.append(te2)
            lastpe = tk
        slab_free[b_] = lastpe
        b_, tsl = load_slab(l, 3)
        slab = slabs[b_]
        tu = []
        for g in range(4):
            i, tk = mm_group(lambda k, g=g, slab=slab: slab[:, k, g * 128:(g + 1) * 128],
                             lambda k: hT[:, k, 0:ntok], 8, ntok, [tsl], kdeps=hts)
            te = ev_copy(evac_eng(), uview[:, g, :, 16:16 + lb],
                         slot_aps[i][:, 0:ntok].rearrange("p (b c) -> p b c", b=nb),
                         [tk, G("uext_free"), G("hist_tok"), rf])
            slot_free[i] = te
            tu.append(te)
            lastpe = tk
        if is_s or t == NTP - 1:
            for gi, (g0, gn) in enumerate(groups):
                if (not is_s) and gi != 3:
                    continue
                i, tk = mm_group(lambda k, g0=g0, gn=gn: hT[:, k, g0:g0 + gn],
                                 lambda k, slab=slab: slab[:, k, :], 8, 512, [tsl], m=gn, kdeps=hts)
                si = get_stage()
                te = ev_copy(evac_eng(), stage[si][0:gn, :], slot_aps[i][0:gn, :], [tk, stage_free[si]])
                slot_free[i] = te
                dst = pool_sample[l, gi] if is_s else pool_prompt[l]
                store_stage(si, dst, stage[si][gn - 15:gn, :], te)
                lastpe = tk
        slab_free[b_] = lastpe
        state["hT_free"] = lastpe
        first_stream = (not is_s) and t == 0
        tpool = []
        tpy = []
        for g in range(4):
            w = 2 << g
            cur = uview[:, g, :, :]
            lo = 1
            tk = [tu[g]]
            for lev in range(g + 1):
                sh = 1 << lev
                nlo = lo + sh
                dstb = utmp[lev % 2][:, 0:nb * (16 + lb)].rearrange("p (b c) -> p b c", b=nb)
                tk = POOL.add(lambda e, dstb=dstb, cur=cur, nlo=nlo, sh=sh: e.tensor_tensor(
                    out=dstb[:, :, nlo:16 + lb], in0=cur[:, :, nlo:16 + lb],
                    in1=cur[:, :, nlo - sh:16 + lb - sh], op=ALU.add), [tk, G("utmp_free")])
                cur = dstb
                lo = nlo
            pb2 = pooled2[g]
            pvw = pb2[:, 0:ntok].rearrange("p (b c) -> p b c", b=nb)
            tk2 = DVE.add(lambda e, pvw=pvw, cur=cur, g=g, w=w: e.scalar_tensor_tensor(
                out=pvw, in0=uview[:, g, :, 16:16 + lb], scalar=-float(w), in1=cur[:, :, 16:16 + lb],
                op0=ALU.mult, op1=ALU.add), [tk, G("pooled_free")])
            if first_stream:
                tk3 = POOL.add(lambda e, cur=cur, g=g: e.tensor_tensor(
                    out=cur[:, 0, 16:32], in0=cur[:, 0, 16:32], in1=icnt[:, g, :], op=ALU.mult), [tk2, ic_toks])
                tk2 = DVE.add(lambda e, cur=cur, g=g, pb2=pb2, w=w: e.scalar_tensor_tensor(
                    out=pb2[:, 0:16], in0=uext[:, g, 16:32], scalar=-float(w), in1=cur[:, 0, 16:32],
                    op0=ALU.mult, op1=ALU.add), [tk3])
            state["utmp_free"] = tk2
            tpool.append(tk2)
        if not is_s:
            state["hk_tok"] = POOL.add(lambda e: e.tensor_copy(out=hist_keep[:, :, :], in_=uext[:, :, 512:528]), [tpool])
            state["uext_free"] = [state["hk_tok"]] + tpool
        else:
            state["uext_free"] = tpool
        if is_s:
            ta = attn_sample(l, tq, tkT, tv)
        else:
            ta = attn_prompt(l, t, tq, tkT, tv, tpool)
        for g in range(4):
            i = get_slot()
            tk = PE.add(lambda e, i=i, g=g: e.matmul(slot_aps[i][:, 0:ntok], lhsT=wpool_bf[:, l, g, :],
                                                     rhs=pooled2[g][:, 0:ntok], start=True, stop=True),
                        [tpool[g], slot_free[i], t_wp])
            state["pooled_free"] = tk
            te = ev_scale(evac_eng(), pyT[:, g, 0:ntok], slot_aps[i][:, 0:ntok], pscol[:, l, g:g + 1],
                          [tk, t_const, G("pyT_free")])
            slot_free[i] = te
            tpy.append(te)
        taT = []
        tp = None
        for cc in range(4):
            i, view, tp = tr_group([a_tok[:, s, cc * 128:(cc + 1) * 128] for s in range(ns)], [ta])
            tk = ev_scale(evac_eng(), aT[:, cc, 0:ntok], view[:, 0:ntok], sgcol[:, l:l + 1], [tp, G("aT_free"), lam_tok[l]])
            slot_free[i] = tk
            taT.append(tk)
        txo = []
        for h in range(2):
            b_, tsl = load_slab(l, 4 + h)
            slab = slabs[b_]
            for s in range(ns):
                i, tk = mm_group(lambda k, s=s: (aT if k < 4 else pyT)[:, k % 4, s * 128:(s + 1) * 128],
                                 lambda k, slab=slab: slab[:, k, :], 8, 512, [taT, tpy, tsl])
                te = DVE.add(lambda e, i=i, s=s, h=h: e.tensor_tensor(
                    out=xt[:, s, h * 512:(h + 1) * 512], in0=slot_aps[i][:, :],
                    in1=xt[:, s, h * 512:(h + 1) * 512], op=ALU.add), [tk])
                slot_free[i] = te
                txo.append((s, te))
                lastpe = tk
            slab_free[b_] = lastpe
        state["aT_free"] = lastpe
        state["pyT_free"] = lastpe
        regA_done = [lastpe, ta, tpool, tp, G("kst_last")]
        st["per_s"] = True
        hts = rmsnorm_hT(ns, gcols[:, l, 1, :], [[te for (s2, te) in txo if s2 == s] for s in range(ns)], [G("hT_free")])
        st["per_s"] = False
        tqx = []
        for h in range(2):
            b_, tsl = load_slab(l, 6 + h)
            slab = slabs[b_]
            for cc in range(4):
                i, tk = mm_group(lambda k, cc=cc, slab=slab: slab[:, k, cc * 128:(cc + 1) * 128],
                                 lambda k: hT[:, k, 0:ntok], 8, ntok, [tsl], kdeps=hts)
                te = ev_copy(evac_eng(), qxT[:, h * 4 + cc, 0:ntok], slot_aps[i][:, 0:ntok], [tk, regA_done])
                slot_free[i] = te
                tqx.append(te)
                lastpe = tk
            slab_free[b_] = lastpe
        state["hT_free"] = lastpe
        state["ox_toks"] = []
        state["ox_free"] = regA_done
        if is_s:
            for bq in range(NBS):
                tmk = load_mem_sample(l, bq)
                lastpe = cross_attn(l, bq * 64, 64, [tqx, tmk])
                state["mem_free"] = lastpe
        else:
            lastpe = cross_attn(l, 0, 512, [tqx, G("memkv_tok")])
            state["mem_free"] = lastpe
        tox = state["ox_toks"]
        txo = []
        for h in range(2):
            b_, tsl = load_slab(l, 12 + h)
            slab = slabs[b_]
            for s in range(ns):
                i, tk = mm_group(lambda k, s=s: oxT[:, k, s * 128:(s + 1) * 128],
                                 lambda k, slab=slab: slab[:, k, :], 8, 512, [tsl, tox])
                te = DVE.add(lambda e, i=i, s=s, h=h: e.tensor_tensor(
                    out=xt[:, s, h * 512:(h + 1) * 512], in0=slot_aps[i][:, :],
                    in1=xt[:, s, h * 512:(h + 1) * 512], op=ALU.add), [tk])
                slot_free[i] = te
                txo.append((s, te))
                lastpe = tk
            slab_free[b_] = lastpe
        st["per_s"] = True
        hts = rmsnorm_hT(ns, gcols[:, l, 3, :], [[te for (s2, te) in txo if s2 == s] for s in range(ns)], [G("hT_free")])
        st["per_s"] = False
        txm = []
        for qd in range(4):
            hb = qd % 2
            thid = []
            for h in range(2):
                b_, tsl = load_slab(l, 14 + qd * 2 + h)
                slab = slabs[b_]
                for cc in range(4):
                    i, tk = mm_group(lambda k, cc=cc, slab=slab: slab[:, k, cc * 128:(cc + 1) * 128],
                                     lambda k: hT[:, k, 0:ntok], 8, ntok, [tsl], kdeps=hts)
                    rb = (h * 4 + cc) % 2
                    te = ACT.add(lambda e, i=i, rb=rb: e.activation(out=rtmp[rb][:, 0:ntok], in_=slot_aps[i][:, 0:ntok],
                                                                  func=AF.Relu), [tk, G("rt_free%d" % rb), regA_done])
                    slot_free[i] = te
                    tsq = POOL.add(lambda e, rb=rb, hb=hb, h=h, cc=cc: e.tensor_tensor(
                        out=hid[hb][:, h * 4 + cc, 0:ntok], in0=rtmp[rb][:, 0:ntok], in1=rtmp[rb][:, 0:ntok],
                        op=ALU.mult), [te, G("hid_free%d" % hb), regA_done])
                    state["rt_free%d" % rb] = tsq
                    thid.append(tsq)
                    lastpe = tk
                slab_free[b_] = lastpe
            for h in range(2):
                b_, tsl = load_slab(l, 22 + qd * 2 + h)
                slab = slabs[b_]
                for s in range(ns):
                    i, tk = mm_group(lambda k, s=s, hb=hb: hid[hb][:, k, s * 128:(s + 1) * 128],
                                     lambda k, slab=slab: slab[:, k, :], 8, 512, [tsl], kdeps=thid)
                    te = DVE.add(lambda e, i=i, s=s, h=h: e.tensor_tensor(
                        out=xt[:, s, h * 512:(h + 1) * 512], in0=slot_aps[i][:, :],
                        in1=xt[:, s, h * 512:(h + 1) * 512], op=ALU.add), [tk])
                    slot_free[i] = te
                    if qd == 3:
                        txm.append((s, te))
                    lastpe = tk
                slab_free[b_] = lastpe
            state["hid_free%d" % hb] = lastpe
        state["hT_free"] = lastpe
        state["region_free"] = [lastpe, tox]
        txs = [[te for (s2, te) in txm if s2 == s] for s in range(ns)]
        stores = []
        if l == DEPTH - 1:
            st["per_s"] = True
            rs, t3 = rms_rstd(ns, txs, G("xn_free"))
            st["per_s"] = False
            ydst = y_sample if is_s else y_prompt[row0:row0 + ntok, :]
            for s in range(ns):
                ty = DVE.add(lambda e, s=s: e.scalar_tensor_tensor(
                    out=xt[:, s, :], in0=xt[:, s, :], scalar=rs[:, s:s + 1], in1=fg_bc[:, :],
                    op0=ALU.mult, op1=ALU.mult), [t3, t_const])
                stores.append(POOL.dma(xs_ds[s], ydst[s * 128:(s + 1) * 128, :], xt[:, s, :], deps=[ty]))
        else:
            for s in range(ns):
                stores.append(POOL.dma(xs_ds[s], xscr[row0 + s * 128:row0 + (s + 1) * 128, :], xt[:, s, :],
                                       deps=[txs[s]]))
        out_toks.extend(stores)
        for s in range(ns):
            xt_store[s] = stores[s]
        state["xt_free"] = list(xt_store)
        state["xs_all"] = list(xt_store)

    def schedule():
        step = 0
        for l in range(DEPTH):
            if upto is not None and step >= upto:
                return
            t_v1 = POOL.add(lambda e: e.memset(Vres[:, :, :, 128:129], 1.0), [G("samp_free")])
            state["vres_wr"] = [t_v1, G("samp_free")]
            state["memkv_tok"] = [mem_kv_prompt(l)]
            state["hk_tok"] = POOL.add(lambda e: e.memset(hist_keep[:, :, :], 0.0), [G("hist_tok")])
            step += 1
            for t in range(NTP):
                if upto is not None and step >= upto:
                    return
                run_tile(l, t)
                step += 1
            if upto is not None and step >= upto:
                return
            state["hist_tok"] = load_pool_state(l)
            run_tile(l, NTP)
            step += 1

    try:
        schedule()
    except _Stop:
        pass
    SP.wait_only(out_toks)

    block = es.enter_context(nc.Block())

    @block.tensor
    def _(e):
        PE.replay(e)

    @block.scalar
    def _(e):
        ACT.replay(e)

    @block.vector
    def _(e):
        DVE.replay(e)

    @block.gpsimd
    def _(e):
        POOL.replay(e)

    @block.sync
    def _(e):
        SP.replay(e)

    es.close()
    return nc


def make_in_maps(inputs, ncores=8, S=8192, PAST=2048):
    f = lambda a: np.ascontiguousarray(np.asarray(a, dtype=np.float32))
    maps = []
    for c in range(ncores):
        sl = slice(c * NBS, (c + 1) * NBS)
        m = {
            "x_prompt": f(inputs["x_prompt"][c]),
            "x_sample": f(inputs["x_sample"][sl]).reshape(NBS * SSEQ, D),
            "cache_k": f(inputs["cache_k"][:, sl]).reshape(DEPTH, NBS, PAST, 512),
            "cache_v": f(inputs["cache_v"][:, sl]),
            "state_pool": f(inputs["state_pool"][:, sl]),
            "cache_mem_k": f(inputs["cache_mem_k"][:, sl]).reshape(DEPTH, NBS, NMEM, D),
            "cache_mem_v": f(inputs["cache_mem_v"][:, sl]).reshape(DEPTH, NBS, NMEM, D),
            "mem_prompt": f(inputs["mem_prompt"][c]),
            "lam_q": f(inputs["lam_q"]).reshape(DEPTH, 128),
            "lam_k": f(inputs["lam_k"]).reshape(DEPTH, 128),
            "final_g": f(inputs["final_g"]).reshape(1, D),
        }
        for k in ["norm_mix_g", "w_in", "subln_g", "w_pool", "pool_scale", "w_out", "norm_x_g", "norm_mem_g",
                  "wq_x", "wk_x", "wv_x", "wo_x", "norm_mlp_g", "w_up", "w_down"]:
            m[k] = f(inputs[k])
        maps.append(m)
    return maps


def assemble(results, ncores=8, S=8192):
    def cat(name, axis, shape_fn):
        return np.concatenate([shape_fn(r[name]) for r in results], axis=axis)
    y_prompt = np.stack([r["y_prompt"] for r in results], 0)
    y_sample = np.concatenate([r["y_sample"].reshape(NBS, SSEQ, D) for r in results], 0)
    k_prompt = np.stack([r["k_prompt"].reshape(DEPTH, S, 2, 4, 64) for r in results], 1)
    v_prompt = np.stack([r["v_prompt"].reshape(DEPTH, S, 4, 128) for r in results], 1)
    pool_prompt = np.stack([r["pool_prompt"] for r in results], 1)
    mem_k = np.stack([r["mem_k_prompt"].reshape(DEPTH, NMEM, 4, 256) for r in results], 1)
    mem_v = np.stack([r["mem_v_prompt"].reshape(DEPTH, NMEM, 4, 256) for r in results], 1)
    k_sample = np.concatenate([r["k_sample"].reshape(DEPTH, NBS, SSEQ, 2, 4, 64) for r in results], 1)
    v_sample = np.concatenate([r["v_sample"].reshape(DEPTH, NBS, SSEQ, 4, 128) for r in results], 1)
    pool_sample = np.concatenate([r["pool_sample"] for r in results], 1)
    outs = (y_prompt, y_sample, k_prompt, v_prompt, pool_prompt, mem_k, mem_v, k_sample, v_sample, pool_sample)
    return tuple(np.ascontiguousarray(o, dtype=np.float32) for o in outs)


def kernel(**inputs):
    ncores = 8
    nc = build_program()
    in_maps = make_in_maps(inputs, ncores)
    res = run_bass_kernel_spmd(nc, in_maps, core_ids=list(range(ncores)))
    return assemble(res.results, ncores)
```

```python
import math
from contextlib import ExitStack

import numpy as np
import concourse.bass as bass
import concourse.mybir as mybir
from concourse.bass_utils import run_bass_kernel_spmd

F32 = mybir.dt.float32
BF16 = mybir.dt.bfloat16
AF = mybir.ActivationFunctionType
ALU = mybir.AluOpType

D = 1024
DEPTH = 2
NBS = 4
SSEQ = 64
NMEM = 256
NSLAB = 30


def _flat(deps):
    if deps is None:
        return
    if isinstance(deps, tuple) and len(deps) == 2 and isinstance(deps[1], int):
        yield deps
        return
    for d in deps:
        yield from _flat(d)


class DSem:
    def __init__(self, sem):
        self.sem = sem
        self.n = 0


class Eng:
    def __init__(self, name, sem):
        self.name = name
        self.sem = sem
        self.n = 0
        self.seen = {}
        self.prog = []
        self.chain = (name in ("act", "dve", "pool"))

    def _waits(self, deps):
        waits = []
        for d in _flat(deps):
            sem, val = d
            k = id(sem)
            if self.seen.get(k, 0) >= val:
                continue
            self.seen[k] = val
            waits.append((sem, val))
        return waits

    def add(self, fn, deps=(), sig=True):
        if self.chain and self.n > 0:
            deps = [deps, (self.sem, self.n)]
        waits = self._waits(deps)
        tok = None
        if sig:
            self.n += 1
            tok = (self.sem, self.n)
        self.prog.append((waits, fn, 1 if sig else 0, None))
        return tok

    def dma(self, dsem, out, in_, deps=(), **kw):
        waits = self._waits(deps)
        dsem.n += 16
        self.prog.append((waits, lambda e: e.dma_start(out=out, in_=in_, **kw), 2, dsem.sem))
        return (dsem.sem, dsem.n)

    def wait_only(self, deps):
        waits = self._waits(deps)
        if waits:
            self.prog.append((waits, None, 0, None))

    def replay(self, e):
        for waits, fn, kind, dsem in self.prog:
            for sem, val in waits:
                e.wait_ge(sem, val)
            if fn is None:
                continue
            inst = fn(e)
            if kind == 1:
                inst.then_inc(self.sem, 1)
            elif kind == 2:
                inst.then_inc(dsem, 16)


class _Stop(Exception):
    pass


def build_program(S=8192, PAST=2048, upto=None, dbg=None):
    NTP = S // 512
    NKB = S // 128
    NCB = PAST // 128
    nc = bass.Bass("TRN2", target_bir_lowering=False)
    es = ExitStack()

    def din(name, shape):
        return nc.dram_tensor(name, shape, F32, kind="ExternalInput").ap()

    def dout(name, shape):
        return nc.dram_tensor(name, shape, F32, kind="ExternalOutput").ap()

    x_prompt = din("x_prompt", [S, D])
    x_sample = din("x_sample", [NBS * SSEQ, D])
    cache_k = din("cache_k", [DEPTH, NBS, PAST, 512])
    cache_v = din("cache_v", [DEPTH, NBS, PAST, 4, 128])
    state_pool = din("state_pool", [DEPTH, NBS, 15, 512])
    cache_mem_k = din("cache_mem_k", [DEPTH, NBS, NMEM, D])
    cache_mem_v = din("cache_mem_v", [DEPTH, NBS, NMEM, D])
    mem_prompt = din("mem_prompt", [NMEM, D])
    norm_mix_g = din("norm_mix_g", [DEPTH, D])
    w_in = din("w_in", [DEPTH, D, 2048])
    lam_q = din("lam_q", [DEPTH, 128])
    lam_k = din("lam_k", [DEPTH, 128])
    subln_g = din("subln_g", [DEPTH, 128])
    w_pool = din("w_pool", [DEPTH, 4, 128, 128])
    pool_scale = din("pool_scale", [DEPTH, 512])
    w_out = din("w_out", [DEPTH, D, D])
    norm_x_g = din("norm_x_g", [DEPTH, D])
    norm_mem_g = din("norm_mem_g", [DEPTH, D])
    wq_x = din("wq_x", [DEPTH, D, D])
    wk_x = din("wk_x", [DEPTH, D, D])
    wv_x = din("wv_x", [DEPTH, D, D])
    wo_x = din("wo_x", [DEPTH, D, D])
    norm_mlp_g = din("norm_mlp_g", [DEPTH, D])
    w_up = din("w_up", [DEPTH, D, 4096])
    w_down = din("w_down", [DEPTH, 4096, D])
    final_g = din("final_g", [1, D])

    y_prompt = dout("y_prompt", [S, D])
    y_sample = dout("y_sample", [NBS * SSEQ, D])
    k_prompt = dout("k_prompt", [DEPTH, S, 512])
    v_prompt = dout("v_prompt", [DEPTH, S, 512])
    pool_prompt = dout("pool_prompt", [DEPTH, 15, 512])
    mem_k_prompt = dout("mem_k_prompt", [DEPTH, NMEM, D])
    mem_v_prompt = dout("mem_v_prompt", [DEPTH, NMEM, D])
    k_sample = dout("k_sample", [DEPTH, NBS * SSEQ, 512])
    v_sample = dout("v_sample", [DEPTH, NBS * SSEQ, 512])
    pool_sample = dout("pool_sample", [DEPTH, NBS, 15, 512])

    wscr = nc.dram_tensor("wscr", [DEPTH, NSLAB, 128, 8, 512], BF16, kind="Internal").ap()
    xscr = nc.dram_tensor("xscr", [S + NBS * SSEQ, D], F32, kind="Internal").ap()
    ktscr = nc.dram_tensor("ktscr", [DEPTH, 4, 128, S], BF16, kind="Internal").ap()

    def sb(name, shape, dt):
        return es.enter_context(nc.sbuf_tensor(name, shape, dt))

    def ps(name, shape, dt):
        return es.enter_context(nc.psum_tensor(name, shape, dt))

    def newsem(name):
        return es.enter_context(nc.semaphore(name))

    PE = Eng("pe", newsem("s_pe"))
    ACT = Eng("act", newsem("s_act"))
    DVE = Eng("dve", newsem("s_dve"))
    POOL = Eng("pool", newsem("s_pool"))
    SP = Eng("sp", newsem("s_sp"))
    _dsn = [0]

    def dsem():
        _dsn[0] += 1
        return DSem(newsem("d%d" % _dsn[0]))

    VRES_COLS = NKB * 4 * 129
    SAMP_COLS = NCB * 512 + 4 * (PAST) + (NCB + 1) * 4 * 129
    vreg = sb("vreg", [128, max(VRES_COLS, SAMP_COLS)], BF16)
    Vres = vreg[:, 0:VRES_COLS].rearrange("p (k h d) -> p k h d", h=4, d=129)
    o0 = 0
    kc_tok = vreg[:, o0:o0 + NCB * 512].rearrange("p (k c) -> p k c", c=512)
    o0 += NCB * 512
    kT_s = vreg[:, o0:o0 + 4 * PAST].rearrange("p (c k) -> p c k", c=4)
    o0 += 4 * PAST
    V_s = vreg[:, o0:o0 + (NCB + 1) * 4 * 129].rearrange("p (k h d) -> p k h d", h=4, d=129)

    xt = sb("xt", [128, 4, D], F32)
    xn = sb("xn", [128, 4, D], BF16)
    mktok = xn[:, 0:2, :]
    hT = sb("hT", [128, 8, 512], BF16)
    slabs = [sb("slab%d" % i, [128, 8, 512], BF16) for i in range(3)]
    NSTAGE = 3
    stage = [sb("stage%d" % i, [128, 512], F32) for i in range(NSTAGE)]
    aT = sb("aT", [128, 4, 512], BF16)
    pyT = sb("pyT", [128, 4, 512], BF16)
    mkT = sb("mkT", [128, 8, NMEM], BF16)
    mv = sb("mv", [128, 2, D], BF16)
    stats = sb("stats", [128, 512], F32)
    ident = sb("ident", [128, 128], BF16)
    identf = sb("identf", [128, 128], F32)
    ones_bf = sb("ones_bf", [128, 128], BF16)
    gcols = sb("gcols", [128, DEPTH, 4, 8], F32)
    pscol = sb("pscol", [128, DEPTH, 4], F32)
    fg_bc = sb("fg_bc", [128, D], F32)
    sg_bc = sb("sg_bc", [128, DEPTH, 128], F32)
    lam_t = sb("lam_t", [128, DEPTH, 8], F32)
    wpool_bf = sb("wpool_bf", [128, DEPTH, 4, 128], BF16)
    icnt = sb("icnt", [128, 4, 16], F32)
    o1b = sb("o1b", [128, 8, 128], F32)
    osq = sb("osq", [128, 128], F32)
    hist_keep = sb("hist_keep", [128, 4, 16], F32)
    sgcol = sb("sgcol", [128, DEPTH], F32)
    RB = 47 * 1024
    region = sb("region", [128, RB], mybir.dt.uint8)

    class Carver:
        def __init__(self):
            self.off = 0

        def take(self, shape, dt):
            n = int(np.prod(shape[1:]))
            bpe = 2 if dt == BF16 else 4
            nbytes = n * bpe
            off = (self.off + 63) // 64 * 64
            assert off + nbytes <= RB, (off, nbytes, RB)
            ap = region[:, off:off + nbytes].bitcast(dt)
            self.off = off + nbytes
            if len(shape) == 2:
                return ap
            names = "abcd"[:len(shape) - 1]
            pat = "p (%s) -> p %s" % (" ".join(names), " ".join(names))
            kw = {names[i]: shape[i + 1] for i in range(1, len(names))}
            return ap.rearrange(pat, **kw)

    ca = Carver()
    qT = ca.take([128, 4, 512], BF16)
    kTcur = ca.take([128, 4, 512], BF16)
    ksreg = ca.take([128, 4096 + 64], BF16)
    kstream = [ksreg[:, i * 2048:(i + 1) * 2048] for i in range(2)]
    vnew = [ksreg[0:64, i * 516:(i + 1) * 516].rearrange("p (h d) -> p h d", h=4) for i in range(4)]
    PT = [ca.take([128, 2, 512], BF16) for _ in range(3)]
    o1 = ca.take([128, 8, 128], F32)
    a_tok = ca.take([128, 4, 512], BF16)
    uext = ca.take([128, 4, 528], F32)
    utmp_all = ca.take([128, 1056], F32)
    utmp = [utmp_all[:, i * 528:(i + 1) * 528] for i in range(2)]
    otmp = utmp_all[:, 0:1024].rearrange("p (a d) -> p a d", a=8)
    pooled2 = [ca.take([128, 512], BF16) for _ in range(4)]
    lamw = region[:, 0:2048].bitcast(F32).rearrange("p (a b) -> p a b", a=4)
    cb = Carver()
    qxT = cb.take([128, 8, 512], BF16)
    oxT = cb.take([128, 8, 512], BF16)
    PxT = [cb.take([128, 2, 512], BF16) for _ in range(2)]
    rrec = [cb.take([128, 512], F32) for _ in range(2)]
    hid = [cb.take([128, 8, 512], BF16) for _ in range(2)]
    rtmp = [cb.take([128, 512], BF16) for _ in range(2)]

    psA = ps("psA", [128, 2, 512], F32)
    psB = ps("psB", [128, 2, 512], F32)
    psO = ps("psO", [128, 3, 512], F32)
    psT = ps("psT", [128, 512], F32)

    st = {"stat": 0, "slab": 0, "slot": 0, "stage": 0, "ev": 0, "tb": 0}
    slab_free = [None, None, None]
    slab_ds = [dsem() for _ in range(3)]
    slot_aps = [psA[:, 0, :], psA[:, 1, :], psB[:, 0, :], psB[:, 1, :], psT[:, :]]
    slot_bf = [a_.bitcast(BF16) for a_ in slot_aps]
    NSLOT = 5
    slot_free = [None] * NSLOT
    stage_ds = [dsem() for _ in range(NSTAGE)]
    stage_free = [None] * NSTAGE
    conv_tok = {}
    out_toks = []
    state = {"hT_free": None, "xt_free": None, "region_free": None, "kst": [None] * max(NTP, 1),
             "o1_tok": [None, None]}

    def G(k):
        return state.get(k)

    ckn = [0]

    def ck(n=None):
        ckn[0] += 1
        if dbg == ckn[0]:
            raise _Stop()

    def stat(n):
        if st["stat"] + n > 512:
            st["stat"] = 0
        a_ = stats[:, st["stat"]:st["stat"] + n]
        st["stat"] += n
        return a_

    def get_slot():
        i = st["slot"]
        st["slot"] = (i + 1) % NSLOT
        return i

    def tr_group(ins, deps, f32=False):
        i = get_slot()
        view = slot_aps[i] if f32 else slot_bf[i]
        idt = identf if f32 else ident
        tp = None
        off = 0
        for k, in_ap in enumerate(ins):
            n = in_ap.shape[0]
            tp = PE.add(lambda e, in_ap=in_ap, off=off, n=n: e.transpose(
                out=view[:, off:off + n], in_=in_ap, identity=idt[0:n, 0:n]),
                [deps, slot_free[i], t_ident, t_idf] if k == 0 else (), sig=(k == len(ins) - 1))
            off += n
        return i, view, tp

    def evac_eng():
        st["ev"] ^= 1
        return ACT if st["ev"] else DVE

    def ev_copy(eng, out, in_, deps):
        if eng is ACT:
            return ACT.add(lambda e: e.copy(out=out, in_=in_), deps)
        return eng.add(lambda e: e.tensor_copy(out=out, in_=in_), deps)

    def ev_scale(eng, out, in_, sc, deps):
        if eng is ACT:
            return ACT.add(lambda e: e.mul(out=out, in_=in_, mul=sc), deps)
        return eng.add(lambda e: e.tensor_scalar(out=out, in0=in_, scalar1=sc, scalar2=None, op0=ALU.mult), deps)

    def load_slab(l, idx):
        if l == 0:
            ensure_conv0(conv_pos[idx] + 6)
        b_ = st["slab"]
        st["slab"] = (b_ + 1) % 3
        tok = SP.dma(slab_ds[b_], slabs[b_][:], wscr[l, idx], deps=[conv_tok[(l, idx)], slab_free[b_]])
        return b_, tok

    def get_stage():
        i = st["stage"]
        st["stage"] = (i + 1) % NSTAGE
        return i

    def store_stage(i, dram_ap, src_ap, dep):
        tok = POOL.dma(stage_ds[i], dram_ap, src_ap, deps=[dep])
        stage_free[i] = tok
        out_toks.append(tok)
        return tok

    def acc_ap(a_, r0=0, rows=128, cols=129):
        return psO[r0:r0 + rows, a_ // 3, (a_ % 3) * 160:(a_ % 3) * 160 + cols]

    t_id0 = POOL.add(lambda e: e.memset(identf[:], 0.0))
    t_idf = POOL.add(lambda e: e.affine_select(out=identf[:], in_=identf[:], pattern=[[-1, 128]],
                                               compare_op=ALU.not_equal, fill=1.0, base=0,
                                               channel_multiplier=1), [t_id0])
    t_ident = POOL.add(lambda e: e.tensor_copy(out=ident[:], in_=identf[:]), [t_idf])
    t_ones = POOL.add(lambda e: e.memset(ones_bf[:], 1.0))
    ic_toks = []
    for g in range(4):
        w = 2 << g
        ic_toks.append(POOL.add(lambda e, g=g, w=w: e.memset(icnt[:, g, :], 1.0)))
        for tt in range(min(w - 1, 16)):
            ic_toks.append(POOL.add(lambda e, g=g, tt=tt, w=w: e.memset(icnt[:, g, tt:tt + 1], float(w) / (tt + 1))))
    c_ds = dsem()
    t_const = None
    gsrc = [norm_mix_g, norm_x_g, norm_mem_g, norm_mlp_g]
    for l in range(DEPTH):
        for i, gs in enumerate(gsrc):
            t_const = SP.dma(c_ds, gcols[:, l, i, :], gs[l].rearrange("(c p) -> p c", p=128),
                             allow_slow_non_contiguous=True)
        t_const = SP.dma(c_ds, pscol[:, l, :], pool_scale[l].rearrange("(c p) -> p c", p=128),
                         allow_slow_non_contiguous=True)
        t_const = SP.dma(c_ds, sg_bc[:, l, :], subln_g[l:l + 1, :].broadcast_to([128, 128]))
        t_const = SP.dma(c_ds, sgcol[:, l:l + 1], subln_g[l:l + 1, :].rearrange("o d -> d o"),
                         allow_slow_non_contiguous=True)
        t_const = SP.dma(c_ds, lamw[:, l * 2 + 0, :], lam_q[l:l + 1, :].broadcast_to([128, 128]))
        t_const = SP.dma(c_ds, lamw[:, l * 2 + 1, :], lam_k[l:l + 1, :].broadcast_to([128, 128]))
    t_const = SP.dma(c_ds, fg_bc[:], final_g[0:1, :].broadcast_to([128, D]))
    for l in range(DEPTH):
        for g in range(4):
            t_const = DVE.add(lambda e, l=l, g=g: e.tensor_scalar(out=pscol[:, l, g:g + 1], in0=pscol[:, l, g:g + 1],
                                                                 scalar1=1.0 / (2 << g), scalar2=None, op0=ALU.mult),
                              [t_const])
    wp_ds = dsem()
    t_wp = None
    for l in range(DEPTH):
        t_wp = POOL.dma(wp_ds, wpool_bf[:, l, :, :], w_pool[l].rearrange("g c d -> c g d"))

    def conv(l, idx, src2d):
        d_ = dsem()
        conv_tok[(l, idx)] = POOL.dma(d_, wscr[l, idx], src2d.rearrange("(kc p) c -> p kc c", p=128))

    def conv_list(l):
        out = []
        for h in range(2):
            out.append((8 + h, wk_x[l][:, h * 512:(h + 1) * 512]))
        for h in range(2):
            out.append((10 + h, wv_x[l][:, h * 512:(h + 1) * 512]))
        for s in range(4):
            out.append((s, w_in[l][:, s * 512:(s + 1) * 512]))
        for h in range(2):
            out.append((4 + h, w_out[l][:, h * 512:(h + 1) * 512]))
        for h in range(2):
            out.append((6 + h, wq_x[l][:, h * 512:(h + 1) * 512]))
        for h in range(2):
            out.append((12 + h, wo_x[l][:, h * 512:(h + 1) * 512]))
        for qd in range(4):
            for h in range(2):
                out.append((14 + qd * 2 + h, w_up[l][:, (qd * 2 + h) * 512:(qd * 2 + h + 1) * 512]))
            for h in range(2):
                out.append((22 + qd * 2 + h, w_down[l][qd * 1024:(qd + 1) * 1024, h * 512:(h + 1) * 512]))
        return out

    pending_conv = {l: conv_list(l) for l in range(DEPTH)}
    conv_pos = {idx_: i for i, (idx_, _) in enumerate(conv_list(0))}
    issued0 = [0]

    def ensure_conv0(upto_pos):
        while issued0[0] <= min(upto_pos, NSLAB - 1):
            idx_, src_ = pending_conv[0].pop(0)
            conv(0, idx_, src_)
            issued0[0] += 1

    ensure_conv0(7)

    def drip_conv(l, n):
        lst = pending_conv.get(l, [])
        for _ in range(min(n, len(lst))):
            idx_, src_ = lst.pop(0)
            conv(l, idx_, src_)

    lam_inits = [0.8 - 0.6 * math.exp(-0.3 * l) for l in range(DEPTH)]
    lam_tok = []
    for l in range(DEPTH):
        tk = None
        for i in range(2):
            tk = DVE.add(lambda e, l=l, i=i: e.scalar_tensor_tensor(
                out=osq[:, i * 64:(i + 1) * 64], in0=lamw[:, l * 2, i * 64:(i + 1) * 64], scalar=1.0,
                in1=lamw[:, l * 2 + 1, i * 64:(i + 1) * 64], op0=ALU.mult, op1=ALU.mult,
                accum_out=lam_t[:, l, i:i + 1]), [t_const, tk])
        t1 = ACT.add(lambda e, l=l: e.activation(out=lam_t[:, l, 2:4], in_=lam_t[:, l, 0:2], func=AF.Exp), [tk])
        t2 = DVE.add(lambda e, l=l: e.tensor_tensor(out=lam_t[:, l, 4:5], in0=lam_t[:, l, 2:3],
                                                    in1=lam_t[:, l, 3:4], op=ALU.subtract), [t1])
        t3 = DVE.add(lambda e, l=l: e.tensor_scalar(out=lam_t[:, l, 5:6], in0=lam_t[:, l, 4:5],
                                                    scalar1=-1.0, scalar2=-lam_inits[l],
                                                    op0=ALU.mult, op1=ALU.add), [t2])
        t4 = DVE.add(lambda e, l=l: e.tensor_scalar(out=sgcol[:, l:l + 1], in0=sgcol[:, l:l + 1],
                                                    scalar1=1.0 - lam_inits[l], scalar2=None,
                                                    op0=ALU.mult), [t_const, t3])
        lam_tok.append(t4)
    state["region_free"] = list(lam_tok)

    def rms_rstd(ns, deps_x, junk_deps):
        ss = stat(ns)
        rs = stat(ns)
        tks = []
        for s in range(ns):
            dx = deps_x[s] if (isinstance(deps_x, list) and len(deps_x) == ns and st.get("per_s")) else deps_x
            if s % 2 == 0:
                tks.append(ACT.add(lambda e, s=s: e.activation(out=xn[:, s, :], in_=xt[:, s, :], func=AF.Square,
                                                              accum_out=ss[:, s:s + 1]), [dx, junk_deps]))
            else:
                tks.append(DVE.add(lambda e, s=s: e.scalar_tensor_tensor(
                    out=xn[:, s, :], in0=xt[:, s, :], scalar=1.0, in1=xt[:, s, :], op0=ALU.mult, op1=ALU.mult,
                    accum_out=ss[:, s:s + 1]), [dx, junk_deps]))
        t1 = DVE.add(lambda e: e.tensor_scalar(out=ss, in0=ss, scalar1=1.0 / D, scalar2=1e-6,
                                               op0=ALU.mult, op1=ALU.add), tks)
        t2 = ACT.add(lambda e: e.activation(out=ss, in_=ss, func=AF.Ln), [t1])
        t3 = ACT.add(lambda e: e.activation(out=rs, in_=ss, func=AF.Exp, scale=-0.5), [t2])
        ck()
        return rs, t3

    def rmsnorm_hT(ns, gcol, deps_x, deps_hT_free):
        rs, t3 = rms_rstd(ns, deps_x, G("xn_free"))
        xtk = []
        for s in range(ns):
            xtk.append(ev_scale(evac_eng(), xn[:, s, :], xt[:, s, :], rs[:, s:s + 1], [t3]))
        ck()
        out = []
        tp = None
        for kc in range(8):
            i, view, tp = tr_group([xn[:, s, kc * 128:(kc + 1) * 128] for s in range(ns)], [xtk])
            tk = ev_scale(evac_eng(), hT[:, kc, 0:ns * 128], view[:, 0:ns * 128], gcol[:, kc:kc + 1],
                          [tp, t_const, deps_hT_free])
            slot_free[i] = tk
            out.append(tk)
            ck()
        state["xn_free"] = tp
        ck()
        return out

    def mm_group(lhs_fn, rhs_fn, nk, ncol, deps, m=128, kdeps=None):
        i = get_slot()
        tk = None
        for k in range(nk):
            dk = [deps, slot_free[i]] if k == 0 else []
            if kdeps is not None:
                dk = dk + [kdeps[k]]
            tk = PE.add(lambda e, k=k, i=i: e.matmul(slot_aps[i][0:m, 0:ncol], lhsT=lhs_fn(k), rhs=rhs_fn(k),
                                                     start=(k == 0), stop=(k == nk - 1)),
                        dk, sig=(k == nk - 1))
        return i, tk

    ks_ds = [dsem(), dsem()]
    ks_free = [None, None]
    ks_i = [0]
    kst_ds = [dsem(), dsem()]
    xl_ds = [dsem() for _ in range(4)]
    xs_ds = [dsem() for _ in range(4)]
    ptfree = [None, None, None]
    accfree = [None] * 8
    misc_ds = dsem()
    mem_ds = dsem()
    ps_ds = [dsem() for _ in range(NBS)]
    kc_ds = dsem()
    xt_store = [None] * 4

    def mem_kv_prompt(l):
        d_ = dsem()
        tl = SP.dma(d_, xt[:, 0:2, :], mem_prompt.rearrange("(s p) c -> p s c", p=128), deps=[G("xt_free")])
        hts = rmsnorm_hT(2, gcols[:, l, 2, :], [tl], G("hT_free"))
        last_pe = None
        for which in range(2):
            for h in range(2):
                b_, tsl = load_slab(l, 8 + which * 2 + h)
                slab = slabs[b_]
                ck()
                for s in range(2):
                    i, tk = mm_group(lambda k, s=s: hT[:, k, s * 128:(s + 1) * 128],
                                     lambda k, slab=slab: slab[:, k, :], 8, 512, [tsl], kdeps=hts)
                    si = get_stage()
                    te = ev_copy(evac_eng(), stage[si][:], slot_aps[i][:, :], [tk, stage_free[si]])
                    dst = (mem_k_prompt if which == 0 else mem_v_prompt)[l, s * 128:(s + 1) * 128, h * 512:(h + 1) * 512]
                    store_stage(si, dst, stage[si][:], te)
                    ck()
                    if which == 1:
                        te2 = ev_copy(DVE, mv[:, s, h * 512:(h + 1) * 512], slot_aps[i][:, :], [tk, te])
                        slot_free[i] = [te, te2]
                    else:
                        slot_free[i] = te
                    last_pe = tk
                    ck()
                if which == 0:
                    for cc in range(4):
                        i, tk = mm_group(lambda k, cc=cc, slab=slab: slab[:, k, cc * 128:(cc + 1) * 128],
                                         lambda k: hT[:, k, 0:256], 8, 256, [tsl], kdeps=hts)
                        te = ev_copy(evac_eng(), mkT[:, h * 4 + cc, :], slot_aps[i][:, 0:256], [tk])
                        slot_free[i] = te
                        last_pe = tk
                        ck()
                slab_free[b_] = last_pe
        state["hT_free"] = last_pe
        state["xt_free"] = last_pe
        return last_pe

    def cross_attn(l, c0, ncol, dep_in):
        last = None
        for hx in range(4):
            pb = hx % 2
            ptok = []
            for mb in range(2):
                i = get_slot()
                tk = None
                for half in range(2):
                    tk = PE.add(lambda e, i=i, half=half, mb=mb, hx=hx: e.matmul(
                        slot_aps[i][:, 0:ncol], lhsT=mkT[:, hx * 2 + half, mb * 128:(mb + 1) * 128],
                        rhs=qxT[:, hx * 2 + half, c0:c0 + ncol], start=(half == 0), stop=(half == 1)),
                        [dep_in, slot_free[i]] if half == 0 else (), sig=(half == 1))
                te = ACT.add(lambda e, i=i, mb=mb, pb=pb: e.activation(
                    out=PxT[pb][:, mb, 0:ncol], in_=slot_aps[i][:, 0:ncol], func=AF.Exp, scale=1.0 / 16.0),
                    [tk, G("px_free%d" % pb)])
                slot_free[i] = te
                ptok.append(te)
            i = get_slot()
            tk = None
            for mb in range(2):
                tk = PE.add(lambda e, i=i, mb=mb, pb=pb: e.matmul(
                    slot_aps[i][:, 0:ncol], lhsT=ones_bf[:, :], rhs=PxT[pb][:, mb, 0:ncol],
                    start=(mb == 0), stop=(mb == 1)),
                    [ptok, slot_free[i], t_ones] if mb == 0 else (), sig=(mb == 1))
            tr = DVE.add(lambda e, i=i, pb=pb: e.reciprocal(out=rrec[pb][:, 0:ncol], in_=slot_aps[i][:, 0:ncol]),
                         [tk, G("rr_free%d" % pb)])
            slot_free[i] = tr
            tms = []
            for half in range(2):
                i = get_slot()
                tk = None
                for mb in range(2):
                    tk = PE.add(lambda e, i=i, mb=mb, pb=pb, hx=hx, half=half: e.matmul(
                        slot_aps[i][:, 0:ncol], lhsT=mv[:, mb, hx * 256 + half * 128: hx * 256 + (half + 1) * 128],
                        rhs=PxT[pb][:, mb, 0:ncol], start=(mb == 0), stop=(mb == 1)),
                        [ptok, slot_free[i]] if mb == 0 else (), sig=(mb == 1))
                tm = DVE.add(lambda e, i=i, pb=pb, hx=hx, half=half: e.tensor_tensor(
                    out=oxT[:, hx * 2 + half, c0:c0 + ncol], in0=slot_aps[i][:, 0:ncol],
                    in1=rrec[pb][:, 0:ncol], op=ALU.mult), [tk, tr, G("ox_free")])
                slot_free[i] = tm
                tms.append(tm)
                last = tk
            state["px_free%d" % pb] = last
            state["rr_free%d" % pb] = tms
            state["ox_toks"] = state.get("ox_toks", []) + tms
        return last

    def subln(l, j, tev, r0, rows, items):
        ob = o1 if j == 0 else o1b
        n = len(items)
        ssq = stat(n)
        R = slice(r0, r0 + rows)
        tks = []
        for k, (a_, h, sub) in enumerate(items):
            tks.append(DVE.add(lambda e, a_=a_, k=k: e.scalar_tensor_tensor(
                out=osq[R, :], in0=ob[R, a_, :], scalar=1.0, in1=ob[R, a_, :], op0=ALU.mult, op1=ALU.mult,
                accum_out=ssq[R, k:k + 1]), [tev]))
        t1 = DVE.add(lambda e: e.tensor_scalar(out=ssq[R, :], in0=ssq[R, :], scalar1=1.0 / 128, scalar2=1e-5,
                                               op0=ALU.mult, op1=ALU.add), tks)
        t2 = ACT.add(lambda e: e.activation(out=ssq[R, :], in_=ssq[R, :], func=AF.Ln), [t1])
        t3 = ACT.add(lambda e: e.activation(out=ssq[R, :], in_=ssq[R, :], func=AF.Exp, scale=-0.5), [t2])
        outs = []
        for k, (a_, h, sub) in enumerate(items):
            outs.append(DVE.add(lambda e, a_=a_, h=h, sub=sub, k=k: e.tensor_scalar(
                out=a_tok[R, sub, h * 128:(h + 1) * 128], in0=ob[R, a_, :],
                scalar1=ssq[R, k:k + 1], scalar2=None, op0=ALU.mult),
                [t3, lam_tok[l], G("aT_free")]))
        state["o1_free"] = outs
        return outs

    def evac_pass(l, m, j, tlast, r0, rows, accs):
        ob = o1 if j == 0 else o1b
        rc = stat(8)
        R = slice(r0, r0 + rows)
        tev = []
        for a_ in accs:
            tr = DVE.add(lambda e, a_=a_: e.reciprocal(out=rc[R, a_:a_ + 1],
                                                       in_=acc_ap(a_, r0, rows)[:, 128:129]), [tlast])
            if m == 0:
                te = DVE.add(lambda e, a_=a_: e.tensor_scalar(
                    out=ob[R, a_, :], in0=acc_ap(a_, r0, rows)[:, 0:128], scalar1=rc[R, a_:a_ + 1],
                    scalar2=None, op0=ALU.mult), [tr, G("o1_free")])
            else:
                tn = DVE.add(lambda e, a_=a_: e.tensor_scalar(out=rc[R, a_:a_ + 1], in0=rc[R, a_:a_ + 1],
                                                              scalar1=lam_t[R, l, 5:6], scalar2=None, op0=ALU.mult),
                             [tr, lam_tok[l]])
                te = DVE.add(lambda e, a_=a_: e.scalar_tensor_tensor(
                    out=ob[R, a_, :], in0=acc_ap(a_, r0, rows)[:, 0:128], scalar=rc[R, a_:a_ + 1],
                    in1=ob[R, a_, :], op0=ALU.mult, op1=ALU.add), [tn, state["o1_tok"][j]])
            tev.append(te)
        for a_ in range(8):
            accfree[a_] = tev
        return tev

    def subln_fast(l, j, tev):
        ob = o1 if j == 0 else o1b
        ssq = stat(8)
        t0 = DVE.add(lambda e: e.tensor_tensor(out=otmp[:, :, :], in0=ob[:, :, :], in1=ob[:, :, :], op=ALU.mult),
                     [tev, G("otmp_free")])
        t0b = DVE.add(lambda e: e.tensor_reduce(out=ssq, in_=otmp[:, :, :], axis=mybir.AxisListType.X, op=ALU.add), [t0])
        t1 = DVE.add(lambda e: e.tensor_scalar(out=ssq, in0=ssq, scalar1=1.0 / 128, scalar2=1e-5,
                                               op0=ALU.mult, op1=ALU.add), [t0b])
        t2 = ACT.add(lambda e: e.activation(out=ssq, in_=ssq, func=AF.Ln), [t1])
        t3 = ACT.add(lambda e: e.activation(out=ssq, in_=ssq, func=AF.Exp, scale=-0.5), [t2])
        dst = a_tok[:, :, 2 * j * 128:(2 * j + 2) * 128].rearrange("p q (h d) -> p h q d", h=2)
        t5 = DVE.add(lambda e: e.tensor_tensor(
            out=dst, in0=ob[:, :, :].rearrange("p (h q) d -> p h q d", h=2),
            in1=ssq.rearrange("p (h q) -> p h q", h=2).unsqueeze(3).to_broadcast([128, 2, 4, 128]), op=ALU.mult),
            [t3, lam_tok[l], G("aT_free")])
        state["o1_free"] = [t5]
        state["otmp_free"] = t5
        return [t5]

    def evac_pass_fast(l, m, j, tlast, tpool_done):
        ob = o1 if j == 0 else o1b
        rc = stat(8)
        gate = []
        tmul = []
        for b_ in range(3):
            na = 3 if b_ < 2 else 2
            bank = psO[:, b_, 0:480].rearrange("p (a c) -> p a c", c=160)
            rcb = rc[:, 3 * b_:3 * b_ + na]
            tr = DVE.add(lambda e, bank=bank, rcb=rcb, na=na: e.reciprocal(out=rcb, in_=bank[:, 0:na, 128]), [tlast])
            dst = (ob if m == 0 else otmp)[:, 3 * b_:3 * b_ + na, :]
            te = DVE.add(lambda e, bank=bank, rcb=rcb, na=na, dst=dst: e.tensor_tensor(
                out=dst, in0=bank[:, 0:na, 0:128], in1=rcb.unsqueeze(2).to_broadcast([128, na, 128]),
                op=ALU.mult), [tr, G("o1_free") if m == 0 else tpool_done, G("otmp_free")])
            gate.append(te)
            tmul.append(te)
        for a_ in range(8):
            accfree[a_] = gate[a_ // 3]
        if m == 0:
            return tmul
        tadd = DVE.add(lambda e: e.scalar_tensor_tensor(out=ob[:, :, :], in0=otmp[:, :, :], scalar=lam_t[:, l, 5:6],
                                                         in1=ob[:, :, :], op0=ALU.mult, op1=ALU.add),
                       [tmul, state["o1_tok"][j], lam_tok[l]])
        state["otmp_free"] = tadd
        return [tadd]

    def attn_prompt(l, t, tq, tkT, tv, tpool_done):
        nprev = 4 * t
        rf_attn = G("region_free")
        deferred = []
        ta_all = []
        tlast = None
        for c in range(4):
            m, j = divmod(c, 2)
            chunks = [(i * 16, min((i + 1) * 16, nprev)) for i in range((nprev + 15) // 16)]
            cbuf = []

            def issue_chunk(ci):
                b0, b1 = chunks[ci]
                bi = ks_i[0]
                ks_i[0] ^= 1
                tkl = SP.dma(ks_ds[bi], kstream[bi][:, 0:(b1 - b0) * 128], ktscr[l, c, :, b0 * 128:b1 * 128],
                             deps=[ks_free[bi], state["kst"][t - 1], rf_attn])
                cbuf.append((bi, tkl))

            if chunks:
                issue_chunk(0)
            nkb = nprev + 4
            pend = {}

            def emit_qk(kb):
                dg = kb - nprev
                q0 = max(0, dg) * 128
                sp_i = kb % 2
                spair = psA if sp_i == 0 else psB
                pti = kb % 3
                tk = None
                if kb < nprev and kb % 16 == 0 and kb // 16 + 1 < len(chunks):
                    issue_chunk(kb // 16 + 1)
                for hl in range(2):
                    if kb < nprev:
                        bi, tkl = cbuf[kb // 16]
                        lo = (kb % 16) * 128
                        ksrc = kstream[bi][hl * 64:(hl + 1) * 64, lo:lo + 128]
                        kdep = tkl
                    else:
                        ksrc = kTcur[hl * 64:(hl + 1) * 64, c, dg * 128:(dg + 1) * 128]
                        kdep = tkT[c]
                    qsrc = qT[hl * 64:(hl + 1) * 64, c, q0:512]
                    tk = PE.add(lambda e, hl=hl, ksrc=ksrc, q0=q0, spair=spair, qsrc=qsrc: e.matmul(
                        spair[:, hl, q0:512], lhsT=ksrc, rhs=qsrc,
                        start=True, stop=True),
                        [kdep, tq[c], slot_free[sp_i * 2], slot_free[sp_i * 2 + 1]] if hl == 0 else (),
                        sig=(hl == 1))
                if kb < nprev and (kb % 16 == 15 or kb == nprev - 1):
                    ks_free[cbuf[kb // 16][0]] = tk
                te = ACT.add(lambda e, spair=spair, pti=pti, q0=q0: e.activation(
                    out=PT[pti][:, :, q0:512], in_=spair[:, :, q0:512], func=AF.Exp, scale=0.125),
                    [tk, ptfree[pti]])
                slot_free[sp_i * 2] = te
                slot_free[sp_i * 2 + 1] = te
                if dg >= 0:
                    te = POOL.add(lambda e, pti=pti, q0=q0: e.memset(PT[pti][64:128, :, q0:q0 + 64], 0.0), [te])
                pend[kb] = te

            def emit_pv(kb):
                dg = kb - nprev
                q0 = max(0, dg) * 128
                pti = kb % 3
                te = pend.pop(kb)
                tk2 = None
                for hl in range(2):
                    h = 2 * j + hl
                    for qs in range(q0 // 128, 4):
                        a_ = hl * 4 + qs
                        first = (kb == 0)
                        lastq = (kb == nprev + qs)
                        dps = [te, tv] if (hl == 0 and qs == q0 // 128) else []
                        if first:
                            dps = dps + [accfree[a_]]
                        stf = first and (a_ % 3 == 0)
                        tk2 = PE.add(lambda e, a_=a_, hl=hl, qs=qs, h=h, stf=stf, lastq=lastq: e.matmul(
                            acc_ap(a_), lhsT=PT[pti][:, hl, qs * 128:(qs + 1) * 128], rhs=Vres[:, kb, h, :],
                            start=stf, stop=lastq, skip_group_check=True), dps, sig=(hl == 1 and qs == 3))
                ptfree[pti] = tk2
                return tk2

            tk2 = None
            for step in range(nkb + 1):
                if step < nkb:
                    emit_qk(step)
                if step >= 1:
                    tk2 = emit_pv(step - 1)
                if step == 3 and deferred:
                    ta_all.append(deferred.pop()())
            tlast = tk2
            tev = evac_pass_fast(l, m, j, tlast, tpool_done)
            if m == 0:
                state["o1_tok"][j] = tev
            elif c == 2:
                deferred.append(lambda j=j, tev=tev: subln_fast(l, j, tev))
            else:
                ta_all.append(subln_fast(l, j, tev))
        state["vres_free"] = tlast
        return ta_all

    def issue_k_load(l, bq, deps):
        return POOL.dma(kc_ds, kc_tok[:, :, :], cache_k[l, bq].rearrange("(k p) c -> p k c", p=128), deps=deps)

    def issue_v_load(l, bq, deps):
        t2 = None
        for hh in range(4):
            t2 = POOL.dma(misc_ds, V_s[:, 0:NCB, hh, 0:128],
                          cache_v[l, bq][:, hh, :].rearrange("(k p) d -> p k d", p=128), deps=deps)
        t3 = POOL.add(lambda e: e.memset(V_s[:, :, :, 128:129], 1.0), deps)
        return [t2, t3]

    def attn_sample(l, tq, tkT, tv):
        ta_all = []
        pre = state.pop("samp_pre")
        tk_load, tv_load = pre
        for bq in range(NBS):
            r0 = (bq % 2) * 64
            prev = [G("samp_free"), G("vres_free")]
            tc = [tk_load, tv_load]
            tkt = []
            for c in range(4):
                for k0 in range(0, NCB, 4):
                    nn = min(4, NCB - k0)
                    i, view, tp = tr_group([kc_tok[:, k0 + kk, c * 128:(c + 1) * 128] for kk in range(nn)], [tk_load])
                    te = ev_copy(DVE, kT_s[:, c, k0 * 128:(k0 + nn) * 128], view[:, 0:nn * 128], [tp, prev])
                    slot_free[i] = te
                    tkt.append(te)
            tvn = DVE.add(lambda e, bq=bq: e.tensor_copy(out=V_s[0:64, NCB, :, 0:128], in_=vnew[bq][0:64, :, 0:128]),
                          [tv, tv_load])
            if bq + 1 < NBS:
                tk_load = issue_k_load(l, bq + 1, [tp])
            lastpe = None
            for c in range(4):
                m, j = divmod(c, 2)
                pend = {}
                groups = [list(range(g0, min(g0 + 8, NCB))) for g0 in range(0, NCB, 8)] + [[NCB]]

                def s_qk(gi):
                    kbs = groups[gi]
                    nk = 128 if kbs[0] < NCB else 64
                    sp_i = gi % 2
                    spair = psA if sp_i == 0 else psB
                    pti = gi % 3
                    tk = None
                    n = len(kbs)
                    for ki, kb in enumerate(kbs):
                        for hl in range(2):
                            if kb < NCB:
                                ksrc = kT_s[hl * 64:(hl + 1) * 64, c, kb * 128:(kb + 1) * 128]
                            else:
                                ksrc = kTcur[hl * 64:(hl + 1) * 64, c, bq * 64:(bq + 1) * 64]
                            qsrc = qT[hl * 64:(hl + 1) * 64, c, bq * 64:(bq + 1) * 64]
                            osl = spair[0:nk, hl, ki * 64:(ki + 1) * 64]
                            first = (ki == 0 and hl == 0)
                            tk = PE.add(lambda e, ksrc=ksrc, qsrc=qsrc, osl=osl: e.matmul(
                                osl, lhsT=ksrc, rhs=qsrc, start=True, stop=True, skip_group_check=True),
                                [tkt, tkT[c], tq[c], slot_free[sp_i * 2], slot_free[sp_i * 2 + 1]] if first else (),
                                sig=(ki == n - 1 and hl == 1))
                    src_ap = spair[0:nk, :, 0:n * 64]
                    dst_ap = PT[pti][0:nk, :, 0:n * 64]
                    te = ACT.add(lambda e, src_ap=src_ap, dst_ap=dst_ap: e.activation(
                        out=dst_ap, in_=src_ap, func=AF.Exp, scale=0.125), [tk, ptfree[pti]])
                    slot_free[sp_i * 2] = te
                    slot_free[sp_i * 2 + 1] = te
                    pend[gi] = te

                def s_pv(gi):
                    kbs = groups[gi]
                    nk = 128 if kbs[0] < NCB else 64
                    pti = gi % 3
                    te = pend.pop(gi)
                    tk2 = None
                    n = len(kbs)
                    for ki, kb in enumerate(kbs):
                        for hl in range(2):
                            h = 2 * j + hl
                            a_ = hl * 3
                            first = (kb == 0)
                            dps = [te, tvn] if (ki == 0 and hl == 0) else []
                            if first:
                                dps = dps + [accfree[a_]]
                            oacc = acc_ap(a_, r0, 64)
                            lsrc = PT[pti][0:nk, hl, ki * 64:(ki + 1) * 64]
                            rsrc = V_s[0:nk, kb, h, :]
                            tk2 = PE.add(lambda e, oacc=oacc, lsrc=lsrc, rsrc=rsrc, first=first, kb=kb: e.matmul(
                                oacc, lhsT=lsrc, rhs=rsrc, start=first, stop=(kb == NCB), skip_group_check=True),
                                dps, sig=(ki == n - 1 and hl == 1))
                    ptfree[pti] = tk2
                    return tk2

                tk2 = None
                for step in range(len(groups) + 1):
                    if step < len(groups):
                        s_qk(step)
                    if step >= 1:
                        tk2 = s_pv(step - 1)
                lastpe = tk2
                tev = evac_pass(l, m, j, lastpe, r0, 64, [0, 3])
                if m == 0:
                    state["o1_tok"][j] = tev
                else:
                    ta_all.append(subln(l, j, tev, r0, 64, [(hl * 3, 2 * j + hl, bq // 2) for hl in range(2)]))
            state["samp_free"] = [lastpe, ta_all[-2:]]
            if bq + 1 < NBS:
                tv_load = issue_v_load(l, bq + 1, [lastpe])
        return ta_all

    def load_mem_sample(l, bq):
        prev = [G("mem_free")]
        t1 = POOL.dma(mem_ds, mktok[:, :, :], cache_mem_k[l, bq].rearrange("(s p) c -> p s c", p=128),
                      deps=[prev, G("xn_free")])
        t2 = POOL.dma(mem_ds, mv[:, :, :], cache_mem_v[l, bq].rearrange("(s p) c -> p s c", p=128), deps=[prev])
        toks = []
        tp = None
        for cc in range(8):
            i, view, tp = tr_group([mktok[:, s, cc * 128:(cc + 1) * 128] for s in range(2)], [t2, prev])
            te = ev_copy(evac_eng(), mkT[:, cc, :], view[:, 0:256], [tp])
            slot_free[i] = te
            toks.append(te)
        state["xn_free"] = tp
        return toks + [t2]

    def load_pool_state(l):
        uview = uext[:, :, 0:4 * 80].rearrange("p g (b c) -> p g b c", b=4)
        wdeps = [G("uext_free"), G("region_free")]
        toks = []
        for bq in range(NBS):
            si = get_stage()
            t1 = POOL.dma(ps_ds[bq], stage[si][0:15, :], state_pool[l, bq], deps=[stage_free[si]])
            tp = None
            for g in range(4):
                i, view, tp = tr_group([stage[si][0:15, g * 128:(g + 1) * 128]], [t1], f32=True)
                te = DVE.add(lambda e, g=g, bq=bq, view=view: e.tensor_copy(out=uview[:, g, bq, 1:16], in_=view[:, 0:15]),
                             [tp, wdeps])
                slot_free[i] = te
                toks.append(te)
            stage_free[si] = tp
        tz = POOL.add(lambda e: e.memset(uview[:, :, :, 0:1], 0.0), wdeps)
        return toks + [tz]

    def run_tile(l, t):
        is_s = (t == NTP)
        ns = 2 if is_s else 4
        ntok = ns * 128
        row0 = S if is_s else t * 512
        orow = 0 if is_s else row0
        groups = [(b_ * 64, 64) for b_ in range(4)] if is_s else [(s * 128, 128) for s in range(4)]
        lb = 64 if is_s else 512
        nb = 4 if is_s else 1
        uview = uext[:, :, 0:nb * (16 + lb)].rearrange("p g (b c) -> p g b c", b=nb)
        if l == 0:
            src_x = (x_sample if is_s else x_prompt[row0:row0 + ntok, :])
        else:
            src_x = xscr[row0:row0 + ntok, :]
        xf = G("xt_free")
        tl = []
        for s in range(ns):
            dps = [xf[s] if isinstance(xf, list) and s < len(xf) else xf, G("xs_all") if l > 0 else None]
            tl.append(SP.dma(xl_ds[s], xt[:, s, :], src_x[s * 128:(s + 1) * 128, :], deps=dps))
        st["per_s"] = True
        hts = rmsnorm_hT(ns, gcols[:, l, 0, :], tl, [G("hT_free")])
        st["per_s"] = False
        if l + 1 < DEPTH:
            per = -(-NSLAB // max(NTP - 1, 1))
            drip_conv(l + 1, NSLAB if (is_s or t == NTP - 1) else per)
        rf = G("region_free")
        if is_s:
            d0 = [G("samp_free"), G("vres_free")]
            state["samp_pre"] = (issue_k_load(l, 0, d0), issue_v_load(l, 0, d0))
        if not is_s:
            state["hist_tok"] = POOL.add(lambda e: e.tensor_copy(out=uext[:, :, 0:16], in_=hist_keep[:, :, :]),
                                         [rf, G("hk_tok"), G("uext_free")])
        b_, tsl = load_slab(l, 0)
        slab = slabs[b_]
        tq = []
        lastpe = None
        for cc in range(4):
            i, tk = mm_group(lambda k, cc=cc, slab=slab: slab[:, k, cc * 128:(cc + 1) * 128],
                             lambda k: hT[:, k, 0:ntok], 8, ntok, [tsl], kdeps=hts)
            te = ev_copy(evac_eng(), qT[:, cc, 0:ntok], slot_aps[i][:, 0:ntok], [tk, rf])
            slot_free[i] = te
            tq.append(te)
            lastpe = tk
        slab_free[b_] = lastpe
        b_, tsl = load_slab(l, 1)
        slab = slabs[b_]
        tkT = []
        for cc in range(4):
            i, tk = mm_group(lambda k, cc=cc, slab=slab: slab[:, k, cc * 128:(cc + 1) * 128],
                             lambda k: hT[:, k, 0:ntok], 8, ntok, [tsl], kdeps=hts)
            te = ev_copy(evac_eng(), kTcur[:, cc, 0:ntok], slot_aps[i][:, 0:ntok], [tk, rf, G("kst_last")])
            slot_free[i] = te
            tkT.append(te)
        if not is_s:
            tks_ = POOL.dma(kst_ds[t % 2], ktscr[l, :, :, t * 512:(t + 1) * 512].rearrange("c p k -> p c k"),
                            kTcur[:, :, :], deps=[tkT])
            state["kst"][t] = tks_
            state["kst_last"] = tks_
            out_toks.append(tks_)
        for gi, (g0, gn) in enumerate(groups):
            i, tk = mm_group(lambda k, g0=g0, gn=gn: hT[:, k, g0:g0 + gn],
                             lambda k, slab=slab: slab[:, k, :], 8, 512, [tsl], m=gn, kdeps=hts)
            si = get_stage()
            te = ev_copy(evac_eng(), stage[si][0:gn, :], slot_aps[i][0:gn, :], [tk, stage_free[si]])
            slot_free[i] = te
            dst = (k_sample if is_s else k_prompt)[l, orow + g0:orow + g0 + gn, :]
            store_stage(si, dst, stage[si][0:gn, :], te)
            lastpe = tk
        slab_free[b_] = lastpe
        b_, tsl = load_slab(l, 2)
        slab = slabs[b_]
        tv = []
        for gi, (g0, gn) in enumerate(groups):
            i, tk = mm_group(lambda k, g0=g0, gn=gn: hT[:, k, g0:g0 + gn],
                             lambda k, slab=slab: slab[:, k, :], 8, 512, [tsl], m=gn, kdeps=hts)
            si = get_stage()
            te = ev_copy(ACT, stage[si][0:gn, :], slot_aps[i][0:gn, :], [tk, stage_free[si]])
            dst = (v_sample if is_s else v_prompt)[l, orow + g0:orow + g0 + gn, :]
            store_stage(si, dst, stage[si][0:gn, :], te)
            vdst = vnew[gi][0:gn, :, 0:128] if is_s else Vres[:, t * 4 + gi, :, 0:128]
            te2 = DVE.add(lambda e, i=i, gn=gn, vdst=vdst: e.tensor_copy(
                out=vdst, in_=slot_aps[i][0:gn, :].rearrange("p (h d) -> p h d", h=4)),
                [tk, te, G("vres_wr"), rf])
            slot_free[i] = [te, te2]
            tv.append(te2)
            lastpe = tk
        slab_free[b_] = lastpe
        b_, tsl = load_slab(l, 3)
        slab = slabs[b_]
        tu = []
        for g in range(4):
            i, tk = mm_group(lambda k, g=g, slab=slab: slab[:, k, g * 128:(g + 1) * 128],
                             lambda k: hT[:, k, 0:ntok], 8, ntok, [tsl], kdeps=hts)
            te = ev_copy(evac_eng(), uview[:, g, :, 16:16 + lb],
                         slot_aps[i][:, 0:ntok].rearrange("p (b c) -> p b c", b=nb),
                         [tk, G("uext_free"), G("hist_tok"), rf])
            slot_free[i] = te
            tu.append(te)
            lastpe = tk
        if is_s or t == NTP - 1:
            for gi, (g0, gn) in enumerate(groups):
                if (not is_s) and gi != 3:
                    continue
                i, tk = mm_group(lambda k, g0=g0, gn=gn: hT[:, k, g0:g0 + gn],
                                 lambda k, slab=slab: slab[:, k, :], 8, 512, [tsl], m=gn, kdeps=hts)
                si = get_stage()
                te = ev_copy(evac_eng(), stage[si][0:gn, :], slot_aps[i][0:gn, :], [tk, stage_free[si]])
                slot_free[i] = te
                dst = pool_sample[l, gi] if is_s else pool_prompt[l]
                store_stage(si, dst, stage[si][gn - 15:gn, :], te)
                lastpe = tk
        slab_free[b_] = lastpe
        state["hT_free"] = lastpe
        first_stream = (not is_s) and t == 0
        tpool = []
        tpy = []
        for g in range(4):
            w = 2 << g
            cur = uview[:, g, :, :]
            lo = 1
            tk = [tu[g]]
            for lev in range(g + 1):
                sh = 1 << lev
                nlo = lo + sh
                dstb = utmp[lev % 2][:, 0:nb * (16 + lb)].rearrange("p (b c) -> p b c", b=nb)
                tk = POOL.add(lambda e, dstb=dstb, cur=cur, nlo=nlo, sh=sh: e.tensor_tensor(
                    out=dstb[:, :, nlo:16 + lb], in0=cur[:, :, nlo:16 + lb],
                    in1=cur[:, :, nlo - sh:16 + lb - sh], op=ALU.add), [tk, G("utmp_free")])
                cur = dstb
                lo = nlo
            pb2 = pooled2[g]
            pvw = pb2[:, 0:ntok].rearrange("p (b c) -> p b c", b=nb)
            tk2 = DVE.add(lambda e, pvw=pvw, cur=cur, g=g, w=w: e.scalar_tensor_tensor(
                out=pvw, in0=uview[:, g, :, 16:16 + lb], scalar=-float(w), in1=cur[:, :, 16:16 + lb],
                op0=ALU.mult, op1=ALU.add), [tk, G("pooled_free")])
            if first_stream:
                tk3 = POOL.add(lambda e, cur=cur, g=g: e.tensor_tensor(
                    out=cur[:, 0, 16:32], in0=cur[:, 0, 16:32], in1=icnt[:, g, :], op=ALU.mult), [tk2, ic_toks])
                tk2 = DVE.add(lambda e, cur=cur, g=g, pb2=pb2, w=w: e.scalar_tensor_tensor(
                    out=pb2[:, 0:16], in0=uext[:, g, 16:32], scalar=-float(w), in1=cur[:, 0, 16:32],
                    op0=ALU.mult, op1=ALU.add), [tk3])
            state["utmp_free"] = tk2
            tpool.append(tk2)
        if not is_s:
            state["hk_tok"] = POOL.add(lambda e: e.tensor_copy(out=hist_keep[:, :, :], in_=uext[:, :, 512:528]), [tpool])
            state["uext_free"] = [state["hk_tok"]] + tpool
        else:
            state["uext_free"] = tpool
        if is_s:
            ta = attn_sample(l, tq, tkT, tv)
        else:
            ta = attn_prompt(l, t, tq, tkT, tv, tpool)
        for g in range(4):
            i = get_slot()
            tk = PE.add(lambda e, i=i, g=g: e.matmul(slot_aps[i][:, 0:ntok], lhsT=wpool_bf[:, l, g, :],
                                                     rhs=pooled2[g][:, 0:ntok], start=True, stop=True),
                        [tpool[g], slot_free[i], t_wp])
            state["pooled_free"] = tk
            te = ev_scale(evac_eng(), pyT[:, g, 0:ntok], slot_aps[i][:, 0:ntok], pscol[:, l, g:g + 1],
                          [tk, t_const, G("pyT_free")])
            slot_free[i] = te
            tpy.append(te)
        taT = []
        tp = None
        for cc in range(4):
            i, view, tp = tr_group([a_tok[:, s, cc * 128:(cc + 1) * 128] for s in range(ns)], [ta])
            tk = ev_scale(evac_eng(), aT[:, cc, 0:ntok], view[:, 0:ntok], sgcol[:, l:l + 1], [tp, G("aT_free"), lam_tok[l]])
            slot_free[i] = tk
            taT.append(tk)
        txo = []
        for h in range(2):
            b_, tsl = load_slab(l, 4 + h)
            slab = slabs[b_]
            for s in range(ns):
                i, tk = mm_group(lambda k, s=s: (aT if k < 4 else pyT)[:, k % 4, s * 128:(s + 1) * 128],
                                 lambda k, slab=slab: slab[:, k, :], 8, 512, [taT, tpy, tsl])
                te = DVE.add(lambda e, i=i, s=s, h=h: e.tensor_tensor(
                    out=xt[:, s, h * 512:(h + 1) * 512], in0=slot_aps[i][:, :],
                    in1=xt[:, s, h * 512:(h + 1) * 512], op=ALU.add), [tk])
                slot_free[i] = te
                txo.append((s, te))
                lastpe = tk
            slab_free[b_] = lastpe
        state["aT_free"] = lastpe
        state["pyT_free"] = lastpe
        regA_done = [lastpe, ta, tpool, tp, G("kst_last")]
        st["per_s"] = True
        hts = rmsnorm_hT(ns, gcols[:, l, 1, :], [[te for (s2, te) in txo if s2 == s] for s in range(ns)], [G("hT_free")])
        st["per_s"] = False
        tqx = []
        for h in range(2):
            b_, tsl = load_slab(l, 6 + h)
            slab = slabs[b_]
            for cc in range(4):
                i, tk = mm_group(lambda k, cc=cc, slab=slab: slab[:, k, cc * 128:(cc + 1) * 128],
                                 lambda k: hT[:, k, 0:ntok], 8, ntok, [tsl], kdeps=hts)
                te = ev_copy(evac_eng(), qxT[:, h * 4 + cc, 0:ntok], slot_aps[i][:, 0:ntok], [tk, regA_done])
                slot_free[i] = te
                tqx.append(te)
                lastpe = tk
            slab_free[b_] = lastpe
        state["hT_free"] = lastpe
        state["ox_toks"] = []
        state["ox_free"] = regA_done
        if is_s:
            for bq in range(NBS):
                tmk = load_mem_sample(l, bq)
                lastpe = cross_attn(l, bq * 64, 64, [tqx, tmk])
                state["mem_free"] = lastpe
        else:
            lastpe = cross_attn(l, 0, 512, [tqx, G("memkv_tok")])
            state["mem_free"] = lastpe
        tox = state["ox_toks"]
        txo = []
        for h in range(2):
            b_, tsl = load_slab(l, 12 + h)
            slab = slabs[b_]
            for s in range(ns):
                i, tk = mm_group(lambda k, s=s: oxT[:, k, s * 128:(s + 1) * 128],
                                 lambda k, slab=slab: slab[:, k, :], 8, 512, [tsl, tox])
                te = DVE.add(lambda e, i=i, s=s, h=h: e.tensor_tensor(
                    out=xt[:, s, h * 512:(h + 1) * 512], in0=slot_aps[i][:, :],
                    in1=xt[:, s, h * 512:(h + 1) * 512], op=ALU.add), [tk])
                slot_free[i] = te
                txo.append((s, te))
                lastpe = tk
            slab_free[b_] = lastpe
        st["per_s"] = True
        hts = rmsnorm_hT(ns, gcols[:, l, 3, :], [[te for (s2, te) in txo if s2 == s] for s in range(ns)], [G("hT_free")])
        st["per_s"] = False
        txm = []
        for qd in range(4):
            hb = qd % 2
            thid = []
            for h in range(2):
                b_, tsl = load_slab(l, 14 + qd * 2 + h)
                slab = slabs[b_]
                for cc in range(4):
                    i, tk = mm_group(lambda k, cc=cc, slab=slab: slab[:, k, cc * 128:(cc + 1) * 128],
                                     lambda k: hT[:, k, 0:ntok], 8, ntok, [tsl], kdeps=hts)
                    rb = (h * 4 + cc) % 2
                    te = ACT.add(lambda e, i=i, rb=rb: e.activation(out=rtmp[rb][:, 0:ntok], in_=slot_aps[i][:, 0:ntok],
                                                                  func=AF.Relu), [tk, G("rt_free%d" % rb), regA_done])
                    slot_free[i] = te
                    tsq = POOL.add(lambda e, rb=rb, hb=hb, h=h, cc=cc: e.tensor_tensor(
                        out=hid[hb][:, h * 4 + cc, 0:ntok], in0=rtmp[rb][:, 0:ntok], in1=rtmp[rb][:, 0:ntok],
                        op=ALU.mult), [te, G("hid_free%d" % hb), regA_done])
                    state["rt_free%d" % rb] = tsq
                    thid.append(tsq)
                    lastpe = tk
                slab_free[b_] = lastpe
            for h in range(2):
                b_, tsl = load_slab(l, 22 + qd * 2 + h)
                slab = slabs[b_]
                for s in range(ns):
                    i, tk = mm_group(lambda k, s=s, hb=hb: hid[hb][:, k, s * 128:(s + 1) * 128],
                                     lambda k, slab=slab: slab[:, k, :], 8, 512, [tsl], kdeps=thid)
                    te = DVE.add(lambda e, i=i, s=s, h=h: e.tensor_tensor(
                        out=xt[:, s, h * 512:(h + 1) * 512], in0=slot_aps[i][:, :],
                        in1=xt[:, s, h * 512:(h + 1) * 512], op=ALU.add), [tk])
                    slot_free[i] = te
                    if qd == 3:
                        txm.append((s, te))
                    lastpe = tk
                slab_free[b_] = lastpe
            state["hid_free%d" % hb] = lastpe
        state["hT_free"] = lastpe
        state["region_free"] = [lastpe, tox]
        txs = [[te for (s2, te) in txm if s2 == s] for s in range(ns)]
        stores = []
        if l == DEPTH - 1:
            st["per_s"] = True
            rs, t3 = rms_rstd(ns, txs, G("xn_free"))
            st["per_s"] = False
            ydst = y_sample if is_s else y_prompt[row0:row0 + ntok, :]
            for s in range(ns):
                ty = DVE.add(lambda e, s=s: e.scalar_tensor_tensor(
                    out=xt[:, s, :], in0=xt[:, s, :], scalar=rs[:, s:s + 1], in1=fg_bc[:, :],
                    op0=ALU.mult, op1=ALU.mult), [t3, t_const])
                stores.append(POOL.dma(xs_ds[s], ydst[s * 128:(s + 1) * 128, :], xt[:, s, :], deps=[ty]))
        else:
            for s in range(ns):
                stores.append(POOL.dma(xs_ds[s], xscr[row0 + s * 128:row0 + (s + 1) * 128, :], xt[:, s, :],
                                       deps=[txs[s]]))
        out_toks.extend(stores)
        for s in range(ns):
            xt_store[s] = stores[s]
        state["xt_free"] = list(xt_store)
        state["xs_all"] = list(xt_store)

    def schedule():
        step = 0
        for l in range(DEPTH):
            if upto is not None and step >= upto:
                return
            t_v1 = POOL.add(lambda e: e.memset(Vres[:, :, :, 128:129], 1.0), [G("samp_free")])
            state["vres_wr"] = [t_v1, G("samp_free")]
            state["memkv_tok"] = [mem_kv_prompt(l)]
            state["hk_tok"] = POOL.add(lambda e: e.memset(hist_keep[:, :, :], 0.0), [G("hist_tok")])
            step += 1
            for t in range(NTP):
                if upto is not None and step >= upto:
                    return
                run_tile(l, t)
                step += 1
            if upto is not None and step >= upto:
                return
            state["hist_tok"] = load_pool_state(l)
            run_tile(l, NTP)
            step += 1

    try:
        schedule()
    except _Stop:
        pass
    SP.wait_only(out_toks)

    block = es.enter_context(nc.Block())

    @block.tensor
    def _(e):
        PE.replay(e)

    @block.scalar
    def _(e):
        ACT.replay(e)

    @block.vector
    def _(e):
        DVE.replay(e)

    @block.gpsimd
    def _(e):
        POOL.replay(e)

    @block.sync
    def _(e):
        SP.replay(e)

    es.close()
    return nc


def make_in_maps(inputs, ncores=8, S=8192, PAST=2048):
    f = lambda a: np.ascontiguousarray(np.asarray(a, dtype=np.float32))
    maps = []
    for c in range(ncores):
        sl = slice(c * NBS, (c + 1) * NBS)
        m = {
            "x_prompt": f(inputs["x_prompt"][c]),
            "x_sample": f(inputs["x_sample"][sl]).reshape(NBS * SSEQ, D),
            "cache_k": f(inputs["cache_k"][:, sl]).reshape(DEPTH, NBS, PAST, 512),
            "cache_v": f(inputs["cache_v"][:, sl]),
            "state_pool": f(inputs["state_pool"][:, sl]),
            "cache_mem_k": f(inputs["cache_mem_k"][:, sl]).reshape(DEPTH, NBS, NMEM, D),
            "cache_mem_v": f(inputs["cache_mem_v"][:, sl]).reshape(DEPTH, NBS, NMEM, D),
            "mem_prompt": f(inputs["mem_prompt"][c]),
            "lam_q": f(inputs["lam_q"]).reshape(DEPTH, 128),
            "lam_k": f(inputs["lam_k"]).reshape(DEPTH, 128),
            "final_g": f(inputs["final_g"]).reshape(1, D),
        }
        for k in ["norm_mix_g", "w_in", "subln_g", "w_pool", "pool_scale", "w_out", "norm_x_g", "norm_mem_g",
                  "wq_x", "wk_x", "wv_x", "wo_x", "norm_mlp_g", "w_up", "w_down"]:
            m[k] = f(inputs[k])
        maps.append(m)
    return maps


def assemble(results, ncores=8, S=8192):
    def cat(name, axis, shape_fn):
        return np.concatenate([shape_fn(r[name]) for r in results], axis=axis)
    y_prompt = np.stack([r["y_prompt"] for r in results], 0)
    y_sample = np.concatenate([r["y_sample"].reshape(NBS, SSEQ, D) for r in results], 0)
    k_prompt = np.stack([r["k_prompt"].reshape(DEPTH, S, 2, 4, 64) for r in results], 1)
    v_prompt = np.stack([r["v_prompt"].reshape(DEPTH, S, 4, 128) for r in results], 1)
    pool_prompt = np.stack([r["pool_prompt"] for r in results], 1)
    mem_k = np.stack([r["mem_k_prompt"].reshape(DEPTH, NMEM, 4, 256) for r in results], 1)
    mem_v = np.stack([r["mem_v_prompt"].reshape(DEPTH, NMEM, 4, 256) for r in results], 1)
    k_sample = np.concatenate([r["k_sample"].reshape(DEPTH, NBS, SSEQ, 2, 4, 64) for r in results], 1)
    v_sample = np.concatenate([r["v_sample"].reshape(DEPTH, NBS, SSEQ, 4, 128) for r in results], 1)
    pool_sample = np.concatenate([r["pool_sample"] for r in results], 1)
    outs = (y_prompt, y_sample, k_prompt, v_prompt, pool_prompt, mem_k, mem_v, k_sample, v_sample, pool_sample)
    return tuple(np.ascontiguousarray(o, dtype=np.float32) for o in outs)


def kernel(**inputs):
    ncores = 8
    nc = build_program()
    in_maps = make_in_maps(inputs, ncores)
    res = run_bass_kernel_spmd(nc, in_maps, core_ids=list(range(ncores)))
    return assemble(res.results, ncores)
```

```python
import math
from contextlib import ExitStack

import numpy as np
import concourse.bass as bass
import concourse.mybir as mybir
from concourse.bass_utils import run_bass_kernel_spmd

F32 = mybir.dt.float32
BF16 = mybir.dt.bfloat16
AF = mybir.ActivationFunctionType
ALU = mybir.AluOpType

D = 1024
DEPTH = 2
NBS = 4
SSEQ = 64
NMEM = 256
NSLAB = 30


def _flat(deps):
    if deps is None:
        return
    if isinstance(deps, tuple) and len(deps) == 2 and isinstance(deps[1], int):
        yield deps
        return
    for d in deps:
        yield from _flat(d)


class DSem:
    def __init__(self, sem):
        self.sem = sem
        self.n = 0


class Eng:
    def __init__(self, name, sem):
        self.name = name
        self.sem = sem
        self.n = 0
        self.seen = {}
        self.prog = []
        self.chain = (name in ("act", "dve", "pool"))

    def _waits(self, deps):
        waits = []
        for d in _flat(deps):
            sem, val = d
            k = id(sem)
            if self.seen.get(k, 0) >= val:
                continue
            self.seen[k] = val
            waits.append((sem, val))
        return waits

    def add(self, fn, deps=(), sig=True, nochain=False):
        if self.chain and self.n > 0 and not nochain:
            deps = [deps, (self.sem, self.n)]
        waits = self._waits(deps)
        tok = None
        if sig:
            self.n += 1
            tok = (self.sem, self.n)
        self.prog.append((waits, fn, 1 if sig else 0, None))
        return tok

    def dma(self, dsem, out, in_, deps=(), **kw):
        waits = self._waits(deps)
        dsem.n += 16
        self.prog.append((waits, lambda e: e.dma_start(out=out, in_=in_, **kw), 2, dsem.sem))
        return (dsem.sem, dsem.n)

    def wait_only(self, deps):
        waits = self._waits(deps)
        if waits:
            self.prog.append((waits, None, 0, None))

    def replay(self, e):
        for waits, fn, kind, dsem in self.prog:
            for sem, val in waits:
                e.wait_ge(sem, val)
            if fn is None:
                continue
            inst = fn(e)
            if kind == 1:
                inst.then_inc(self.sem, 1)
            elif kind == 2:
                inst.then_inc(dsem, 16)


class _Stop(Exception):
    pass


def build_program(S=8192, PAST=2048, upto=None, dbg=None):
    NTP = S // 512
    NKB = S // 128
    NCB = PAST // 128
    nc = bass.Bass("TRN2", target_bir_lowering=False)
    es = ExitStack()

    def din(name, shape):
        return nc.dram_tensor(name, shape, F32, kind="ExternalInput").ap()

    def dout(name, shape):
        return nc.dram_tensor(name, shape, F32, kind="ExternalOutput").ap()

    x_prompt = din("x_prompt", [S, D])
    x_sample = din("x_sample", [NBS * SSEQ, D])
    cache_k = din("cache_k", [DEPTH, NBS, PAST, 512])
    cache_v = din("cache_v", [DEPTH, NBS, PAST, 4, 128])
    state_pool = din("state_pool", [DEPTH, NBS, 15, 512])
    cache_mem_k = din("cache_mem_k", [DEPTH, NBS, NMEM, D])
    cache_mem_v = din("cache_mem_v", [DEPTH, NBS, NMEM, D])
    mem_prompt = din("mem_prompt", [NMEM, D])
    norm_mix_g = din("norm_mix_g", [DEPTH, D])
    w_in = din("w_in", [DEPTH, D, 2048])
    lam_q = din("lam_q", [DEPTH, 128])
    lam_k = din("lam_k", [DEPTH, 128])
    subln_g = din("subln_g", [DEPTH, 128])
    w_pool = din("w_pool", [DEPTH, 4, 128, 128])
    pool_scale = din("pool_scale", [DEPTH, 512])
    w_out = din("w_out", [DEPTH, D, D])
    norm_x_g = din("norm_x_g", [DEPTH, D])
    norm_mem_g = din("norm_mem_g", [DEPTH, D])
    wq_x = din("wq_x", [DEPTH, D, D])
    wk_x = din("wk_x", [DEPTH, D, D])
    wv_x = din("wv_x", [DEPTH, D, D])
    wo_x = din("wo_x", [DEPTH, D, D])
    norm_mlp_g = din("norm_mlp_g", [DEPTH, D])
    w_up = din("w_up", [DEPTH, D, 4096])
    w_down = din("w_down", [DEPTH, 4096, D])
    final_g = din("final_g", [1, D])

    y_prompt = dout("y_prompt", [S, D])
    y_sample = dout("y_sample", [NBS * SSEQ, D])
    k_prompt = dout("k_prompt", [DEPTH, S, 512])
    v_prompt = dout("v_prompt", [DEPTH, S, 512])
    pool_prompt = dout("pool_prompt", [DEPTH, 15, 512])
    mem_k_prompt = dout("mem_k_prompt", [DEPTH, NMEM, D])
    mem_v_prompt = dout("mem_v_prompt", [DEPTH, NMEM, D])
    k_sample = dout("k_sample", [DEPTH, NBS * SSEQ, 512])
    v_sample = dout("v_sample", [DEPTH, NBS * SSEQ, 512])
    pool_sample = dout("pool_sample", [DEPTH, NBS, 15, 512])

    wscr = nc.dram_tensor("wscr", [DEPTH, NSLAB, 128, 8, 512], BF16, kind="Internal").ap()
    xscr = nc.dram_tensor("xscr", [S + NBS * SSEQ, D], F32, kind="Internal").ap()
    ktscr = nc.dram_tensor("ktscr", [DEPTH, 4, 128, S], BF16, kind="Internal").ap()

    def sb(name, shape, dt):
        return es.enter_context(nc.sbuf_tensor(name, shape, dt))

    def ps(name, shape, dt):
        return es.enter_context(nc.psum_tensor(name, shape, dt))

    def newsem(name):
        return es.enter_context(nc.semaphore(name))

    PE = Eng("pe", newsem("s_pe"))
    ACT = Eng("act", newsem("s_act"))
    DVE = Eng("dve", newsem("s_dve"))
    POOL = Eng("pool", newsem("s_pool"))
    SP = Eng("sp", newsem("s_sp"))
    _dsn = [0]

    def dsem():
        _dsn[0] += 1
        return DSem(newsem("d%d" % _dsn[0]))

    VRES_COLS = NKB * 4 * 129
    SAMP_COLS = NCB * 512 + 4 * (PAST) + (NCB + 1) * 4 * 129
    vreg = sb("vreg", [128, max(VRES_COLS, SAMP_COLS)], BF16)
    Vres = vreg[:, 0:VRES_COLS].rearrange("p (k h d) -> p k h d", h=4, d=129)
    o0 = 0
    kc_tok = vreg[:, o0:o0 + NCB * 512].rearrange("p (k c) -> p k c", c=512)
    o0 += NCB * 512
    kT_s = vreg[:, o0:o0 + 4 * PAST].rearrange("p (c k) -> p c k", c=4)
    o0 += 4 * PAST
    V_s = vreg[:, o0:o0 + (NCB + 1) * 4 * 129].rearrange("p (k h d) -> p k h d", h=4, d=129)

    xt = sb("xt", [128, 4, D], F32)
    xn = sb("xn", [128, 4, D], BF16)
    mktok = xn[:, 0:2, :]
    hT = sb("hT", [128, 8, 512], BF16)
    slabs = [sb("slab%d" % i, [128, 8, 512], BF16) for i in range(3)]
    NSTAGE = 3
    stage = [sb("stage%d" % i, [128, 512], F32) for i in range(NSTAGE)]
    aT = sb("aT", [128, 4, 512], BF16)
    pyT = sb("pyT", [128, 4, 512], BF16)
    mkT = sb("mkT", [128, 8, NMEM], BF16)
    mv = sb("mv", [128, 2, D], BF16)
    stats = sb("stats", [128, 512], F32)
    ident = sb("ident", [128, 128], BF16)
    identf = sb("identf", [128, 128], F32)
    ones_bf = sb("ones_bf", [128, 128], BF16)
    gcols = sb("gcols", [128, DEPTH, 4, 8], F32)
    pscol = sb("pscol", [128, DEPTH, 4], F32)
    fg_bc = sb("fg_bc", [128, D], F32)
    sg_bc = sb("sg_bc", [128, DEPTH, 128], F32)
    lam_t = sb("lam_t", [128, DEPTH, 8], F32)
    wpool_bf = sb("wpool_bf", [128, DEPTH, 4, 128], BF16)
    icnt = sb("icnt", [128, 4, 16], F32)
    o1b = sb("o1b", [128, 8, 128], F32)
    osq = sb("osq", [128, 128], F32)
    hist_keep = sb("hist_keep", [128, 4, 16], F32)
    sgcol = sb("sgcol", [128, DEPTH], F32)
    RB = 47 * 1024
    region = sb("region", [128, RB], mybir.dt.uint8)

    class Carver:
        def __init__(self):
            self.off = 0

        def take(self, shape, dt):
            n = int(np.prod(shape[1:]))
            bpe = 2 if dt == BF16 else 4
            nbytes = n * bpe
            off = (self.off + 63) // 64 * 64
            assert off + nbytes <= RB, (off, nbytes, RB)
            ap = region[:, off:off + nbytes].bitcast(dt)
            self.off = off + nbytes
            if len(shape) == 2:
                return ap
            names = "abcd"[:len(shape) - 1]
            pat = "p (%s) -> p %s" % (" ".join(names), " ".join(names))
            kw = {names[i]: shape[i + 1] for i in range(1, len(names))}
            return ap.rearrange(pat, **kw)

    ca = Carver()
    qT = ca.take([128, 4, 512], BF16)
    kTcur = ca.take([128, 4, 512], BF16)
    ksreg = ca.take([128, 4096 + 64], BF16)
    kstream = [ksreg[:, i * 2048:(i + 1) * 2048] for i in range(2)]
    vnew = [ksreg[0:64, i * 516:(i + 1) * 516].rearrange("p (h d) -> p h d", h=4) for i in range(4)]
    PT = [ca.take([128, 2, 512], BF16) for _ in range(3)]
    o1 = ca.take([128, 8, 128], F32)
    a_tok = ca.take([128, 4, 512], BF16)
    uext = ca.take([128, 4, 528], F32)
    utmp_all = ca.take([128, 1056], F32)
    utmp = [utmp_all[:, i * 528:(i + 1) * 528] for i in range(2)]
    otmp = utmp_all[:, 0:1024].rearrange("p (a d) -> p a d", a=8)
    pooled2 = [ca.take([128, 512], BF16) for _ in range(4)]
    lamw = region[:, 0:2048].bitcast(F32).rearrange("p (a b) -> p a b", a=4)
    cb = Carver()
    qxT = cb.take([128, 8, 512], BF16)
    oxT = cb.take([128, 8, 512], BF16)
    PxT = [cb.take([128, 2, 512], BF16) for _ in range(2)]
    rrec = [cb.take([128, 512], F32) for _ in range(2)]
    hid = [cb.take([128, 8, 512], BF16) for _ in range(2)]
    rtmp = [cb.take([128, 512], BF16) for _ in range(2)]

    psA = ps("psA", [128, 2, 512], F32)
    psB = ps("psB", [128, 2, 512], F32)
    psO = ps("psO", [128, 3, 512], F32)
    psT = ps("psT", [128, 512], F32)

    st = {"stat": 0, "slab": 0, "slot": 0, "stage": 0, "ev": 0, "tb": 0}
    slab_free = [None, None, None]
    slab_ds = [dsem() for _ in range(3)]
    slot_aps = [psA[:, 0, :], psA[:, 1, :], psB[:, 0, :], psB[:, 1, :], psT[:, :]]
    slot_bf = [a_.bitcast(BF16) for a_ in slot_aps]
    NSLOT = 5
    slot_free = [None] * NSLOT
    stage_ds = [dsem() for _ in range(NSTAGE)]
    stage_free = [None] * NSTAGE
    conv_tok = {}
    out_toks = []
    state = {"hT_free": None, "xt_free": None, "region_free": None, "kst": [None] * max(NTP, 1),
             "o1_tok": [None, None]}

    def G(k):
        return state.get(k)

    ckn = [0]

    def ck(n=None):
        ckn[0] += 1
        if dbg == ckn[0]:
            raise _Stop()

    def stat(n):
        if st["stat"] + n > 512:
            st["stat"] = 0
        a_ = stats[:, st["stat"]:st["stat"] + n]
        st["stat"] += n
        return a_

    def get_slot():
        i = st["slot"]
        st["slot"] = (i + 1) % NSLOT
        return i

    def tr_group(ins, deps, f32=False):
        i = get_slot()
        view = slot_aps[i] if f32 else slot_bf[i]
        idt = identf if f32 else ident
        tp = None
        off = 0
        for k, in_ap in enumerate(ins):
            n = in_ap.shape[0]
            tp = PE.add(lambda e, in_ap=in_ap, off=off, n=n: e.transpose(
                out=view[:, off:off + n], in_=in_ap, identity=idt[0:n, 0:n]),
                [deps, slot_free[i], t_ident, t_idf] if k == 0 else (), sig=(k == len(ins) - 1))
            off += n
        return i, view, tp

    def evac_eng():
        st["ev"] ^= 1
        return ACT if st["ev"] else DVE

    def ev_copy(eng, out, in_, deps):
        if eng is ACT:
            return ACT.add(lambda e: e.copy(out=out, in_=in_), deps)
        return eng.add(lambda e: e.tensor_copy(out=out, in_=in_), deps)

    def ev_scale(eng, out, in_, sc, deps):
        if eng is ACT:
            return ACT.add(lambda e: e.mul(out=out, in_=in_, mul=sc), deps)
        return eng.add(lambda e: e.tensor_scalar(out=out, in0=in_, scalar1=sc, scalar2=None, op0=ALU.mult), deps)

    def load_slab(l, idx):
        b_ = st["slab"]
        st["slab"] = (b_ + 1) % 3
        tok = SP.dma(slab_ds[b_], slabs[b_][:], wscr[l, idx], deps=[conv_tok[(l, idx)], slab_free[b_]])
        return b_, tok

    def get_stage():
        i = st["stage"]
        st["stage"] = (i + 1) % NSTAGE
        return i

    def store_stage(i, dram_ap, src_ap, dep):
        tok = POOL.dma(stage_ds[i], dram_ap, src_ap, deps=[dep])
        stage_free[i] = tok
        out_toks.append(tok)
        return tok

    def acc_ap(a_, r0=0, rows=128, cols=129):
        return psO[r0:r0 + rows, a_ // 3, (a_ % 3) * 160:(a_ % 3) * 160 + cols]

    t_id0 = POOL.add(lambda e: e.memset(identf[:], 0.0))
    t_idf = POOL.add(lambda e: e.affine_select(out=identf[:], in_=identf[:], pattern=[[-1, 128]],
                                               compare_op=ALU.not_equal, fill=1.0, base=0,
                                               channel_multiplier=1), [t_id0])
    t_ident = POOL.add(lambda e: e.tensor_copy(out=ident[:], in_=identf[:]), [t_idf])
    t_ones = POOL.add(lambda e: e.memset(ones_bf[:], 1.0))
    ic_toks = []
    for g in range(4):
        w = 2 << g
        ic_toks.append(POOL.add(lambda e, g=g, w=w: e.memset(icnt[:, g, :], 1.0)))
        for tt in range(min(w - 1, 16)):
            ic_toks.append(POOL.add(lambda e, g=g, tt=tt, w=w: e.memset(icnt[:, g, tt:tt + 1], float(w) / (tt + 1))))
    c_ds = dsem()
    t_const = None
    gsrc = [norm_mix_g, norm_x_g, norm_mem_g, norm_mlp_g]
    for l in range(DEPTH):
        for i, gs in enumerate(gsrc):
            t_const = SP.dma(c_ds, gcols[:, l, i, :], gs[l].rearrange("(c p) -> p c", p=128),
                             allow_slow_non_contiguous=True)
        t_const = SP.dma(c_ds, pscol[:, l, :], pool_scale[l].rearrange("(c p) -> p c", p=128),
                         allow_slow_non_contiguous=True)
        t_const = SP.dma(c_ds, sg_bc[:, l, :], subln_g[l:l + 1, :].broadcast_to([128, 128]))
        t_const = SP.dma(c_ds, sgcol[:, l:l + 1], subln_g[l:l + 1, :].rearrange("o d -> d o"),
                         allow_slow_non_contiguous=True)
        t_const = SP.dma(c_ds, lamw[:, l * 2 + 0, :], lam_q[l:l + 1, :].broadcast_to([128, 128]))
        t_const = SP.dma(c_ds, lamw[:, l * 2 + 1, :], lam_k[l:l + 1, :].broadcast_to([128, 128]))
    t_const = SP.dma(c_ds, fg_bc[:], final_g[0:1, :].broadcast_to([128, D]))
    for l in range(DEPTH):
        for g in range(4):
            t_const = DVE.add(lambda e, l=l, g=g: e.tensor_scalar(out=pscol[:, l, g:g + 1], in0=pscol[:, l, g:g + 1],
                                                                 scalar1=1.0 / (2 << g), scalar2=None, op0=ALU.mult),
                              [t_const])
    wp_ds = dsem()
    t_wp = None
    for l in range(DEPTH):
        t_wp = POOL.dma(wp_ds, wpool_bf[:, l, :, :], w_pool[l].rearrange("g c d -> c g d"))

    def conv(l, idx, src2d):
        d_ = dsem()
        conv_tok[(l, idx)] = POOL.dma(d_, wscr[l, idx], src2d.rearrange("(kc p) c -> p kc c", p=128))

    def conv_list(l):
        out = []
        for h in range(2):
            out.append((8 + h, wk_x[l][:, h * 512:(h + 1) * 512]))
        for h in range(2):
            out.append((10 + h, wv_x[l][:, h * 512:(h + 1) * 512]))
        for s in range(4):
            out.append((s, w_in[l][:, s * 512:(s + 1) * 512]))
        for h in range(2):
            out.append((4 + h, w_out[l][:, h * 512:(h + 1) * 512]))
        for h in range(2):
            out.append((6 + h, wq_x[l][:, h * 512:(h + 1) * 512]))
        for h in range(2):
            out.append((12 + h, wo_x[l][:, h * 512:(h + 1) * 512]))
        for qd in range(4):
            for h in range(2):
                out.append((14 + qd * 2 + h, w_up[l][:, (qd * 2 + h) * 512:(qd * 2 + h + 1) * 512]))
            for h in range(2):
                out.append((22 + qd * 2 + h, w_down[l][qd * 1024:(qd + 1) * 1024, h * 512:(h + 1) * 512]))
        return out

    for (idx_, src_) in conv_list(0):
        conv(0, idx_, src_)
    pending_conv = {l: conv_list(l) for l in range(1, DEPTH)}

    def drip_conv(l, n):
        lst = pending_conv.get(l, [])
        for _ in range(min(n, len(lst))):
            idx_, src_ = lst.pop(0)
            conv(l, idx_, src_)

    lam_inits = [0.8 - 0.6 * math.exp(-0.3 * l) for l in range(DEPTH)]
    lam_tok = []
    for l in range(DEPTH):
        tk = None
        for i in range(2):
            tk = DVE.add(lambda e, l=l, i=i: e.scalar_tensor_tensor(
                out=osq[:, i * 64:(i + 1) * 64], in0=lamw[:, l * 2, i * 64:(i + 1) * 64], scalar=1.0,
                in1=lamw[:, l * 2 + 1, i * 64:(i + 1) * 64], op0=ALU.mult, op1=ALU.mult,
                accum_out=lam_t[:, l, i:i + 1]), [t_const, tk])
        t1 = ACT.add(lambda e, l=l: e.activation(out=lam_t[:, l, 2:4], in_=lam_t[:, l, 0:2], func=AF.Exp), [tk])
        t2 = DVE.add(lambda e, l=l: e.tensor_tensor(out=lam_t[:, l, 4:5], in0=lam_t[:, l, 2:3],
                                                    in1=lam_t[:, l, 3:4], op=ALU.subtract), [t1])
        t3 = DVE.add(lambda e, l=l: e.tensor_scalar(out=lam_t[:, l, 5:6], in0=lam_t[:, l, 4:5],
                                                    scalar1=-1.0, scalar2=-lam_inits[l],
                                                    op0=ALU.mult, op1=ALU.add), [t2])
        t4 = DVE.add(lambda e, l=l: e.tensor_scalar(out=sgcol[:, l:l + 1], in0=sgcol[:, l:l + 1],
                                                    scalar1=1.0 - lam_inits[l], scalar2=None,
                                                    op0=ALU.mult), [t_const, t3])
        lam_tok.append(t4)
    state["region_free"] = list(lam_tok)

    def rms_rstd(ns, deps_x, junk_deps):
        ss = stat(ns)
        rs = stat(ns)
        tks = []
        for s in range(ns):
            dx = deps_x[s] if (isinstance(deps_x, list) and len(deps_x) == ns and st.get("per_s")) else deps_x
            if s % 2 == 0:
                tks.append(ACT.add(lambda e, s=s: e.activation(out=xn[:, s, :], in_=xt[:, s, :], func=AF.Square,
                                                              accum_out=ss[:, s:s + 1]), [dx, junk_deps]))
            else:
                tks.append(DVE.add(lambda e, s=s: e.scalar_tensor_tensor(
                    out=xn[:, s, :], in0=xt[:, s, :], scalar=1.0, in1=xt[:, s, :], op0=ALU.mult, op1=ALU.mult,
                    accum_out=ss[:, s:s + 1]), [dx, junk_deps]))
        t1 = DVE.add(lambda e: e.tensor_scalar(out=ss, in0=ss, scalar1=1.0 / D, scalar2=1e-6,
                                               op0=ALU.mult, op1=ALU.add), tks)
        t2 = ACT.add(lambda e: e.activation(out=ss, in_=ss, func=AF.Ln), [t1])
        t3 = ACT.add(lambda e: e.activation(out=rs, in_=ss, func=AF.Exp, scale=-0.5), [t2])
        ck()
        return rs, t3

    def rmsnorm_hT(ns, gcol, deps_x, deps_hT_free):
        rs, t3 = rms_rstd(ns, deps_x, G("xn_free"))
        xtk = []
        for s in range(ns):
            xtk.append(ev_scale(evac_eng(), xn[:, s, :], xt[:, s, :], rs[:, s:s + 1], [t3]))
        ck()
        out = []
        tp = None
        for kc in range(8):
            i, view, tp = tr_group([xn[:, s, kc * 128:(kc + 1) * 128] for s in range(ns)], [xtk])
            tk = ev_scale(evac_eng(), hT[:, kc, 0:ns * 128], view[:, 0:ns * 128], gcol[:, kc:kc + 1],
                          [tp, t_const, deps_hT_free])
            slot_free[i] = tk
            out.append(tk)
            ck()
        state["xn_free"] = tp
        ck()
        return out

    def mm_group(lhs_fn, rhs_fn, nk, ncol, deps, m=128, kdeps=None):
        i = get_slot()
        tk = None
        for k in range(nk):
            dk = [deps, slot_free[i]] if k == 0 else []
            if kdeps is not None:
                dk = dk + [kdeps[k]]
            tk = PE.add(lambda e, k=k, i=i: e.matmul(slot_aps[i][0:m, 0:ncol], lhsT=lhs_fn(k), rhs=rhs_fn(k),
                                                     start=(k == 0), stop=(k == nk - 1)),
                        dk, sig=(k == nk - 1))
        return i, tk

    ks_ds = [dsem(), dsem()]
    ks_free = [None, None]
    ks_i = [0]
    kst_ds = [dsem(), dsem()]
    xl_ds = [dsem() for _ in range(4)]
    xs_ds = [dsem() for _ in range(4)]
    ptfree = [None, None, None]
    accfree = [None] * 8
    misc_ds = dsem()
    mem_ds = dsem()
    ps_ds = [dsem() for _ in range(NBS)]
    kc_ds = dsem()
    xt_store = [None] * 4

    def mem_kv_prompt(l):
        d_ = dsem()
        tl = SP.dma(d_, xt[:, 0:2, :], mem_prompt.rearrange("(s p) c -> p s c", p=128), deps=[G("xt_free")])
        hts = rmsnorm_hT(2, gcols[:, l, 2, :], [tl], G("hT_free"))
        last_pe = None
        for which in range(2):
            for h in range(2):
                b_, tsl = load_slab(l, 8 + which * 2 + h)
                slab = slabs[b_]
                ck()
                for s in range(2):
                    i, tk = mm_group(lambda k, s=s: hT[:, k, s * 128:(s + 1) * 128],
                                     lambda k, slab=slab: slab[:, k, :], 8, 512, [tsl], kdeps=hts)
                    si = get_stage()
                    te = ev_copy(evac_eng(), stage[si][:], slot_aps[i][:, :], [tk, stage_free[si]])
                    dst = (mem_k_prompt if which == 0 else mem_v_prompt)[l, s * 128:(s + 1) * 128, h * 512:(h + 1) * 512]
                    store_stage(si, dst, stage[si][:], te)
                    ck()
                    if which == 1:
                        te2 = ev_copy(DVE, mv[:, s, h * 512:(h + 1) * 512], slot_aps[i][:, :], [tk, te])
                        slot_free[i] = [te, te2]
                    else:
                        slot_free[i] = te
                    last_pe = tk
                    ck()
                if which == 0:
                    for cc in range(4):
                        i, tk = mm_group(lambda k, cc=cc, slab=slab: slab[:, k, cc * 128:(cc + 1) * 128],
                                         lambda k: hT[:, k, 0:256], 8, 256, [tsl], kdeps=hts)
                        te = ev_copy(evac_eng(), mkT[:, h * 4 + cc, :], slot_aps[i][:, 0:256], [tk])
                        slot_free[i] = te
                        last_pe = tk
                        ck()
                slab_free[b_] = last_pe
        state["hT_free"] = last_pe
        state["xt_free"] = last_pe
        return last_pe

    def cross_attn(l, c0, ncol, dep_in):
        last = None
        for hx in range(4):
            pb = hx % 2
            ptok = []
            for mb in range(2):
                i = get_slot()
                tk = None
                for half in range(2):
                    tk = PE.add(lambda e, i=i, half=half, mb=mb, hx=hx: e.matmul(
                        slot_aps[i][:, 0:ncol], lhsT=mkT[:, hx * 2 + half, mb * 128:(mb + 1) * 128],
                        rhs=qxT[:, hx * 2 + half, c0:c0 + ncol], start=(half == 0), stop=(half == 1)),
                        [dep_in, slot_free[i]] if half == 0 else (), sig=(half == 1))
                te = ACT.add(lambda e, i=i, mb=mb, pb=pb: e.activation(
                    out=PxT[pb][:, mb, 0:ncol], in_=slot_aps[i][:, 0:ncol], func=AF.Exp, scale=1.0 / 16.0),
                    [tk, G("px_free%d" % pb)])
                slot_free[i] = te
                ptok.append(te)
            i = get_slot()
            tk = None
            for mb in range(2):
                tk = PE.add(lambda e, i=i, mb=mb, pb=pb: e.matmul(
                    slot_aps[i][:, 0:ncol], lhsT=ones_bf[:, :], rhs=PxT[pb][:, mb, 0:ncol],
                    start=(mb == 0), stop=(mb == 1)),
                    [ptok, slot_free[i], t_ones] if mb == 0 else (), sig=(mb == 1))
            tr = DVE.add(lambda e, i=i, pb=pb: e.reciprocal(out=rrec[pb][:, 0:ncol], in_=slot_aps[i][:, 0:ncol]),
                         [tk, G("rr_free%d" % pb)])
            slot_free[i] = tr
            tms = []
            for half in range(2):
                i = get_slot()
                tk = None
                for mb in range(2):
                    tk = PE.add(lambda e, i=i, mb=mb, pb=pb, hx=hx, half=half: e.matmul(
                        slot_aps[i][:, 0:ncol], lhsT=mv[:, mb, hx * 256 + half * 128: hx * 256 + (half + 1) * 128],
                        rhs=PxT[pb][:, mb, 0:ncol], start=(mb == 0), stop=(mb == 1)),
                        [ptok, slot_free[i]] if mb == 0 else (), sig=(mb == 1))
                tm = DVE.add(lambda e, i=i, pb=pb, hx=hx, half=half: e.tensor_tensor(
                    out=oxT[:, hx * 2 + half, c0:c0 + ncol], in0=slot_aps[i][:, 0:ncol],
                    in1=rrec[pb][:, 0:ncol], op=ALU.mult), [tk, tr, G("ox_free")])
                slot_free[i] = tm
                tms.append(tm)
                last = tk
            state["px_free%d" % pb] = last
            state["rr_free%d" % pb] = tms
            state["ox_toks"] = state.get("ox_toks", []) + tms
        return last

    def subln(l, j, tev, r0, rows, items):
        ob = o1 if j == 0 else o1b
        n = len(items)
        ssq = stat(n)
        R = slice(r0, r0 + rows)
        tks = []
        for k, (a_, h, sub) in enumerate(items):
            tks.append(DVE.add(lambda e, a_=a_, k=k: e.scalar_tensor_tensor(
                out=osq[R, :], in0=ob[R, a_, :], scalar=1.0, in1=ob[R, a_, :], op0=ALU.mult, op1=ALU.mult,
                accum_out=ssq[R, k:k + 1]), [tev]))
        t1 = DVE.add(lambda e: e.tensor_scalar(out=ssq[R, :], in0=ssq[R, :], scalar1=1.0 / 128, scalar2=1e-5,
                                               op0=ALU.mult, op1=ALU.add), tks)
        t2 = ACT.add(lambda e: e.activation(out=ssq[R, :], in_=ssq[R, :], func=AF.Ln), [t1])
        t3 = ACT.add(lambda e: e.activation(out=ssq[R, :], in_=ssq[R, :], func=AF.Exp, scale=-0.5), [t2])
        outs = []
        for k, (a_, h, sub) in enumerate(items):
            outs.append(DVE.add(lambda e, a_=a_, h=h, sub=sub, k=k: e.tensor_scalar(
                out=a_tok[R, sub, h * 128:(h + 1) * 128], in0=ob[R, a_, :],
                scalar1=ssq[R, k:k + 1], scalar2=None, op0=ALU.mult),
                [t3, lam_tok[l], G("aT_free")]))
        state["o1_free"] = outs
        return outs

    def evac_pass(l, m, j, tlast, r0, rows, accs):
        ob = o1 if j == 0 else o1b
        rc = stat(8)
        R = slice(r0, r0 + rows)
        tev = []
        for a_ in accs:
            tr = DVE.add(lambda e, a_=a_: e.reciprocal(out=rc[R, a_:a_ + 1],
                                                       in_=acc_ap(a_, r0, rows)[:, 128:129]), [tlast])
            if m == 0:
                te = DVE.add(lambda e, a_=a_: e.tensor_scalar(
                    out=ob[R, a_, :], in0=acc_ap(a_, r0, rows)[:, 0:128], scalar1=rc[R, a_:a_ + 1],
                    scalar2=None, op0=ALU.mult), [tr, G("o1_free")])
            else:
                tn = DVE.add(lambda e, a_=a_: e.tensor_scalar(out=rc[R, a_:a_ + 1], in0=rc[R, a_:a_ + 1],
                                                              scalar1=lam_t[R, l, 5:6], scalar2=None, op0=ALU.mult),
                             [tr, lam_tok[l]])
                te = DVE.add(lambda e, a_=a_: e.scalar_tensor_tensor(
                    out=ob[R, a_, :], in0=acc_ap(a_, r0, rows)[:, 0:128], scalar=rc[R, a_:a_ + 1],
                    in1=ob[R, a_, :], op0=ALU.mult, op1=ALU.add), [tn, state["o1_tok"][j]])
            tev.append(te)
        for a_ in range(8):
            accfree[a_] = tev
        return tev

    def subln_fast(l, j, tev):
        ob = o1 if j == 0 else o1b
        ssq = stat(8)
        t0 = DVE.add(lambda e: e.tensor_tensor(out=otmp[:, :, :], in0=ob[:, :, :], in1=ob[:, :, :], op=ALU.mult),
                     [tev, G("otmp_free")])
        t0b = DVE.add(lambda e: e.tensor_reduce(out=ssq, in_=otmp[:, :, :], axis=mybir.AxisListType.X, op=ALU.add), [t0])
        t1 = DVE.add(lambda e: e.tensor_scalar(out=ssq, in0=ssq, scalar1=1.0 / 128, scalar2=1e-5,
                                               op0=ALU.mult, op1=ALU.add), [t0b])
        t2 = ACT.add(lambda e: e.activation(out=ssq, in_=ssq, func=AF.Ln), [t1])
        t3 = ACT.add(lambda e: e.activation(out=ssq, in_=ssq, func=AF.Exp, scale=-0.5), [t2])
        dst = a_tok[:, :, 2 * j * 128:(2 * j + 2) * 128].rearrange("p q (h d) -> p h q d", h=2)
        t5 = DVE.add(lambda e: e.tensor_tensor(
            out=dst, in0=ob[:, :, :].rearrange("p (h q) d -> p h q d", h=2),
            in1=ssq.rearrange("p (h q) -> p h q", h=2).unsqueeze(3).to_broadcast([128, 2, 4, 128]), op=ALU.mult),
            [t3, lam_tok[l], G("aT_free")])
        state["o1_free"] = [t5]
        state["otmp_free"] = t5
        return [t5]

    def evac_pass_fast(l, m, j, tlast, tpool_done):
        ob = o1 if j == 0 else o1b
        rc = stat(8)
        gate = []
        tmul = []
        for b_ in range(3):
            na = 3 if b_ < 2 else 2
            bank = psO[:, b_, 0:480].rearrange("p (a c) -> p a c", c=160)
            rcb = rc[:, 3 * b_:3 * b_ + na]
            tr = DVE.add(lambda e, bank=bank, rcb=rcb, na=na: e.reciprocal(out=rcb, in_=bank[:, 0:na, 128]), [tlast])
            dst = (ob if m == 0 else otmp)[:, 3 * b_:3 * b_ + na, :]
            te = DVE.add(lambda e, bank=bank, rcb=rcb, na=na, dst=dst: e.tensor_tensor(
                out=dst, in0=bank[:, 0:na, 0:128], in1=rcb.unsqueeze(2).to_broadcast([128, na, 128]),
                op=ALU.mult), [tr, G("o1_free") if m == 0 else tpool_done, G("otmp_free")])
            gate.append(te)
            tmul.append(te)
        for a_ in range(8):
            accfree[a_] = gate[a_ // 3]
        if m == 0:
            return tmul
        tadd = DVE.add(lambda e: e.scalar_tensor_tensor(out=ob[:, :, :], in0=otmp[:, :, :], scalar=lam_t[:, l, 5:6],
                                                         in1=ob[:, :, :], op0=ALU.mult, op1=ALU.add),
                       [tmul, state["o1_tok"][j], lam_tok[l]])
        state["otmp_free"] = tadd
        return [tadd]

    def attn_prompt(l, t, tq, tkT, tv, tpool_done):
        nprev = 4 * t
        rf_attn = G("region_free")
        deferred = []
        ta_all = []
        tlast = None
        for c in range(4):
            m, j = divmod(c, 2)
            chunks = [(i * 16, min((i + 1) * 16, nprev)) for i in range((nprev + 15) // 16)]
            cbuf = []

            def issue_chunk(ci):
                b0, b1 = chunks[ci]
                bi = ks_i[0]
                ks_i[0] ^= 1
                tkl = SP.dma(ks_ds[bi], kstream[bi][:, 0:(b1 - b0) * 128], ktscr[l, c, :, b0 * 128:b1 * 128],
                             deps=[ks_free[bi], state["kst"][t - 1], rf_attn])
                cbuf.append((bi, tkl))

            if chunks:
                issue_chunk(0)
            nkb = nprev + 4
            pend = {}

            def emit_qk(kb):
                dg = kb - nprev
                q0 = max(0, dg) * 128
                sp_i = kb % 2
                spair = psA if sp_i == 0 else psB
                pti = kb % 3
                tk = None
                if kb < nprev and kb % 16 == 0 and kb // 16 + 1 < len(chunks):
                    issue_chunk(kb // 16 + 1)
                for hl in range(2):
                    if kb < nprev:
                        bi, tkl = cbuf[kb // 16]
                        lo = (kb % 16) * 128
                        ksrc = kstream[bi][hl * 64:(hl + 1) * 64, lo:lo + 128]
                        kdep = tkl
                    else:
                        ksrc = kTcur[hl * 64:(hl + 1) * 64, c, dg * 128:(dg + 1) * 128]
                        kdep = tkT[c]
                    qsrc = qT[hl * 64:(hl + 1) * 64, c, q0:512]
                    tk = PE.add(lambda e, hl=hl, ksrc=ksrc, q0=q0, spair=spair, qsrc=qsrc: e.matmul(
                        spair[:, hl, q0:512], lhsT=ksrc, rhs=qsrc,
                        start=True, stop=True),
                        [kdep, tq[c], slot_free[sp_i * 2], slot_free[sp_i * 2 + 1]] if hl == 0 else (),
                        sig=(hl == 1))
                if kb < nprev and (kb % 16 == 15 or kb == nprev - 1):
                    ks_free[cbuf[kb // 16][0]] = tk
                te = ACT.add(lambda e, spair=spair, pti=pti, q0=q0: e.activation(
                    out=PT[pti][:, :, q0:512], in_=spair[:, :, q0:512], func=AF.Exp, scale=0.125),
                    [tk, ptfree[pti]], nochain=(kb >= 3))
                slot_free[sp_i * 2] = te
                slot_free[sp_i * 2 + 1] = te
                if dg >= 0:
                    te = POOL.add(lambda e, pti=pti, q0=q0: e.memset(PT[pti][64:128, :, q0:q0 + 64], 0.0), [te])
                pend[kb] = te

            def emit_pv(kb):
                dg = kb - nprev
                q0 = max(0, dg) * 128
                pti = kb % 3
                te = pend.pop(kb)
                tk2 = None
                for hl in range(2):
                    h = 2 * j + hl
                    for qs in range(q0 // 128, 4):
                        a_ = hl * 4 + qs
                        first = (kb == 0)
                        lastq = (kb == nprev + qs)
                        dps = [te, tv] if (hl == 0 and qs == q0 // 128) else []
                        if first:
                            dps = dps + [accfree[a_]]
                        stf = first and (a_ % 3 == 0)
                        tk2 = PE.add(lambda e, a_=a_, hl=hl, qs=qs, h=h, stf=stf, lastq=lastq: e.matmul(
                            acc_ap(a_), lhsT=PT[pti][:, hl, qs * 128:(qs + 1) * 128], rhs=Vres[:, kb, h, :],
                            start=stf, stop=lastq, skip_group_check=True), dps, sig=(hl == 1 and qs == 3))
                ptfree[pti] = tk2
                return tk2

            tk2 = None
            for step in range(nkb + 1):
                if step < nkb:
                    emit_qk(step)
                if step >= 1:
                    tk2 = emit_pv(step - 1)
                if step == 3 and deferred:
                    ta_all.append(deferred.pop()())
            tlast = tk2
            tev = evac_pass_fast(l, m, j, tlast, tpool_done)
            if m == 0:
                state["o1_tok"][j] = tev
            elif c == 2:
                deferred.append(lambda j=j, tev=tev: subln_fast(l, j, tev))
            else:
                ta_all.append(subln_fast(l, j, tev))
        state["vres_free"] = tlast
        return ta_all

    def issue_k_load(l, bq, deps):
        return POOL.dma(kc_ds, kc_tok[:, :, :], cache_k[l, bq].rearrange("(k p) c -> p k c", p=128), deps=deps)

    def issue_v_load(l, bq, deps):
        t2 = None
        for hh in range(4):
            t2 = POOL.dma(misc_ds, V_s[:, 0:NCB, hh, 0:128],
                          cache_v[l, bq][:, hh, :].rearrange("(k p) d -> p k d", p=128), deps=deps)
        t3 = POOL.add(lambda e: e.memset(V_s[:, :, :, 128:129], 1.0), deps)
        return [t2, t3]

    def attn_sample(l, tq, tkT, tv):
        ta_all = []
        pre = state.pop("samp_pre")
        tk_load, tv_load = pre
        for bq in range(NBS):
            r0 = (bq % 2) * 64
            prev = [G("samp_free"), G("vres_free")]
            tc = [tk_load, tv_load]
            tkt = []
            for c in range(4):
                for k0 in range(0, NCB, 4):
                    nn = min(4, NCB - k0)
                    i, view, tp = tr_group([kc_tok[:, k0 + kk, c * 128:(c + 1) * 128] for kk in range(nn)], [tk_load])
                    te = ev_copy(DVE, kT_s[:, c, k0 * 128:(k0 + nn) * 128], view[:, 0:nn * 128], [tp, prev])
                    slot_free[i] = te
                    tkt.append(te)
            tvn = DVE.add(lambda e, bq=bq: e.tensor_copy(out=V_s[0:64, NCB, :, 0:128], in_=vnew[bq][0:64, :, 0:128]),
                          [tv, tv_load])
            if bq + 1 < NBS:
                tk_load = issue_k_load(l, bq + 1, [tp])
            lastpe = None
            for c in range(4):
                m, j = divmod(c, 2)
                pend = {}
                groups = [list(range(g0, min(g0 + 8, NCB))) for g0 in range(0, NCB, 8)] + [[NCB]]

                def s_qk(gi):
                    kbs = groups[gi]
                    nk = 128 if kbs[0] < NCB else 64
                    sp_i = gi % 2
                    spair = psA if sp_i == 0 else psB
                    pti = gi % 3
                    tk = None
                    n = len(kbs)
                    for ki, kb in enumerate(kbs):
                        for hl in range(2):
                            if kb < NCB:
                                ksrc = kT_s[hl * 64:(hl + 1) * 64, c, kb * 128:(kb + 1) * 128]
                            else:
                                ksrc = kTcur[hl * 64:(hl + 1) * 64, c, bq * 64:(bq + 1) * 64]
                            qsrc = qT[hl * 64:(hl + 1) * 64, c, bq * 64:(bq + 1) * 64]
                            osl = spair[0:nk, hl, ki * 64:(ki + 1) * 64]
                            first = (ki == 0 and hl == 0)
                            tk = PE.add(lambda e, ksrc=ksrc, qsrc=qsrc, osl=osl: e.matmul(
                                osl, lhsT=ksrc, rhs=qsrc, start=True, stop=True, skip_group_check=True),
                                [tkt, tkT[c], tq[c], slot_free[sp_i * 2], slot_free[sp_i * 2 + 1]] if first else (),
                                sig=(ki == n - 1 and hl == 1))
                    src_ap = spair[0:nk, :, 0:n * 64]
                    dst_ap = PT[pti][0:nk, :, 0:n * 64]
                    te = ACT.add(lambda e, src_ap=src_ap, dst_ap=dst_ap: e.activation(
                        out=dst_ap, in_=src_ap, func=AF.Exp, scale=0.125), [tk, ptfree[pti]])
                    slot_free[sp_i * 2] = te
                    slot_free[sp_i * 2 + 1] = te
                    pend[gi] = te

                def s_pv(gi):
                    kbs = groups[gi]
                    nk = 128 if kbs[0] < NCB else 64
                    pti = gi % 3
                    te = pend.pop(gi)
                    tk2 = None
                    n = len(kbs)
                    for ki, kb in enumerate(kbs):
                        for hl in range(2):
                            h = 2 * j + hl
                            a_ = hl * 3
                            first = (kb == 0)
                            dps = [te, tvn] if (ki == 0 and hl == 0) else []
                            if first:
                                dps = dps + [accfree[a_]]
                            oacc = acc_ap(a_, r0, 64)
                            lsrc = PT[pti][0:nk, hl, ki * 64:(ki + 1) * 64]
                            rsrc = V_s[0:nk, kb, h, :]
                            tk2 = PE.add(lambda e, oacc=oacc, lsrc=lsrc, rsrc=rsrc, first=first, kb=kb: e.matmul(
                                oacc, lhsT=lsrc, rhs=rsrc, start=first, stop=(kb == NCB), skip_group_check=True),
                                dps, sig=(ki == n - 1 and hl == 1))
                    ptfree[pti] = tk2
                    return tk2

                tk2 = None
                for step in range(len(groups) + 1):
                    if step < len(groups):
                        s_qk(step)
                    if step >= 1:
                        tk2 = s_pv(step - 1)
                lastpe = tk2
                tev = evac_pass(l, m, j, lastpe, r0, 64, [0, 3])
                if m == 0:
                    state["o1_tok"][j] = tev
                else:
                    ta_all.append(subln(l, j, tev, r0, 64, [(hl * 3, 2 * j + hl, bq // 2) for hl in range(2)]))
            state["samp_free"] = [lastpe, ta_all[-2:]]
            if bq + 1 < NBS:
                tv_load = issue_v_load(l, bq + 1, [lastpe])
        return ta_all

    def load_mem_sample(l, bq):
        prev = [G("mem_free")]
        t1 = POOL.dma(mem_ds, mktok[:, :, :], cache_mem_k[l, bq].rearrange("(s p) c -> p s c", p=128),
                      deps=[prev, G("xn_free")])
        t2 = POOL.dma(mem_ds, mv[:, :, :], cache_mem_v[l, bq].rearrange("(s p) c -> p s c", p=128), deps=[prev])
        toks = []
        tp = None
        for cc in range(8):
            i, view, tp = tr_group([mktok[:, s, cc * 128:(cc + 1) * 128] for s in range(2)], [t2, prev])
            te = ev_copy(evac_eng(), mkT[:, cc, :], view[:, 0:256], [tp])
            slot_free[i] = te
            toks.append(te)
        state["xn_free"] = tp
        return toks + [t2]

    def load_pool_state(l):
        uview = uext[:, :, 0:4 * 80].rearrange("p g (b c) -> p g b c", b=4)
        wdeps = [G("uext_free"), G("region_free")]
        toks = []
        for bq in range(NBS):
            si = get_stage()
            t1 = POOL.dma(ps_ds[bq], stage[si][0:15, :], state_pool[l, bq], deps=[stage_free[si]])
            tp = None
            for g in range(4):
                i, view, tp = tr_group([stage[si][0:15, g * 128:(g + 1) * 128]], [t1], f32=True)
                te = DVE.add(lambda e, g=g, bq=bq, view=view: e.tensor_copy(out=uview[:, g, bq, 1:16], in_=view[:, 0:15]),
                             [tp, wdeps])
                slot_free[i] = te
                toks.append(te)
            stage_free[si] = tp
        tz = POOL.add(lambda e: e.memset(uview[:, :, :, 0:1], 0.0), wdeps)
        return toks + [tz]

    def run_tile(l, t):
        is_s = (t == NTP)
        ns = 2 if is_s else 4
        ntok = ns * 128
        row0 = S if is_s else t * 512
        orow = 0 if is_s else row0
        groups = [(b_ * 64, 64) for b_ in range(4)] if is_s else [(s * 128, 128) for s in range(4)]
        lb = 64 if is_s else 512
        nb = 4 if is_s else 1
        uview = uext[:, :, 0:nb * (16 + lb)].rearrange("p g (b c) -> p g b c", b=nb)
        if l == 0:
            src_x = (x_sample if is_s else x_prompt[row0:row0 + ntok, :])
        else:
            src_x = xscr[row0:row0 + ntok, :]
        xf = G("xt_free")
        tl = []
        for s in range(ns):
            dps = [xf[s] if isinstance(xf, list) and s < len(xf) else xf, G("xs_all") if l > 0 else None]
            tl.append(SP.dma(xl_ds[s], xt[:, s, :], src_x[s * 128:(s + 1) * 128, :], deps=dps))
        st["per_s"] = True
        hts = rmsnorm_hT(ns, gcols[:, l, 0, :], tl, [G("hT_free")])
        st["per_s"] = False
        if l + 1 < DEPTH:
            per = -(-NSLAB // max(NTP - 1, 1))
            drip_conv(l + 1, NSLAB if (is_s or t == NTP - 1) else per)
        rf = G("region_free")
        if is_s:
            d0 = [G("samp_free"), G("vres_free")]
            state["samp_pre"] = (issue_k_load(l, 0, d0), issue_v_load(l, 0, d0))
        if not is_s:
            state["hist_tok"] = POOL.add(lambda e: e.tensor_copy(out=uext[:, :, 0:16], in_=hist_keep[:, :, :]),
                                         [rf, G("hk_tok"), G("uext_free")])
        b_, tsl = load_slab(l, 0)
        slab = slabs[b_]
        tq = []
        lastpe = None
        for cc in range(4):
            i, tk = mm_group(lambda k, cc=cc, slab=slab: slab[:, k, cc * 128:(cc + 1) * 128],
                             lambda k: hT[:, k, 0:ntok], 8, ntok, [tsl], kdeps=hts)
            te = ev_copy(evac_eng(), qT[:, cc, 0:ntok], slot_aps[i][:, 0:ntok], [tk, rf])
            slot_free[i] = te
            tq.append(te)
            lastpe = tk
        slab_free[b_] = lastpe
        b_, tsl = load_slab(l, 1)
        slab = slabs[b_]
        tkT = []
        for cc in range(4):
            i, tk = mm_group(lambda k, cc=cc, slab=slab: slab[:, k, cc * 128:(cc + 1) * 128],
                             lambda k: hT[:, k, 0:ntok], 8, ntok, [tsl], kdeps=hts)
            te = ev_copy(evac_eng(), kTcur[:, cc, 0:ntok], slot_aps[i][:, 0:ntok], [tk, rf, G("kst_last")])
            slot_free[i] = te
            tkT.append(te)
        if not is_s:
            tks_ = POOL.dma(kst_ds[t % 2], ktscr[l, :, :, t * 512:(t + 1) * 512].rearrange("c p k -> p c k"),
                            kTcur[:, :, :], deps=[tkT])
            state["kst"][t] = tks_
            state["kst_last"] = tks_
            out_toks.append(tks_)
        for gi, (g0, gn) in enumerate(groups):
            i, tk = mm_group(lambda k, g0=g0, gn=gn: hT[:, k, g0:g0 + gn],
                             lambda k, slab=slab: slab[:, k, :], 8, 512, [tsl], m=gn, kdeps=hts)
            si = get_stage()
            te = ev_copy(evac_eng(), stage[si][0:gn, :], slot_aps[i][0:gn, :], [tk, stage_free[si]])
            slot_free[i] = te
            dst = (k_sample if is_s else k_prompt)[l, orow + g0:orow + g0 + gn, :]
            store_stage(si, dst, stage[si][0:gn, :], te)
            lastpe = tk
        slab_free[b_] = lastpe
        b_, tsl = load_slab(l, 2)
        slab = slabs[b_]
        tv = []
        for gi, (g0, gn) in enumerate(groups):
            i, tk = mm_group(lambda k, g0=g0, gn=gn: hT[:, k, g0:g0 + gn],
                             lambda k, slab=slab: slab[:, k, :], 8, 512, [tsl], m=gn, kdeps=hts)
            si = get_stage()
            te = ev_copy(ACT, stage[si][0:gn, :], slot_aps[i][0:gn, :], [tk, stage_free[si]])
            dst = (v_sample if is_s else v_prompt)[l, orow + g0:orow + g0 + gn, :]
            store_stage(si, dst, stage[si][0:gn, :], te)
            vdst = vnew[gi][0:gn, :, 0:128] if is_s else Vres[:, t * 4 + gi, :, 0:128]
            te2 = DVE.add(lambda e, i=i, gn=gn, vdst=vdst: e.tensor_copy(
                out=vdst, in_=slot_aps[i][0:gn, :].rearrange("p (h d) -> p h d", h=4)),
                [tk, te, G("vres_wr"), rf])
            slot_free[i] = [te, te2]
            tv.append(te2)
            lastpe = tk
        slab_free[b_] = lastpe
        b_, tsl = load_slab(l, 3)
        slab = slabs[b_]
        tu = []
        for g in range(4):
            i, tk = mm_group(lambda k, g=g, slab=slab: slab[:, k, g * 128:(g + 1) * 128],
                             lambda k: hT[:, k, 0:ntok], 8, ntok, [tsl], kdeps=hts)
            te = ev_copy(evac_eng(), uview[:, g, :, 16:16 + lb],
                         slot_aps[i][:, 0:ntok].rearrange("p (b c) -> p b c", b=nb),
                         [tk, G("uext_free"), G("hist_tok"), rf])
            slot_free[i] = te
            tu.append(te)
            lastpe = tk
        if is_s or t == NTP - 1:
            for gi, (g0, gn) in enumerate(groups):
                if (not is_s) and gi != 3:
                    continue
                i, tk = mm_group(lambda k, g0=g0, gn=gn: hT[:, k, g0:g0 + gn],
                                 lambda k, slab=slab: slab[:, k, :], 8, 512, [tsl], m=gn, kdeps=hts)
                si = get_stage()
                te = ev_copy(evac_eng(), stage[si][0:gn, :], slot_aps[i][0:gn, :], [tk, stage_free[si]])
                slot_free[i] = te
                dst = pool_sample[l, gi] if is_s else pool_prompt[l]
                store_stage(si, dst, stage[si][gn - 15:gn, :], te)
                lastpe = tk
        slab_free[b_] = lastpe
        state["hT_free"] = lastpe
        first_stream = (not is_s) and t == 0
        tpool = []
        tpy = []
        for g in range(4):
            w = 2 << g
            cur = uview[:, g, :, :]
            lo = 1
            tk = [tu[g]]
            for lev in range(g + 1):
                sh = 1 << lev
                nlo = lo + sh
                dstb = utmp[lev % 2][:, 0:nb * (16 + lb)].rearrange("p (b c) -> p b c", b=nb)
                tk = POOL.add(lambda e, dstb=dstb, cur=cur, nlo=nlo, sh=sh: e.tensor_tensor(
                    out=dstb[:, :, nlo:16 + lb], in0=cur[:, :, nlo:16 + lb],
                    in1=cur[:, :, nlo - sh:16 + lb - sh], op=ALU.add), [tk, G("utmp_free")])
                cur = dstb
                lo = nlo
            pb2 = pooled2[g]
            pvw = pb2[:, 0:ntok].rearrange("p (b c) -> p b c", b=nb)
            tk2 = DVE.add(lambda e, pvw=pvw, cur=cur, g=g, w=w: e.scalar_tensor_tensor(
                out=pvw, in0=uview[:, g, :, 16:16 + lb], scalar=-float(w), in1=cur[:, :, 16:16 + lb],
                op0=ALU.mult, op1=ALU.add), [tk, G("pooled_free")])
            if first_stream:
                tk3 = POOL.add(lambda e, cur=cur, g=g: e.tensor_tensor(
                    out=cur[:, 0, 16:32], in0=cur[:, 0, 16:32], in1=icnt[:, g, :], op=ALU.mult), [tk2, ic_toks])
                tk2 = DVE.add(lambda e, cur=cur, g=g, pb2=pb2, w=w: e.scalar_tensor_tensor(
                    out=pb2[:, 0:16], in0=uext[:, g, 16:32], scalar=-float(w), in1=cur[:, 0, 16:32],
                    op0=ALU.mult, op1=ALU.add), [tk3])
            state["utmp_free"] = tk2
            tpool.append(tk2)
        if not is_s:
            state["hk_tok"] = POOL.add(lambda e: e.tensor_copy(out=hist_keep[:, :, :], in_=uext[:, :, 512:528]), [tpool])
            state["uext_free"] = [state["hk_tok"]] + tpool
        else:
            state["uext_free"] = tpool
        if is_s:
            ta = attn_sample(l, tq, tkT, tv)
        else:
            ta = attn_prompt(l, t, tq, tkT, tv, tpool)
        for g in range(4):
            i = get_slot()
            tk = PE.add(lambda e, i=i, g=g: e.matmul(slot_aps[i][:, 0:ntok], lhsT=wpool_bf[:, l, g, :],
                                                     rhs=pooled2[g][:, 0:ntok], start=True, stop=True),
                        [tpool[g], slot_free[i], t_wp])
            state["pooled_free"] = tk
            te = ev_scale(evac_eng(), pyT[:, g, 0:ntok], slot_aps[i][:, 0:ntok], pscol[:, l, g:g + 1],
                          [tk, t_const, G("pyT_free")])
            slot_free[i] = te
            tpy.append(te)
        taT = []
        tp = None
        for cc in range(4):
            i, view, tp = tr_group([a_tok[:, s, cc * 128:(cc + 1) * 128] for s in range(ns)], [ta])
            tk = ev_scale(evac_eng(), aT[:, cc, 0:ntok], view[:, 0:ntok], sgcol[:, l:l + 1], [tp, G("aT_free"), lam_tok[l]])
            slot_free[i] = tk
            taT.append(tk)
        txo = []
        for h in range(2):
            b_, tsl = load_slab(l, 4 + h)
            slab = slabs[b_]
            for s in range(ns):
                i, tk = mm_group(lambda k, s=s: (aT if k < 4 else pyT)[:, k % 4, s * 128:(s + 1) * 128],
                                 lambda k, slab=slab: slab[:, k, :], 8, 512, [taT, tpy, tsl])
                te = DVE.add(lambda e, i=i, s=s, h=h: e.tensor_tensor(
                    out=xt[:, s, h * 512:(h + 1) * 512], in0=slot_aps[i][:, :],
                    in1=xt[:, s, h * 512:(h + 1) * 512], op=ALU.add), [tk])
                slot_free[i] = te
                txo.append((s, te))
                lastpe = tk
            slab_free[b_] = lastpe
        state["aT_free"] = lastpe
        state["pyT_free"] = lastpe
        regA_done = [lastpe, ta, tpool, tp, G("kst_last")]
        st["per_s"] = True
        hts = rmsnorm_hT(ns, gcols[:, l, 1, :], [[te for (s2, te) in txo if s2 == s] for s in range(ns)], [G("hT_free")])
        st["per_s"] = False
        tqx = []
        for h in range(2):
            b_, tsl = load_slab(l, 6 + h)
            slab = slabs[b_]
            for cc in range(4):
                i, tk = mm_group(lambda k, cc=cc, slab=slab: slab[:, k, cc * 128:(cc + 1) * 128],
                                 lambda k: hT[:, k, 0:ntok], 8, ntok, [tsl], kdeps=hts)
                te = ev_copy(evac_eng(), qxT[:, h * 4 + cc, 0:ntok], slot_aps[i][:, 0:ntok], [tk, regA_done])
                slot_free[i] = te
                tqx.append(te)
                lastpe = tk
            slab_free[b_] = lastpe
        state["hT_free"] = lastpe
        state["ox_toks"] = []
        state["ox_free"] = regA_done
        if is_s:
            for bq in range(NBS):
                tmk = load_mem_sample(l, bq)
                lastpe = cross_attn(l, bq * 64, 64, [tqx, tmk])
                state["mem_free"] = lastpe
        else:
            lastpe = cross_attn(l, 0, 512, [tqx, G("memkv_tok")])
            state["mem_free"] = lastpe
        tox = state["ox_toks"]
        txo = []
        for h in range(2):
            b_, tsl = load_slab(l, 12 + h)
            slab = slabs[b_]
            for s in range(ns):
                i, tk = mm_group(lambda k, s=s: oxT[:, k, s * 128:(s + 1) * 128],
                                 lambda k, slab=slab: slab[:, k, :], 8, 512, [tsl, tox])
                te = DVE.add(lambda e, i=i, s=s, h=h: e.tensor_tensor(
                    out=xt[:, s, h * 512:(h + 1) * 512], in0=slot_aps[i][:, :],
                    in1=xt[:, s, h * 512:(h + 1) * 512], op=ALU.add), [tk])
                slot_free[i] = te
                txo.append((s, te))
                lastpe = tk
            slab_free[b_] = lastpe
        st["per_s"] = True
        hts = rmsnorm_hT(ns, gcols[:, l, 3, :], [[te for (s2, te) in txo if s2 == s] for s in range(ns)], [G("hT_free")])
        st["per_s"] = False
        txm = []
        for qd in range(4):
            hb = qd % 2
            thid = []
            for h in range(2):
                b_, tsl = load_slab(l, 14 + qd * 2 + h)
                slab = slabs[b_]
                for cc in range(4):
                    i, tk = mm_group(lambda k, cc=cc, slab=slab: slab[:, k, cc * 128:(cc + 1) * 128],
                                     lambda k: hT[:, k, 0:ntok], 8, ntok, [tsl], kdeps=hts)
                    rb = (h * 4 + cc) % 2
                    te = ACT.add(lambda e, i=i, rb=rb: e.activation(out=rtmp[rb][:, 0:ntok], in_=slot_aps[i][:, 0:ntok],
                                                                  func=AF.Relu), [tk, G("rt_free%d" % rb), regA_done])
                    slot_free[i] = te
                    tsq = POOL.add(lambda e, rb=rb, hb=hb, h=h, cc=cc: e.tensor_tensor(
                        out=hid[hb][:, h * 4 + cc, 0:ntok], in0=rtmp[rb][:, 0:ntok], in1=rtmp[rb][:, 0:ntok],
                        op=ALU.mult), [te, G("hid_free%d" % hb), regA_done])
                    state["rt_free%d" % rb] = tsq
                    thid.append(tsq)
                    lastpe = tk
                slab_free[b_] = lastpe
            for h in range(2):
                b_, tsl = load_slab(l, 22 + qd * 2 + h)
                slab = slabs[b_]
                for s in range(ns):
                    i, tk = mm_group(lambda k, s=s, hb=hb: hid[hb][:, k, s * 128:(s + 1) * 128],
                                     lambda k, slab=slab: slab[:, k, :], 8, 512, [tsl], kdeps=thid)
                    te = DVE.add(lambda e, i=i, s=s, h=h: e.tensor_tensor(
                        out=xt[:, s, h * 512:(h + 1) * 512], in0=slot_aps[i][:, :],
                        in1=xt[:, s, h * 512:(h + 1) * 512], op=ALU.add), [tk])
                    slot_free[i] = te
                    if qd == 3:
                        txm.append((s, te))
                    lastpe = tk
                slab_free[b_] = lastpe
            state["hid_free%d" % hb] = lastpe
        state["hT_free"] = lastpe
        state["region_free"] = [lastpe, tox]
        txs = [[te for (s2, te) in txm if s2 == s] for s in range(ns)]
        stores = []
        if l == DEPTH - 1:
            st["per_s"] = True
            rs, t3 = rms_rstd(ns, txs, G("xn_free"))
            st["per_s"] = False
            ydst = y_sample if is_s else y_prompt[row0:row0 + ntok, :]
            for s in range(ns):
                ty = DVE.add(lambda e, s=s: e.scalar_tensor_tensor(
                    out=xt[:, s, :], in0=xt[:, s, :], scalar=rs[:, s:s + 1], in1=fg_bc[:, :],
                    op0=ALU.mult, op1=ALU.mult), [t3, t_const])
                stores.append(POOL.dma(xs_ds[s], ydst[s * 128:(s + 1) * 128, :], xt[:, s, :], deps=[ty]))
        else:
            for s in range(ns):
                stores.append(POOL.dma(xs_ds[s], xscr[row0 + s * 128:row0 + (s + 1) * 128, :], xt[:, s, :],
                                       deps=[txs[s]]))
        out_toks.extend(stores)
        for s in range(ns):
            xt_store[s] = stores[s]
        state["xt_free"] = list(xt_store)
        state["xs_all"] = list(xt_store)

    def schedule():
        step = 0
        for l in range(DEPTH):
            if upto is not None and step >= upto:
                return
            t_v1 = POOL.add(lambda e: e.memset(Vres[:, :, :, 128:129], 1.0), [G("samp_free")])
            state["vres_wr"] = [t_v1, G("samp_free")]
            state["memkv_tok"] = [mem_kv_prompt(l)]
            state["hk_tok"] = POOL.add(lambda e: e.memset(hist_keep[:, :, :], 0.0), [G("hist_tok")])
            step += 1
            for t in range(NTP):
                if upto is not None and step >= upto:
                    return
                run_tile(l, t)
                step += 1
            if upto is not None and step >= upto:
                return
            state["hist_tok"] = load_pool_state(l)
            run_tile(l, NTP)
            step += 1

    try:
        schedule()
    except _Stop:
        pass
    SP.wait_only(out_toks)

    block = es.enter_context(nc.Block())

    @block.tensor
    def _(e):
        PE.replay(e)

    @block.scalar
    def _(e):
        ACT.replay(e)

    @block.vector
    def _(e):
        DVE.replay(e)

    @block.gpsimd
    def _(e):
        POOL.replay(e)

    @block.sync
    def _(e):
        SP.replay(e)

    es.close()
    return nc


def make_in_maps(inputs, ncores=8, S=8192, PAST=2048):
    f = lambda a: np.ascontiguousarray(np.asarray(a, dtype=np.float32))
    maps = []
    for c in range(ncores):
        sl = slice(c * NBS, (c + 1) * NBS)
        m = {
            "x_prompt": f(inputs["x_prompt"][c]),
            "x_sample": f(inputs["x_sample"][sl]).reshape(NBS * SSEQ, D),
            "cache_k": f(inputs["cache_k"][:, sl]).reshape(DEPTH, NBS, PAST, 512),
            "cache_v": f(inputs["cache_v"][:, sl]),
            "state_pool": f(inputs["state_pool"][:, sl]),
            "cache_mem_k": f(inputs["cache_mem_k"][:, sl]).reshape(DEPTH, NBS, NMEM, D),
            "cache_mem_v": f(inputs["cache_mem_v"][:, sl]).reshape(DEPTH, NBS, NMEM, D),
            "mem_prompt": f(inputs["mem_prompt"][c]),
            "lam_q": f(inputs["lam_q"]).reshape(DEPTH, 128),
            "lam_k": f(inputs["lam_k"]).reshape(DEPTH, 128),
            "final_g": f(inputs["final_g"]).reshape(1, D),
        }
        for k in ["norm_mix_g", "w_in", "subln_g", "w_pool", "pool_scale", "w_out", "norm_x_g", "norm_mem_g",
                  "wq_x", "wk_x", "wv_x", "wo_x", "norm_mlp_g", "w_up", "w_down"]:
            m[k] = f(inputs[k])
        maps.append(m)
    return maps


def assemble(results, ncores=8, S=8192):
    def cat(name, axis, shape_fn):
        return np.concatenate([shape_fn(r[name]) for r in results], axis=axis)
    y_prompt = np.stack([r["y_prompt"] for r in results], 0)
    y_sample = np.concatenate([r["y_sample"].reshape(NBS, SSEQ, D) for r in results], 0)
    k_prompt = np.stack([r["k_prompt"].reshape(DEPTH, S, 2, 4, 64) for r in results], 1)
    v_prompt = np.stack([r["v_prompt"].reshape(DEPTH, S, 4, 128) for r in results], 1)
    pool_prompt = np.stack([r["pool_prompt"] for r in results], 1)
    mem_k = np.stack([r["mem_k_prompt"].reshape(DEPTH, NMEM, 4, 256) for r in results], 1)
    mem_v = np.stack([r["mem_v_prompt"].reshape(DEPTH, NMEM, 4, 256) for r in results], 1)
    k_sample = np.concatenate([r["k_sample"].reshape(DEPTH, NBS, SSEQ, 2, 4, 64) for r in results], 1)
    v_sample = np.concatenate([r["v_sample"].reshape(DEPTH, NBS, SSEQ, 4, 128) for r in results], 1)
    pool_sample = np.concatenate([r["pool_sample"] for r in results], 1)
    outs = (y_prompt, y_sample, k_prompt, v_prompt, pool_prompt, mem_k, mem_v, k_sample, v_sample, pool_sample)
    return tuple(np.ascontiguousarray(o, dtype=np.float32) for o in outs)


def kernel(**inputs):
    ncores = 8
    nc = build_program()
    in_maps = make_in_maps(inputs, ncores)
    res = run_bass_kernel_spmd(nc, in_maps, core_ids=list(range(ncores)))
    return assemble(res.results, ncores)
```
